# Optimizing a Trainium2 kernel written in Bass

```python
import jax, jax.numpy as jnp
from jax import lax
import numpy as np

D_MODEL = 1024
BATCH = 8
SEQ = 4096
DEPTH = 1

CONV_WIDTH = D_MODEL
CONV_KERNEL = 31
HEAD_DIM = 64
N_Q_HEADS = 16
N_KV_HEADS = 4
GROUP = N_Q_HEADS // N_KV_HEADS
ATTN_WIDTH = N_Q_HEADS * HEAD_DIM
KV_WIDTH = N_KV_HEADS * HEAD_DIM
WINDOW = 128
BLOCK = 128
ROPE_THETA = 10000.0
RMS_EPS = 1e-5
LN_EPS = 1e-5
N_BRANCHES = 2

SECTION_WIDTHS = (
    2 * CONV_WIDTH,
    CONV_WIDTH,
    ATTN_WIDTH,
    KV_WIDTH,
    KV_WIDTH,
    ATTN_WIDTH,
    N_BRANCHES * D_MODEL,
)
SPLIT_POINTS = tuple(int(v) for v in np.cumsum(SECTION_WIDTHS)[:-1])
IN_WIDTH = sum(SECTION_WIDTHS)

kernel_name = "hybrid_conformer_conv_swa_sink_gated_block"


def rmsnorm(x, g):
    xf = x.astype(jnp.float32)
    y = xf * lax.rsqrt(jnp.mean(xf * xf, axis=-1, keepdims=True) + RMS_EPS)
    return (y * g.astype(jnp.float32)).astype(x.dtype)


def layernorm(x, g, b):
    xf = x.astype(jnp.float32)
    mu = jnp.mean(xf, axis=-1, keepdims=True)
    var = jnp.mean(jnp.square(xf - mu), axis=-1, keepdims=True)
    y = (xf - mu) * lax.rsqrt(var + LN_EPS)
    return (y * g.astype(jnp.float32) + b.astype(jnp.float32)).astype(x.dtype)


def rope(t, pos):
    inv_freq = ROPE_THETA ** (-jnp.arange(0, HEAD_DIM, 2, dtype=jnp.float32) / HEAD_DIM)
    ang = pos.astype(jnp.float32)[:, None] * inv_freq[None, :]
    cos = jnp.cos(ang)[None, :, None, :]
    sin = jnp.sin(ang)[None, :, None, :]
    tf = t.astype(jnp.float32)
    t1, t2 = jnp.split(tf, 2, axis=-1)
    out = jnp.concatenate([t1 * cos - t2 * sin, t2 * cos + t1 * sin], axis=-1)
    return out.astype(t.dtype)


def conformer_conv(glu_in, w_dw, b_dw, ln_g, ln_b):
    a, b = jnp.split(glu_in, 2, axis=-1)
    h = a * jax.nn.sigmoid(b)
    h = lax.conv_general_dilated(
        h, w_dw[:, None, :].astype(h.dtype), window_strides=(1,),
        padding=[(CONV_KERNEL - 1, 0)],
        dimension_numbers=("NWC", "WIO", "NWC"),
        feature_group_count=CONV_WIDTH) + b_dw
    h = layernorm(h, ln_g, ln_b)
    return jax.nn.silu(h)


def sliding_window_attention(q, k, v, sinks):
    B, S = q.shape[0], q.shape[1]
    nb = S // BLOCK
    qb = q.reshape(B, nb, BLOCK, N_KV_HEADS, GROUP, HEAD_DIM)

    def band(t):
        tb = t.reshape(B, nb, BLOCK, N_KV_HEADS, HEAD_DIM)
        prev = jnp.pad(tb, ((0, 0), (1, 0), (0, 0), (0, 0), (0, 0)))[:, :-1]
        return jnp.concatenate([prev, tb], axis=2)

    kb, vb = band(k), band(v)
    scores = jnp.einsum("bnqhgd,bnkhd->bnhgqk", qb, kb).astype(jnp.float32) * (HEAD_DIM ** -0.5)
    qi = jnp.arange(BLOCK)[:, None]
    kj = jnp.arange(2 * BLOCK)[None, :]
    rel = qi + BLOCK - kj
    in_window = (rel >= 0) & (rel < WINDOW)
    key_pos = jnp.arange(nb)[:, None, None] * BLOCK - BLOCK + kj[None]
    mask = in_window[None] & (key_pos >= 0)
    scores = jnp.where(mask[None, :, None, None], scores, jnp.float32(-1e30))
    sink = sinks.astype(jnp.float32).reshape(N_KV_HEADS, GROUP)[None, None, :, :, None, None]
    m = jnp.maximum(jnp.max(scores, axis=-1, keepdims=True), sink)
    p = jnp.exp(scores - m)
    denom = jnp.sum(p, axis=-1, keepdims=True) + jnp.exp(sink - m)
    probs = (p / denom).astype(v.dtype)
    out = jnp.einsum("bnhgqk,bnkhd->bnqhgd", probs, vb)
    return out.reshape(B, S, ATTN_WIDTH)


def setup_inputs(seed: int = 0) -> dict:
    key = jax.random.key(seed)
    ks = jax.random.split(key, 13)
    f32 = jnp.float32
    x = jax.random.normal(ks[0], (BATCH, SEQ, D_MODEL), f32)
    norm_g = 1.0 + 0.02 * jax.random.normal(ks[1], (DEPTH, D_MODEL), f32)
    w_in = jax.random.normal(ks[2], (DEPTH, D_MODEL, IN_WIDTH), f32) * D_MODEL ** -0.5
    conv_dw_w = jax.random.normal(ks[3], (DEPTH, CONV_KERNEL, CONV_WIDTH), f32) * CONV_KERNEL ** -0.5
    conv_dw_b = 0.02 * jax.random.normal(ks[4], (DEPTH, CONV_WIDTH), f32)
    conv_ln_g = 1.0 + 0.02 * jax.random.normal(ks[5], (DEPTH, CONV_WIDTH), f32)
    conv_ln_b = 0.02 * jax.random.normal(ks[6], (DEPTH, CONV_WIDTH), f32)
    w_conv_out = jax.random.normal(ks[7], (DEPTH, CONV_WIDTH, D_MODEL), f32) * CONV_WIDTH ** -0.5
    attn_sinks = 0.5 * jax.random.normal(ks[8], (DEPTH, N_Q_HEADS), f32)
    w_attn_out = jax.random.normal(ks[9], (DEPTH, ATTN_WIDTH, D_MODEL), f32) * ATTN_WIDTH ** -0.5
    w_out = jax.random.normal(ks[10], (DEPTH, D_MODEL, D_MODEL), f32) * D_MODEL ** -0.5
    final_norm_g = 1.0 + 0.02 * jax.random.normal(ks[11], (D_MODEL,), f32)
    return {"x": x, "norm_g": norm_g, "w_in": w_in, "conv_dw_w": conv_dw_w,
            "conv_dw_b": conv_dw_b, "conv_ln_g": conv_ln_g, "conv_ln_b": conv_ln_b,
            "w_conv_out": w_conv_out, "attn_sinks": attn_sinks, "w_attn_out": w_attn_out,
            "w_out": w_out, "final_norm_g": final_norm_g}


def reference(x, norm_g, w_in, conv_dw_w, conv_dw_b, conv_ln_g, conv_ln_b,
              w_conv_out, attn_sinks, w_attn_out, w_out, final_norm_g):
    B, S = x.shape[0], x.shape[1]
    pos = jnp.arange(S, dtype=jnp.int32)
    for l in range(DEPTH):
        h = rmsnorm(x, norm_g[l])
        proj = jnp.einsum("bsd,de->bse", h, w_in[l])
        glu_in, conv_gate, q, k, v, attn_gate, merge_logits = jnp.split(proj, SPLIT_POINTS, axis=-1)

        c = conformer_conv(glu_in, conv_dw_w[l], conv_dw_b[l], conv_ln_g[l], conv_ln_b[l])
        y_conv = jnp.einsum("bsc,cd->bsd", c * jax.nn.silu(conv_gate), w_conv_out[l])

        q = rope(q.reshape(B, S, N_Q_HEADS, HEAD_DIM), pos)
        k = rope(k.reshape(B, S, N_KV_HEADS, HEAD_DIM), pos)
        v = v.reshape(B, S, N_KV_HEADS, HEAD_DIM)
        a = sliding_window_attention(q, k, v, attn_sinks[l])
        y_attn = jnp.einsum("bsa,ad->bsd", a * jax.nn.silu(attn_gate), w_attn_out[l])

        gates = jax.nn.sigmoid(merge_logits)
        g_conv, g_attn = jnp.split(gates, 2, axis=-1)
        merged = g_conv * y_conv + g_attn * y_attn
        x = x + jnp.einsum("bsd,de->bse", merged, w_out[l])
    return rmsnorm(x, final_norm_g)
```

```python
import numpy as np
from contextlib import ExitStack
import concourse.bass as bass
import concourse.mybir as mybir
from concourse.bass_utils import run_bass_kernel_spmd

F32 = mybir.dt.float32
BF16 = mybir.dt.bfloat16
ALU = mybir.AluOpType
AF = mybir.ActivationFunctionType

ENGS = ("pe", "act", "dve", "pool", "sp")

D = 1024
S = 4096
NCORES = 8
TT = 512
NT = S // TT
KC = 8
INW = 7680
NSLOT = 5
WIDTH = {"glu": 2048, "diag": 0, "k": 1024, "v": 2048, "q": 2048}
NDVE = 7
SLOTW = 4096
EPS = 1e-5


class T:
    __slots__ = ("name", "w", "rs", "ps")

    def __init__(self, name, ps=False):
        self.name = name
        self.w = None
        self.rs = []
        self.ps = ps


class Op:
    __slots__ = ("eng", "fn", "deps", "sig", "val", "sem", "isdma", "pos", "tag")


class Prog:
    def __init__(self):
        self.ops = {e: [] for e in ENGS}
        self.n = 0
        self.dma_keys = []
        self.tag = ""

    def add(self, eng, fn, r=(), w=(), dma=None):
        op = Op()
        op.tag = self.tag
        op.eng = eng
        op.fn = fn
        op.sig = False
        op.val = 0
        op.sem = dma
        op.isdma = dma is not None
        if dma is not None and dma not in self.dma_keys:
            self.dma_keys.append(dma)
        op.pos = self.n
        self.n += 1
        deps = []
        for t in r:
            if t.w is not None:
                deps.append(t.w)
            if t.ps:
                deps.extend(o for o in t.rs if o.eng != eng)
        for t in w:
            if t.w is not None:
                deps.append(t.w)
            deps.extend(t.rs)
        keep = []
        seen = set()
        latest = {}
        for d in deps:
            if id(d) in seen or d is op:
                continue
            seen.add(id(d))
            if d.isdma:
                keep.append(d)
                continue
            if d.eng == eng and eng == "pe":
                continue
            if d.eng not in latest or latest[d.eng].pos < d.pos:
                latest[d.eng] = d
        keep.extend(latest.values())
        for d in keep:
            d.sig = True
        op.deps = keep
        for t in r:
            t.rs.append(op)
        for t in w:
            t.w = op
            t.rs = []
        self.ops[eng].append(op)
        return op

    def emit_all(self, block, sems, dma_sems):
        for e in ENGS:
            cnt = 0
            for op in self.ops[e]:
                if op.isdma:
                    continue
                if op.sig:
                    cnt += 1
                    op.val = cnt
        dcnt = {}
        for op in sorted([o for e in ENGS for o in self.ops[e] if o.isdma], key=lambda o: o.pos):
            dcnt[op.sem] = dcnt.get(op.sem, 0) + 16
            op.val = dcnt[op.sem]

        def run(e, eng):
            known = {}
            for op in self.ops[e]:
                need = {}
                for d in op.deps:
                    if d.isdma:
                        key = ("d", d.sem)
                        s = dma_sems[d.sem]
                    else:
                        key = ("c", d.eng)
                        s = sems[d.eng]
                    if need.get(key, (None, 0))[1] < d.val:
                        need[key] = (s, d.val)
                for key, (s, v) in need.items():
                    if known.get(key, 0) >= v:
                        continue
                    eng.wait_ge(s, v)
                    known[key] = v
                if op.fn is None:
                    continue
                inst = op.fn(eng)
                if op.isdma:
                    inst.then_inc(dma_sems[op.sem], 16)
                elif op.sig:
                    inst.then_inc(sems[e], 1)

        @block.tensor
        def _(eng):
            run("pe", eng)

        @block.scalar
        def _(eng):
            run("act", eng)

        @block.vector
        def _(eng):
            run("dve", eng)

        @block.gpsimd
        def _(eng):
            run("pool", eng)

        @block.sync
        def _(eng):
            run("sp", eng)
            for key, v in dcnt.items():
                eng.wait_ge(dma_sems[key], v)


class Ring:
    def __init__(self, aps, name):
        self.aps = aps
        self.ts = [T("%s%d" % (name, i)) for i in range(len(aps))]
        self.i = 0

    def get(self):
        k = self.i % len(self.aps)
        self.i += 1
        return self.aps[k], self.ts[k]


def build_nc(order=None):
    discover = order is None
    nc = bass.Bass("TRN2", target_bir_lowering=False)
    dt_in = lambda name, shape: nc.dram_tensor(name, shape, F32, kind="ExternalInput").ap()
    x = dt_in("x", [S, D])
    w_in = dt_in("w_in", [D, INW])
    w_co = dt_in("w_co", [D, D])
    w_ao = dt_in("w_ao", [D, D])
    w_out = dt_in("w_out", [D, D])
    g_tab_d = dt_in("g_tab", [128, D])
    fg_tab_d = dt_in("fg_tab", [128, D])
    colp_d = dt_in("colp", [128, 24])
    wdw_d = dt_in("wdw", [128, 8 * 31])
    sinks_d = dt_in("sinks", [128, 16])
    cos_d = dt_in("cosT", [128, S])
    sin_d = dt_in("sinT", [128, S])
    mask_d = dt_in("maskT", [128, 256])
    ident_d = dt_in("ident", [128, 128])
    rperm_d = dt_in("rperm", [128, 128])
    y = nc.dram_tensor("y", [S, D], F32, kind="ExternalOutput").ap()
    NB = 37 if discover else len(order)
    wscr = nc.dram_tensor("wscr", [NB, 128, SLOTW], BF16, kind="Internal").ap()
    disc = []

    w_in_v = w_in.rearrange("(k p) e -> p k e", p=128)
    w_co_v = w_co.rearrange("(k p) e -> p k e", p=128)
    w_ao_v = w_ao.rearrange("(k p) e -> p k e", p=128)
    w_out_v = w_out.rearrange("(k p) e -> p k e", p=128)

    P = Prog()
    with ExitStack() as es:
        def sb(name, shape, dt):
            return es.enter_context(nc.sbuf_tensor("sb_" + name, shape, dt))

        def ps(name, shape, dt):
            return es.enter_context(nc.psum_tensor("ps_" + name, shape, dt))

        slots = [sb("wslot%d" % i, [128, SLOTW], BF16) for i in range(NSLOT)]
        slot_parts = [[T("ws%d_%d" % (i, j)) for j in range(4)] for i in range(NSLOT)]
        xs = [sb("xs%d" % i, [128, D], F32) for i in range(8)]
        t_xs = [T("xs%d" % i) for i in range(8)]
        hb = Ring([sb("hb%d" % i, [128, D], BF16) for i in range(4)], "hb")
        hT = sb("hT", [128, KC * TT], BF16)
        t_hT = [T("hT%d" % i) for i in range(4)]
        tmpf = Ring([sb("tmpf%d" % i, [128, TT], F32) for i in range(5)], "tmpf")
        tmpb = Ring([sb("tmpb%d" % i, [128, TT], BF16) for i in range(3)], "tmpb")
        h1 = [sb("h1_%d" % c, [128, 30 + TT], BF16) for c in range(8)]
        t_h1h = [T("h1h%d" % c) for c in range(8)]
        t_h1c = [T("h1c%d" % c) for c in range(8)]
        Vsb = sb("Vsb", [128, 8 * TT], BF16)
        t_V = [T("V%d" % c) for c in range(8)]
        rstdT = sb("rstdT", [128, TT], F32)
        nmr = sb("nmr", [128, TT], F32)
        t_rstdT, t_nmr = T("rstdT"), T("nmr")
        cgT = sb("cgT", [128, 8 * TT], BF16)
        t_cgT = [T("cgT%d" % c) for c in range(8)]
        m1 = sb("m1", [128, 8 * TT], BF16)
        t_m1 = [T("m1_%d" % c) for c in range(8)]
        cosb = Ring([sb("cosb%d" % i, [128, TT], F32) for i in range(1)], "cosb")
        sinb = Ring([sb("sinb%d" % i, [128, TT], F32) for i in range(1)], "sinb")
        qT, t_qT = cgT, t_cgT
        kT = sb("kT", [128, 4 * 640], BF16)
        t_kprev = T("kprev")
        t_kcur = [T("kcur%d" % c) for c in range(4)]
        Vaug = sb("Vaug", [128, 5 * 4 * 65], BF16)
        t_vprev = T("vprev")
        t_vcur = [T("vcur%d" % c) for c in range(4)]
        sgaT, t_sgaT = Vsb, t_V
        Pb = Ring([sb("Pb%d" % i, [128, 1024], BF16) for i in range(2)], "Pb")
        small = Ring([sb("small%d" % i, [128, 8], F32) for i in range(3)], "small")
        a_tm = [sb("a_tm%d" % i, [128, D], BF16) for i in range(4)]
        t_atm = [[T("atm%d_%d" % (i, h)) for h in range(4)] for i in range(4)]
        agT = sb("agT", [128, 8 * TT], BF16)
        t_ag = [[T("ag%d_%d" % (c, b)) for b in range(4)] for c in range(8)]
        mT = sb("mT", [128, 8 * TT], BF16)
        t_mT = [T("mT%d" % c) for c in range(8)]
        xo = Ring([sb("xo%d" % i, [128, D], F32) for i in range(2)], "xo")
        ssq = sb("ssq", [128, 16], F32)
        t_ssq = T("ssq")
        t_ssq2 = T("ssq2")
        g_tab = sb("g_tab", [128, D], F32)
        fg_tab = sb("fg_tab", [128, D], F32)
        colp = sb("colp", [128, 24], F32)
        wdw = sb("wdw", [128, 8 * 31], F32)
        esink = sb("esink", [128, 16], F32)
        maskb = sb("maskb", [128, 256], BF16)
        ident = sb("identb", [128, 128], BF16)
        ones = sb("ones", [128, 128], BF16)
        rpm = sb("rpm", [128, 128], BF16)
        t_rpm = T("rpm")
        qhl = Ring([sb("qhl%d" % i, [128, TT], BF16) for i in range(6)], "qhl")
        accf = Ring([sb("accf%d" % i, [128, TT], F32) for i in range(2)], "accf")
        t_const = T("const")
        t_esink = T("esink")
        dummy = sb("dummy", [128, 8], F32)
        t_dummy = T("dummy")
        t_warm = T("warm")

        def warm(func):
            P.add("act", lambda e: e.activation(out=dummy[:, 4:5], in_=ones[:, 0:1], func=func), r=[t_c7], w=[t_warm])

        pd = [ps("pd%d" % i, [128, 1024], F32) for i in range(3)]
        p6 = ps("p6", [128, 512], F32)
        ptr = ps("ptr", [128, 1024], BF16)
        ptrf = ptr[:, :].bitcast(F32)
        p6b = p6[:, :].bitcast(BF16)
        trs = [(ptr[:, :], None), (p6b, None)]
        tr_i = [0]
        t_pd = [[T("pd%d_%d" % (i, h), ps=True) for h in range(2)] for i in range(3)]
        t_p6 = T("p6", ps=True)
        t_ptr = T("ptr", ps=True)
        bank_all = [(pd[0][:, 0:512], t_pd[0][0]), (pd[1][:, 0:512], t_pd[1][0]), (pd[2][:, 0:512], t_pd[2][0]),
                    (p6[:, :], t_p6),
                    (pd[0][:, 512:1024], t_pd[0][1]), (pd[1][:, 512:1024], t_pd[1][1]), (pd[2][:, 512:1024], t_pd[2][1])]
        bank_job = [(p6[:, :], t_p6), (ptrf, t_ptr)]
        trs = [(ptr[:, :], t_ptr), (p6b, t_p6)]

        def tr_bank():
            k = tr_i[0] % 2
            tr_i[0] += 1
            return trs[k]
        bank_O = [(pd[2][:, 0:512], t_pd[2][0]), (pd[2][:, 512:1024], t_pd[2][1])]
        ring_state = {"all": 0, "job": 0, "O": 0, "S": 0}
        bank_mode = ["all"]

        def gen_bank():
            lst = bank_all if bank_mode[0] == "all" else bank_job
            k = ring_state[bank_mode[0]] % len(lst)
            ring_state[bank_mode[0]] += 1
            return lst[k]

        def o_bank():
            k = ring_state["O"] % 2
            ring_state["O"] += 1
            return bank_O[k]

        def s_double():
            k = ring_state["S"] % 2
            ring_state["S"] += 1
            return pd[k], t_pd[k]

        sems = {e: es.enter_context(nc.semaphore("s_" + e)) for e in ENGS}
        block = es.enter_context(nc.Block())

        def ld(eng, out_ap, in_ap, key, w):
            return P.add(eng, lambda e: e.dma_start(out=out_ap, in_=in_ap), w=w, dma=key)

        ld("sp", g_tab[:, :], g_tab_d[:, :], "c0", [t_const])
        t_c1, t_c2, t_c3, t_c4, t_c5, t_c6, t_c7 = [T("c%d" % i) for i in range(1, 8)]
        ld("sp", fg_tab[:, :], fg_tab_d[:, :], "c1", [t_c1])
        ld("sp", colp[:, :], colp_d[:, :], "c2", [t_c2])
        ld("sp", wdw[:, :], wdw_d[:, :], "c3", [t_c3])
        ld("sp", esink[:, :], sinks_d[:, :], "c4", [t_c4])
        ld("pool", maskb[:, :], mask_d[:, :], "c5", [t_c5])
        ld("pool", ident[:, :], ident_d[:, :], "c6", [t_c6])
        ld("pool", rpm[:, :], rperm_d[:, :], "c8", [t_rpm])
        P.add("pool", lambda e: e.memset(ones[:, :], 1.0), w=[t_c7])
        P.add("pool", lambda e: e.memset(dummy[:, 6:7], 0.125), w=[])
        P.add("act", lambda e: e.activation(out=esink[:, :], in_=esink[:, :], func=AF.Exp), r=[t_c4], w=[t_esink])
        P.add("pool", lambda e: e.memset(Vaug[:, :], 1.0), w=[t_vprev] + t_vcur)
        for c in range(8):
            P.add("pool", lambda e, c=c: e.memset(h1[c][:, 0:30], 0.0), w=[t_h1h[c]])
        P.add("pool", lambda e: e.memset(kT[:, :], 0.0), w=[t_kprev] + t_kcur)

        def slot3(si, n):
            return slots[si][:, 0:8 * n].rearrange("p (k n) -> p k n", k=8)

        def bwidth(kind):
            if kind == "diag":
                return (31 - NDVE) * 128
            return WIDTH.get(kind, SLOTW)

        def prep_block(kind, j, si, bi):
            sl = slots[si]
            tp = slot_parts[si]
            key = "wp%d" % si
            if kind == "glu":
                v = slot3(si, 256)
                ld("pool", v[:, :, 0:128], w_in_v[:, :, j * 128:(j + 1) * 128], key, [tp[0]])
                ld("pool", v[:, :, 128:256], w_in_v[:, :, 1024 + j * 128:1024 + (j + 1) * 128], key, [tp[1]])
            elif kind == "diag":
                v = sl[:, 0:31 * 128].rearrange("p (k n) -> p k n", k=31)
                NPE_ = 31 - NDVE
                for k in range(NPE_):
                    P.add("pool", lambda e, k=k, v=v: e.tensor_tensor(
                        out=v[:, k, :], in0=ident[:, :],
                        in1=wdw[:, j * 31 + k:j * 31 + k + 1].to_broadcast([128, 128]), op=ALU.mult),
                          r=[t_c3, t_c6], w=[tp[0]] if k == 0 else [tp[1]] if k == NPE_ - 1 else [])
            elif kind in ("cgate", "agate", "wout", "gconv", "gattn", "wco", "wao"):
                v = slot3(si, 512)
                src = {"cgate": lambda: w_in_v[:, :, 2048 + j * 512:2048 + (j + 1) * 512],
                       "agate": lambda: w_in_v[:, :, 4608 + j * 512:4608 + (j + 1) * 512],
                       "gconv": lambda: w_in_v[:, :, 5632 + j * 512:5632 + (j + 1) * 512],
                       "gattn": lambda: w_in_v[:, :, 6656 + j * 512:6656 + (j + 1) * 512],
                       "wout": lambda: w_out_v[:, :, j * 512:(j + 1) * 512],
                       "wco": lambda: w_co_v[:, :, j * 512:(j + 1) * 512],
                       "wao": lambda: w_ao_v[:, :, j * 512:(j + 1) * 512]}[kind]()
                ld("pool", v[:, :, 0:256], src[:, :, 0:256], key, [tp[0]])
                ld("pool", v[:, :, 256:512], src[:, :, 256:512], key, [tp[1]])
            elif kind == "v":
                v = slot3(si, 256)
                ld("pool", v[:, :, :], w_in_v[:, :, 4352:4608], key, [tp[0]])
            elif kind == "q":
                v = slot3(si, 256)
                for ci in range(2):
                    c0 = 3072 + (2 * j + ci) * 128
                    ld("pool", v[:, :, ci * 128:(ci + 1) * 128], w_in_v[:, :, c0:c0 + 128], key, [tp[ci]])
            elif kind == "k":
                v = slot3(si, 128)
                ld("pool", v[:, :, 0:64], w_in_v[:, :, 4096 + j * 64:4096 + (j + 1) * 64], key, [tp[0]])
                ld("pool", v[:, :, 64:128], w_in_v[:, :, 4096 + j * 64:4096 + (j + 1) * 64], key, [tp[1]])
            else:
                raise ValueError(kind)
            t_s = T("scr%d" % bi)
            W = bwidth(kind)
            P.add("sp", lambda e, sl=sl, bi=bi, W=W: e.dma_start(out=wscr[bi, :, 0:W], in_=sl[:, 0:W]),
                  r=list(tp), w=[t_s], dma="wst%d" % si)
            return t_s

        t_scr = {}
        st = {"next_load": 0, "next_use": 0}
        loaded = {}
        total_stream = NT * NB

        def issue_load():
            n = st["next_load"]
            st["next_load"] += 1
            it, bi = divmod(n, NB)
            si = n % NSLOT
            tp = slot_parts[si]
            if it == 0:
                kind, j = order[bi]
                P.add("pool", lambda e: e.memset(dummy[:, 0:1], 0.0), w=list(tp) + [t_dummy])
                t_scr[bi] = prep_block(kind, j, si, bi)
            else:
                W = bwidth(order[bi][0])
                P.add("sp", lambda e, si=si, bi=bi, W=W: e.dma_start(out=slots[si][:, 0:W], in_=wscr[bi, :, 0:W]),
                      r=[t_scr[bi]], w=list(tp), dma="w%d" % si)
            loaded[n] = si

        def wnext(kind, j):
            if discover:
                disc.append((kind, j))
                return 0, slot_parts[0]
            n = st["next_use"]
            st["next_use"] += 1
            while st["next_load"] < min(total_stream, n + NSLOT - 1) or n not in loaded:
                issue_load()
            it, bi = divmod(n, NB)
            assert order[bi] == (kind, j), (order[bi], kind, j)
            si = loaded[n]
            return si, slot_parts[si]

        def mm(out, lhsT, rhs, start, stop, r, w):
            P.add("pe", lambda e: e.matmul(out, lhsT=lhsT, rhs=rhs, start=start, stop=stop), r=r, w=w)

        hT3 = hT[:, :].rearrange("p (k n) -> p k n", k=KC)

        def proj_fm(si, tp, col0, bank, t_bank, ncols=512):
            v = slot3(si, ncols)
            for kc in range(KC):
                mm(bank, v[:, kc, col0:col0 + 128], hT3[:, kc, :], kc == 0, kc == KC - 1,
                   r=list(tp) + t_hT, w=[t_bank])

        hbs = {}

        def stage_N_load(it):
            P.tag = "N"
            par = it % 2
            for b in range(4):
                xt, t_x = xs[par * 4 + b], t_xs[par * 4 + b]
                row0 = it * TT + b * 128
                P.add("sp", lambda e, xt=xt, row0=row0: e.dma_start(out=xt[:, :], in_=x[row0:row0 + 128, :]),
                      w=[t_x], dma="x%d" % (par * 4 + b))

        def stage_N_pre(it):
            P.tag = "N"
            par = it % 2
            for b in range(4):
                xt, t_x = xs[par * 4 + b], t_xs[par * 4 + b]
                hbt, t_hb = hb.get()
                hbs[(it, b)] = (hbt, t_hb)
                P.add("act", lambda e, xt=xt, b=b, hbt=hbt: e.activation(out=hbt[:, :], in_=xt[:, :], func=AF.Square,
                                                                          accum_out=ssq[:, b:b + 1]),
                      r=[t_x], w=[t_hb, t_ssq])
            P.add("act", lambda e: e.activation(out=ssq[:, 4:8], in_=ssq[:, 0:4], func=AF.Ln, scale=1.0 / D, bias=EPS),
                  r=[t_ssq], w=[t_ssq])
            P.add("act", lambda e: e.activation(out=ssq[:, 4:8], in_=ssq[:, 4:8], func=AF.Exp, scale=-0.5),
                  r=[t_ssq], w=[t_ssq])
            for b in range(4):
                xt, t_x = xs[par * 4 + b], t_xs[par * 4 + b]
                hbt, t_hb = hbs[(it, b)]
                P.add("dve", lambda e, xt=xt, hbt=hbt, b=b: e.scalar_tensor_tensor(
                    out=hbt[:, :], in0=xt[:, :], scalar=ssq[:, 4 + b:5 + b], in1=g_tab[:, :], op0=ALU.mult, op1=ALU.mult),
                    r=[t_x, t_ssq, t_const], w=[t_hb])

        def stage_N_pe(it):
            P.tag = "N"
            for b in range(4):
                hbt, t_hb = hbs.pop((it, b))
                trb, t_trb = tr_bank()
                for kc in range(KC):
                    P.add("pe", lambda e, hbt=hbt, kc=kc, trb=trb: e.transpose(out=trb[:, kc * 128:(kc + 1) * 128],
                                                                                in_=hbt[:, kc * 128:(kc + 1) * 128],
                                                                                identity=ident[:, :]),
                          r=[t_hb, t_c6], w=[t_trb])
                P.add("act", lambda e, b=b, trb=trb: e.activation(out=hT3[:, :, b * 128:(b + 1) * 128],
                                                                   in_=trb.rearrange("p (k n) -> p k n", k=KC), func=AF.Copy),
                      r=[t_trb], w=[t_hT[b]])

        kT3 = kT[:, :].rearrange("p (h n) -> p h n", h=4)
        rope_tabs = {}
        kstate = {}
        bank_hook = [None]

        def rope_p1(si, tps, col0, bank_fn, cs, t_cs, ncols=512):
            bQ, t_bQ = bank_fn()
            proj_fm(si, tps, col0, bQ, t_bQ, ncols)
            qh, t_qh = qhl.get()
            ql, t_ql = qhl.get()
            a, t_a = tmpf.get()
            P.add("act", lambda e: e.activation(out=qh[:, :], in_=bQ, func=AF.Copy), r=[t_bQ], w=[t_qh])
            P.add("dve", lambda e: e.tensor_tensor(out=ql[:, :], in0=bQ, in1=qh[:, :], op=ALU.subtract),
                  r=[t_bQ, t_qh], w=[t_ql])
            P.add("dve", lambda e: e.tensor_tensor(out=a[:, :], in0=bQ, in1=cs[:, :], op=ALU.mult),
                  r=[t_bQ, t_cs], w=[t_a])
            return (qh, t_qh, ql, t_ql, a, t_a)

        def rope_p2(state, bank_fn, sn, t_sn, out_ap, t_out, defer_add=False):
            qh, t_qh, ql, t_ql, a, t_a = state
            bR, t_bR = bank_fn()
            mm(bR, rpm[:, :], qh[:, :], True, False, r=[t_rpm, t_qh], w=[t_bR])
            mm(bR, rpm[:, :], ql[:, :], False, True, r=[t_rpm, t_ql], w=[t_bR])
            b, t_b = tmpf.get()
            P.add("dve", lambda e: e.tensor_tensor(out=b[:, :], in0=bR, in1=sn[:, :], op=ALU.mult),
                  r=[t_bR, t_sn], w=[t_b])
            def final_add():
                P.add("dve", lambda e: e.tensor_tensor(out=out_ap, in0=a[:, :], in1=b[:, :], op=ALU.add),
                      r=[t_a, t_b], w=[t_out])
            if defer_add:
                return final_add
            final_add()

        def kprep(it):
            P.tag = "B.k"
            cs, t_cs = cosb.get()
            sn, t_sn = sinb.get()
            P.add("sp", lambda e: e.dma_start(out=cs[:, :], in_=cos_d[:, it * TT:(it + 1) * TT]), w=[t_cs], dma="cs0")
            P.add("sp", lambda e: e.dma_start(out=sn[:, :], in_=sin_d[:, it * TT:(it + 1) * TT]), w=[t_sn], dma="sn0")
            rope_tabs[it] = (cs, t_cs, sn, t_sn)

        kpend = {}

        def kjob_p1(it, hk):
            P.tag = "B.k"
            cs, t_cs, sn, t_sn = rope_tabs[it]
            si, tp = wnext("k", hk)
            kpend[(it, hk)] = rope_p1(si, (tp[0], tp[1]), 0, bank_hook[0], cs, t_cs, ncols=128)

        def kjob_p2(it, hk):
            P.tag = "B.k"
            cs, t_cs, sn, t_sn = rope_tabs[it]
            rope_p2(kpend.pop((it, hk)), bank_hook[0], sn, t_sn, kT3[:, hk, 128:640], t_kcur[hk])

        def stage_A(it):
            bank_mode[0] = "all"
            S1, t_S1 = bank_all[5]
            S2, t_S2 = bank_all[6]
            ring5 = [bank_all[i] for i in (0, 1, 2, 3, 4)]
            r5 = [0]

            def bank5():
                k = r5[0] % len(ring5)
                r5[0] += 1
                return ring5[k]

            bank_hook[0] = bank5

            P.tag = "A.cgate"
            for j in range(2):
                si, tp = wnext("cgate", j)
                for cc in range(4):
                    c = j * 4 + cc
                    bG, t_bG = bank5()
                    proj_fm(si, tp[0:2], cc * 128, bG, t_bG)
                    P.add("act", lambda e, bG=bG, c=c: e.activation(out=mT[:, c * TT:(c + 1) * TT], in_=bG, func=AF.Silu),
                          r=[t_bG], w=[t_mT[c]])

            def glu(c):
                P.tag = "A.glu"
                si, tp = wnext("glu", c)
                v = slot3(si, 256)
                bA, t_A = bank5()
                bB, t_B = bank5()
                for kc in range(KC):
                    mm(bA, v[:, kc, 0:128], hT3[:, kc, :], kc == 0, kc == KC - 1, r=list(tp[0:2]) + t_hT, w=[t_A])
                for kc in range(KC):
                    mm(bB, v[:, kc, 128:256], hT3[:, kc, :], kc == 0, kc == KC - 1, r=list(tp[0:2]) + t_hT, w=[t_B])
                sg, t_sg = tmpf.get()
                P.add("act", lambda e: e.activation(out=sg[:, :], in_=bB, func=AF.Sigmoid), r=[t_B], w=[t_sg])
                P.add("dve", lambda e: e.tensor_tensor(out=h1[c][:, 30:30 + TT], in0=bA, in1=sg[:, :], op=ALU.mult),
                      r=[t_A, t_sg], w=[t_h1c[c]])

            pend = []

            def stats(c, v2, t_v2):
                P.tag = "A.conv"
                mm(S1, ones[:, :], Vsb[:, c * TT:(c + 1) * TT], c == 0, c == 7, r=[t_c7, t_V[c]], w=[t_S1])
                mm(S2, ones[:, :], v2[:, :], c == 0, c == 7, r=[t_c7, t_v2], w=[t_S2])

            def conv(c):
                P.tag = "A.conv"
                if c == 7:
                    warm(AF.Ln)
                si, tp = wnext("diag", c)
                dv = slots[si][:, 0:31 * 128].rearrange("p (k n) -> p k n", k=31)
                bV, t_bV = bank5()
                NPE = 31 - NDVE
                for k in range(NPE):
                    mm(bV, dv[:, k, :], h1[c][:, k:k + TT], k == 0, k == NPE - 1, r=list(tp[0:2]) + [t_h1c[c], t_h1h[c]],
                       w=[t_bV])
                acc, t_acc = accf.get()
                for k in range(NPE, 31):
                    wk = wdw[:, c * 31 + k:c * 31 + k + 1]
                    if k == NPE:
                        P.add("dve", lambda e, k=k, wk=wk: e.tensor_scalar(out=acc[:, :], in0=h1[c][:, k:k + TT], scalar1=wk,
                                                                           scalar2=None, op0=ALU.mult),
                              r=[t_h1c[c], t_h1h[c], t_c3], w=[t_acc])
                    else:
                        P.add("dve", lambda e, k=k, wk=wk: e.scalar_tensor_tensor(out=acc[:, :], in0=h1[c][:, k:k + TT], scalar=wk,
                                                                                  in1=acc[:, :], op0=ALU.mult, op1=ALU.add),
                              r=[t_h1c[c], t_h1h[c], t_c3, t_acc], w=[t_acc])
                P.add("dve", lambda e: e.scalar_tensor_tensor(out=Vsb[:, c * TT:(c + 1) * TT], in0=bV, scalar=colp[:, c:c + 1],
                                                              in1=acc[:, :], op0=ALU.add, op1=ALU.add),
                      r=[t_bV, t_c2, t_acc], w=[t_V[c]])
                v2, t_v2 = tmpb.get()
                P.add("act", lambda e: e.activation(out=v2[:, :], in_=Vsb[:, c * TT:(c + 1) * TT], func=AF.Square),
                      r=[t_V[c]], w=[t_v2])
                P.add("pool", lambda e: e.tensor_copy(out=h1[c][:, 0:30], in_=h1[c][:, TT:TT + 30]),
                      r=[t_h1c[c]], w=[t_h1h[c]])
                pend.append((c, v2, t_v2))
                if len(pend) > 1:
                    stats(*pend.pop(0))

            kprep(it)
            glu(0)
            for c in range(8):
                if c + 1 < 8:
                    glu(c + 1)
                conv(c)
                if c % 2 == 0:
                    kjob_p1(it, c // 2)
                else:
                    kjob_p2(it, c // 2)
            stats(*pend.pop(0))

            gjobs = []
            gstate = {}

            def gjob(c):
                P.tag = "A.gconv"
                if c == 2:
                    ring5.extend([bank_all[5], bank_all[6]])
                    r5[0] = 0
                j, cc = divmod(c, 4)
                if j not in gstate:
                    gstate[j] = wnext("gconv", j)
                si, tp = gstate[j]
                bG, t_bG = bank5()
                proj_fm(si, tp[0:2], cc * 128, bG, t_bG)
                P.add("act", lambda e: e.activation(out=agT[:, c * TT:(c + 1) * TT], in_=bG, func=AF.Tanh, scale=0.5),
                      r=[t_bG], w=t_ag[c])

            P.tag = "A.ln"
            mean, t_mean = tmpf.get()
            msq, t_msq = tmpf.get()
            var, t_var = tmpf.get()
            P.add("dve", lambda e: e.tensor_scalar(out=mean[:, :], in0=S1, scalar1=1.0 / D, scalar2=None, op0=ALU.mult),
                  r=[t_S1], w=[t_mean])
            P.add("dve", lambda e: e.tensor_tensor(out=msq[:, :], in0=mean[:, :], in1=mean[:, :], op=ALU.mult),
                  r=[t_mean], w=[t_msq])
            P.add("dve", lambda e: e.scalar_tensor_tensor(out=var[:, :], in0=S2, scalar=1.0 / D, in1=msq[:, :],
                                                           op0=ALU.mult, op1=ALU.subtract),
                  r=[t_S2, t_msq], w=[t_var])
            P.add("act", lambda e: e.activation(out=var[:, :], in_=var[:, :], func=AF.Ln, bias=EPS), r=[t_var], w=[t_var])
            P.add("act", lambda e: e.activation(out=rstdT[:, :], in_=var[:, :], func=AF.Exp, scale=-0.5),
                  r=[t_var], w=[t_rstdT])
            P.add("dve", lambda e: e.scalar_tensor_tensor(out=nmr[:, :], in0=mean[:, :], scalar=-1.0, in1=rstdT[:, :],
                                                           op0=ALU.mult, op1=ALU.mult),
                  r=[t_mean, t_rstdT], w=[t_nmr])

            prev = None
            for c in range(8):
                gjob(c)
                P.tag = "A.norm"
                z, t_z = tmpf.get()
                P.add("dve", lambda e, z=z, c=c: e.tensor_tensor(out=z[:, :], in0=Vsb[:, c * TT:(c + 1) * TT], in1=rstdT[:, :],
                                                                  op=ALU.mult),
                      r=[t_V[c], t_rstdT], w=[t_z])
                P.add("dve", lambda e, z=z: e.tensor_tensor(out=z[:, :], in0=z[:, :], in1=nmr[:, :], op=ALU.add),
                      r=[t_z, t_nmr], w=[t_z])
                ca, t_ca = tmpb.get()
                P.add("act", lambda e, z=z, ca=ca, c=c: e.activation(out=ca[:, :], in_=z[:, :], func=AF.Silu,
                                                                      scale=colp[:, 8 + c:9 + c], bias=colp[:, 16 + c:17 + c]),
                      r=[t_z, t_c2], w=[t_ca])

                def cgmul(ca=ca, t_ca=t_ca, c=c):
                    P.add("dve", lambda e: e.tensor_tensor(out=cgT[:, c * TT:(c + 1) * TT], in0=ca[:, :],
                                                           in1=mT[:, c * TT:(c + 1) * TT], op=ALU.mult),
                          r=[t_ca, t_mT[c]], w=[t_cgT[c]])
                if prev is not None:
                    prev()
                prev = cgmul
            prev()
            stage_B0(it)

            P.tag = "A.wco"
            cg3 = cgT[:, :].rearrange("p (k n) -> p k n", k=8)
            for j in range(2):
                si, tp = wnext("wco", j)
                v = slot3(si, 512)
                for dd in range(4):
                    dch = j * 4 + dd
                    bY, t_bY = bank5()
                    for kc in range(KC):
                        mm(bY, v[:, kc, dd * 128:(dd + 1) * 128], cg3[:, kc, :], kc == 0, kc == KC - 1,
                           r=list(tp[0:2]) + t_cgT, w=[t_bY])
                    P.add("dve", lambda e, bY=bY, dch=dch: e.scalar_tensor_tensor(
                        out=m1[:, dch * TT:(dch + 1) * TT], in0=agT[:, dch * TT:(dch + 1) * TT], scalar=1.0, in1=bY,
                        op0=ALU.add, op1=ALU.mult),
                          r=[t_bY] + t_ag[dch], w=[t_m1[dch]])

        Va4 = Vaug[:, :].rearrange("p (b h d) -> p b h d", b=5, h=4)
        mask3 = maskb[:, :].rearrange("p (k n) -> p k n", k=2)
        q3 = qT[:, :].rearrange("p (c n) -> p c n", c=8)
        ag3 = agT[:, :].rearrange("p (c n) -> p c n", c=8)
        sga3 = sgaT[:, :].rearrange("p (c n) -> p c n", c=8)

        qpend = {}

        def q_p1(it, si, tp, ci, c):
            P.tag = "B.q"
            cs, t_cs, sn, t_sn = rope_tabs[it]
            qpend[(it, c)] = rope_p1(si, (tp[0], tp[1]), ci * 128, gen_bank, cs, t_cs, ncols=256)

        def q_p2(it, c, defer_add=False):
            P.tag = "B.q"
            cs, t_cs, sn, t_sn = rope_tabs[it]
            return rope_p2(qpend.pop((it, c)), gen_bank, sn, t_sn, qT[:, c * TT:(c + 1) * TT], t_qT[c], defer_add)

        def stage_B0(it):
            bank_mode[0] = "all"
            warm(AF.Exp)
            P.tag = "B.v"
            si, tp = wnext("v", 0)
            vv = slot3(si, 256)
            for b in range(4):
                bV, t_bV = gen_bank()
                for kc in range(KC):
                    mm(bV[:, 0:256], hT3[:, kc, b * 128:(b + 1) * 128], vv[:, kc, :], kc == 0, kc == KC - 1,
                       r=[tp[0], t_hT[b]], w=[t_bV])
                P.add("act", lambda e, bV=bV, b=b: e.activation(out=Va4[:, b + 1, :, 0:64],
                                                                 in_=bV[:, 0:256].rearrange("p (h d) -> p h d", h=4),
                                                                 func=AF.Copy),
                      r=[t_bV], w=[t_vcur[b]])
            si, tp = wnext("q", 0)
            q_p1(it, si, tp, 0, 0)
            q_p1(it, si, tp, 1, 1)
            q0_adds[it] = [q_p2(it, 0, defer_add=True), q_p2(it, 1, defer_add=True)]

        q0_adds = {}

        def stage_B(it):
            bank_mode[0] = "all"
            P.tag = "B.q"
            for f in q0_adds.pop(it):
                f()

            jobs = []

            def add_q_jobs(j):
                holder = {}

                def get():
                    if "s" not in holder:
                        holder["s"] = wnext("q", j)
                    return holder["s"]
                for ci in range(2):
                    jobs.append(lambda ci=ci, j=j: q_p1(it, *get(), ci, 2 * j + ci))
                    jobs.append(lambda ci=ci, j=j: q_p2(it, 2 * j + ci))

            def add_gate_jobs(kind, j):
                holder = {}

                def get():
                    if "s" not in holder:
                        holder["s"] = wnext(kind, j)
                    return holder["s"]

                def job(cc):
                    si, tp = get()
                    c = j * 4 + cc
                    P.tag = "B." + kind
                    bG, t_bG = gen_bank()
                    proj_fm(si, tp[0:2], cc * 128, bG, t_bG)
                    if kind == "agate":
                        th, t_th = tmpb.get()
                        P.add("act", lambda e: e.activation(out=th[:, :], in_=bG, func=AF.Tanh, scale=0.5),
                              r=[t_bG], w=[t_th])
                        P.add("dve", lambda e: e.scalar_tensor_tensor(out=sgaT[:, c * TT:(c + 1) * TT], in0=th[:, :], scalar=1.0,
                                                                      in1=bG, op0=ALU.add, op1=ALU.mult),
                              r=[t_bG, t_th], w=[t_sgaT[c]])
                    else:
                        P.add("act", lambda e: e.activation(out=mT[:, c * TT:(c + 1) * TT], in_=bG, func=AF.Tanh, scale=0.5),
                              r=[t_bG], w=[t_mT[c]])
                for cc in range(4):
                    jobs.append(lambda cc=cc: job(cc))

            add_q_jobs(1)
            add_gate_jobs("agate", 0)
            add_q_jobs(2)
            add_gate_jobs("agate", 1)
            add_q_jobs(3)
            add_gate_jobs("gattn", 0)
            add_gate_jobs("gattn", 1)
            sched = [[0], [1, 2], [3, 4], [5, 6], [7, 8], [9, 10], [11, 12], [13, 14], [15, 16], [17, 18], [19, 20], [21, 22],
                     [23, 24], [25, 26], [27], []]

            iters = [(hk, n) for hk in range(4) for n in range(4)]
            state = {}

            def S_part(i):
                hk, n = iters[i]
                P.tag = "B.attn"
                first = (it == 0 and n == 0)
                Sd, tS = s_double()
                for r in range(2):
                    for kb in range(2):
                        if first and kb == 0:
                            continue
                        kcol0 = n * 128 + kb * 128
                        t_k = [t_kcur[hk]] + ([t_kprev] if (n == 0 and kb == 0) else [])
                        base = r * 512 + kb * 256
                        mm(Sd[:, base:base + 256], kT3[r * 64:(r + 1) * 64, hk, kcol0:kcol0 + 128],
                           q3[r * 64:(r + 1) * 64, 2 * hk:2 * hk + 2, n * 128:(n + 1) * 128], True, True,
                           r=t_k + [t_qT[2 * hk], t_qT[2 * hk + 1]], w=[tS[r]])
                pb, t_p = Pb.get()
                k0 = 1 if first else 0
                if first:
                    S3 = Sd[:, :].rearrange("p (r k x) -> p r k x", r=2, k=2)
                    P3 = pb[:, :].rearrange("p (r k x) -> p r k x", r=2, k=2)
                    P.add("act", lambda e: e.activation(out=P3[:, :, 1, :], in_=S3[:, :, 1, :], func=AF.Exp, scale=0.125),
                          r=list(tS), w=[t_p])
                else:
                    P.add("act", lambda e: e.activation(out=pb[:, :], in_=Sd[:, :], func=AF.Exp, scale=0.125),
                          r=list(tS), w=[t_p])
                for r in range(2):
                    Pr = pb[:, r * 512:(r + 1) * 512].rearrange("p (k a q) -> p k a q", k=2, a=2)
                    P.add("dve", lambda e, Pr=Pr: e.tensor_tensor(
                        out=Pr[:, k0:2, :, :], in0=Pr[:, k0:2, :, :],
                        in1=mask3[:, k0:2, :].unsqueeze(2).to_broadcast([128, 2 - k0, 2, 128]), op=ALU.mult),
                        r=[t_p, t_c5], w=[t_p])
                state[i] = (pb, t_p, first)

            def PV_part(i):
                hk, n = iters[i]
                P.tag = "B.attn"
                pb, t_p, first = state.pop(i)
                P5 = pb[:, :].rearrange("p (r k a q) -> p r k a q", r=2, a=2, k=2)
                Ob, t_Ob = o_bank()
                O3 = Ob[:, 0:260].rearrange("p (g d) -> p g d", g=4)
                for g in range(4):
                    r, pair = g % 2, g // 2
                    for kb in range(2):
                        if first and kb == 0:
                            continue
                        vb = n + kb
                        t_v = t_vprev if vb == 0 else t_vcur[vb - 1]
                        mm(O3[:, g, :], P5[:, r, kb, pair, :], Va4[:, vb, hk, :],
                           (kb == 0) or first, kb == 1, r=[t_p, t_v], w=[t_Ob])
                sm, t_sm = small.get()
                at3 = a_tm[n][:, :].rearrange("p (g d) -> p g d", g=16)
                P.add("dve", lambda e: e.tensor_tensor(out=sm[:, 0:4], in0=O3[:, :, 64], in1=esink[:, hk * 4:hk * 4 + 4],
                                                       op=ALU.add),
                      r=[t_Ob, t_esink], w=[t_sm])
                P.add("dve", lambda e: e.reciprocal(out=sm[:, 4:8], in_=sm[:, 0:4]), r=[t_sm], w=[t_sm])
                P.add("dve", lambda e: e.tensor_tensor(
                    out=at3[:, hk * 4:hk * 4 + 4, :], in0=O3[:, :, 0:64],
                    in1=sm[:, 4:8].unsqueeze(2).to_broadcast([128, 4, 64]), op=ALU.mult),
                    r=[t_Ob, t_sm], w=[t_atm[n][hk]])

            bank_mode[0] = "job"
            S_part(0)
            for i in range(16):
                if i + 1 < 16:
                    S_part(i + 1)
                for jn in sched[i]:
                    jobs[jn]()
                PV_part(i)
            bank_mode[0] = "all"

        def stage_B2(it):
            bank_mode[0] = "all"
            P.add("pool", lambda e: e.tensor_copy(out=kT3[:, :, 0:128], in_=kT3[:, :, 512:640]), r=t_kcur, w=[t_kprev])
            P.add("pool", lambda e: e.tensor_copy(out=Va4[:, 0, :, 0:64], in_=Va4[:, 4, :, 0:64]), r=[t_vcur[3]], w=[t_vprev])
            P.tag = "B.attnT"
            for n in range(4):
                trb, t_trb = tr_bank()
                for c in range(8):
                    P.add("pe", lambda e, n=n, c=c, trb=trb: e.transpose(out=trb[:, c * 128:(c + 1) * 128],
                                                                          in_=a_tm[n][:, c * 128:(c + 1) * 128], identity=ident[:, :]),
                          r=t_atm[n] + [t_c6], w=[t_trb])
                P.add("dve", lambda e, n=n, trb=trb: e.scalar_tensor_tensor(out=ag3[:, :, n * 128:(n + 1) * 128],
                                                                            in0=trb.rearrange("p (c n) -> p c n", c=8), scalar=0.5,
                                                                            in1=sga3[:, :, n * 128:(n + 1) * 128], op0=ALU.mult, op1=ALU.mult),
                      r=[t_trb] + t_sgaT, w=[t_ag[c][n] for c in range(8)])
        def stage_B3(it):
            bank_mode[0] = "all"
            P.tag = "B.wao"
            for j in range(2):
                si, tp = wnext("wao", j)
                v = slot3(si, 512)
                for dd in range(4):
                    dch = j * 4 + dd
                    bY, t_bY = gen_bank()
                    for kc in range(KC):
                        mm(bY, v[:, kc, dd * 128:(dd + 1) * 128], ag3[:, kc, :], kc == 0, kc == KC - 1,
                           r=list(tp[0:2]) + t_ag[kc], w=[t_bY])
                    ya, t_ya = tmpf.get()
                    P.add("dve", lambda e, bY=bY, ya=ya, dch=dch: e.scalar_tensor_tensor(
                        out=ya[:, :], in0=mT[:, dch * TT:(dch + 1) * TT], scalar=1.0, in1=bY, op0=ALU.add, op1=ALU.mult),
                          r=[t_bY, t_mT[dch]], w=[t_ya])
                    P.add("dve", lambda e, ya=ya, dch=dch: e.tensor_tensor(out=mT[:, dch * TT:(dch + 1) * TT], in0=ya[:, :],
                                                                           in1=m1[:, dch * TT:(dch + 1) * TT], op=ALU.add),
                          r=[t_ya, t_m1[dch]], w=[t_mT[dch]])

        last_store = []

        def stage_O(it):
            P.tag = "O"
            bank_mode[0] = "all"
            par = it % 2
            si0, tp0 = wnext("wout", 0)
            si1, tp1 = wnext("wout", 1)
            wv = [slot3(si0, 512), slot3(si1, 512)]
            tps = [tp0, tp1]
            m3 = mT[:, :].rearrange("p (k n) -> p k n", k=8)
            for b in range(4):
                xt, t_x = xs[par * 4 + b], t_xs[par * 4 + b]
                xot, t_xo = xo.get()
                for half in range(2):
                    bO, t_bO = gen_bank()
                    for kc in range(KC):
                        mm(bO, m3[:, kc, b * 128:(b + 1) * 128], wv[half][:, kc, :], kc == 0, kc == KC - 1,
                           r=list(tps[half][0:2]) + t_mT, w=[t_bO])
                    P.add("dve", lambda e, bO=bO, xot=xot, xt=xt, half=half: e.scalar_tensor_tensor(
                        out=xot[:, half * 512:(half + 1) * 512], in0=bO, scalar=0.5, in1=xt[:, half * 512:(half + 1) * 512],
                        op0=ALU.mult, op1=ALU.add),
                        r=[t_bO, t_x], w=[t_xo])
                P.add("act", lambda e, xot=xot, b=b: e.activation(out=a_tm[b][:, :], in_=xot[:, :], func=AF.Square,
                                                                  accum_out=ssq[:, 8 + b:9 + b]),
                      r=[t_xo], w=t_atm[b] + [t_ssq2])
                P.add("act", lambda e, b=b: e.activation(out=ssq[:, 12 + b:13 + b], in_=ssq[:, 8 + b:9 + b], func=AF.Ln,
                                                         scale=1.0 / D, bias=EPS), r=[t_ssq2], w=[t_ssq2])
                P.add("act", lambda e, b=b: e.activation(out=ssq[:, 12 + b:13 + b], in_=ssq[:, 12 + b:13 + b], func=AF.Exp,
                                                         scale=-0.5), r=[t_ssq2], w=[t_ssq2])
                P.add("dve", lambda e, xot=xot, b=b: e.scalar_tensor_tensor(out=xot[:, :], in0=xot[:, :], scalar=ssq[:, 12 + b:13 + b],
                                                                            in1=fg_tab[:, :], op0=ALU.mult, op1=ALU.mult),
                      r=[t_xo, t_ssq2, t_c1], w=[t_xo])
                row0 = it * TT + b * 128
                key = "y%d" % ((xo.i - 1) % 2)
                op = P.add("sp", lambda e, xot=xot, row0=row0: e.dma_start(out=y[row0:row0 + 128, :], in_=xot[:, :]),
                           r=[t_xo], w=[], dma=key)
                last_store.append(op)

        if discover:
            stage_A(0)
            stage_B(0)
            stage_B2(0)
            stage_B3(0)
            stage_O(0)
            return disc
        stage_N_load(0)
        stage_N_pre(0)
        stage_N_pe(0)
        for it in range(NT):
            stage_A(it)
            if it + 1 < NT:
                stage_N_load(it + 1)
            stage_B(it)
            stage_B2(it)
            if it + 1 < NT:
                stage_N_pre(it + 1)
            stage_B3(it)
            if it + 1 < NT:
                stage_N_pe(it + 1)
            stage_O(it)
        fin = []
        for op in last_store[-2:]:
            t = T("fin")
            t.w = op
            fin.append(t)
        P.add("sp", None, r=fin)

        import os
        if os.environ.get("KTAGS"):
            import json
            json.dump({e: [o.tag for o in P.ops[e] if o.fn is not None] for e in ENGS}, open(os.environ["KTAGS"], "w"))
        dma_sems = {k: es.enter_context(nc.semaphore("d_" + k)) for k in P.dma_keys}
        P.emit_all(block, sems, dma_sems)
    return nc


def build_program():
    order = build_nc(None)
    return build_nc(order)


def _consts():
    d = np.arange(128) % 64
    f = d % 32
    inv_freq = (10000.0 ** (-(np.arange(0, 64, 2, dtype=np.float32)) / 64.0)).astype(np.float32)
    pos = np.arange(S, dtype=np.float32)
    ang = pos[None, :] * inv_freq[f][:, None]
    cosT = np.cos(ang).astype(np.float32)
    sgn = np.where(d < 32, -1.0, 1.0).astype(np.float32)
    sinT = (np.sin(ang) * sgn[:, None]).astype(np.float32)
    j = np.arange(128)[:, None]
    i = np.arange(128)[None, :]
    mask = np.concatenate([(j > i), (i >= j)], axis=1).astype(np.float32)
    ident = np.eye(128, dtype=np.float32)
    return cosT, sinT, mask, ident


_CACHE = {}


def kernel(x, norm_g, w_in, conv_dw_w, conv_dw_b, conv_ln_g, conv_ln_b, w_conv_out, attn_sinks, w_attn_out,
           w_out, final_norm_g):
    f = lambda a: np.ascontiguousarray(np.asarray(a, dtype=np.float32))
    x = f(x)
    cosT, sinT, mask, ident = _consts()
    col = lambda v: f(np.asarray(v, np.float32).reshape(8, 128).T)
    colp = np.concatenate([col(conv_dw_b[0]), col(conv_ln_g[0]), col(conv_ln_b[0])], axis=1)
    wdw = f(np.asarray(conv_dw_w[0], np.float32).reshape(31, 8, 128).transpose(2, 1, 0).reshape(128, 8 * 31))
    shared = {
        "w_in": f(w_in[0]), "w_co": f(w_conv_out[0]), "w_ao": f(w_attn_out[0]), "w_out": f(w_out[0]),
        "g_tab": f(np.broadcast_to(np.asarray(norm_g[0], np.float32)[None, :], (128, D))),
        "fg_tab": f(np.broadcast_to(np.asarray(final_norm_g, np.float32)[None, :], (128, D))),
        "colp": f(colp), "wdw": wdw,
        "sinks": f(np.broadcast_to(np.asarray(attn_sinks[0], np.float32)[None, :], (128, 16))),
        "cosT": cosT, "sinT": sinT, "maskT": mask, "ident": ident,
        "rperm": np.ascontiguousarray(ident[:, np.arange(128) ^ 32]),
    }
    if "nc" not in _CACHE:
        _CACHE["nc"] = build_program()
    nc = _CACHE["nc"]
    in_maps = []
    for b in range(NCORES):
        m = dict(shared)
        m["x"] = np.ascontiguousarray(x[b])
        in_maps.append(m)
    res = run_bass_kernel_spmd(nc, in_maps, core_ids=list(range(NCORES)))
    out = np.stack([np.asarray(res.results[b]["y"], dtype=np.float32).reshape(S, D) for b in range(NCORES)], axis=0)
    return out
```

```python
import numpy as np
from contextlib import ExitStack
import concourse.bass as bass
import concourse.mybir as mybir
from concourse.bass_utils import run_bass_kernel_spmd

F32 = mybir.dt.float32
BF16 = mybir.dt.bfloat16
ALU = mybir.AluOpType
AF = mybir.ActivationFunctionType

ENGS = ("pe", "act", "dve", "pool", "sp")

D = 1024
S = 4096
NCORES = 8
TT = 512
NT = S // TT
KC = 8
INW = 7680
NSLOT = 5
WIDTH = {"glu": 2048, "diag": 0, "k": 1024, "v": 2048, "q": 2048}
NDVE = 7
SLOTW = 4096
EPS = 1e-5


class T:
    __slots__ = ("name", "w", "rs", "ps")

    def __init__(self, name, ps=False):
        self.name = name
        self.w = None
        self.rs = []
        self.ps = ps


class Op:
    __slots__ = ("eng", "fn", "deps", "sig", "val", "sem", "isdma", "pos", "tag")


class Prog:
    def __init__(self):
        self.ops = {e: [] for e in ENGS}
        self.n = 0
        self.dma_keys = []
        self.tag = ""

    def add(self, eng, fn, r=(), w=(), dma=None):
        op = Op()
        op.tag = self.tag
        op.eng = eng
        op.fn = fn
        op.sig = False
        op.val = 0
        op.sem = dma
        op.isdma = dma is not None
        if dma is not None and dma not in self.dma_keys:
            self.dma_keys.append(dma)
        op.pos = self.n
        self.n += 1
        deps = []
        for t in r:
            if t.w is not None:
                deps.append(t.w)
            if t.ps:
                deps.extend(o for o in t.rs if o.eng != eng)
        for t in w:
            if t.w is not None:
                deps.append(t.w)
            deps.extend(t.rs)
        keep = []
        seen = set()
        latest = {}
        for d in deps:
            if id(d) in seen or d is op:
                continue
            seen.add(id(d))
            if d.isdma:
                keep.append(d)
                continue
            if d.eng == eng and eng == "pe":
                continue
            if d.eng not in latest or latest[d.eng].pos < d.pos:
                latest[d.eng] = d
        keep.extend(latest.values())
        for d in keep:
            d.sig = True
        op.deps = keep
        for t in r:
            t.rs.append(op)
        for t in w:
            t.w = op
            t.rs = []
        self.ops[eng].append(op)
        return op

    def emit_all(self, block, sems, dma_sems):
        for e in ENGS:
            cnt = 0
            for op in self.ops[e]:
                if op.isdma:
                    continue
                if op.sig:
                    cnt += 1
                    op.val = cnt
        dcnt = {}
        for op in sorted([o for e in ENGS for o in self.ops[e] if o.isdma], key=lambda o: o.pos):
            dcnt[op.sem] = dcnt.get(op.sem, 0) + 16
            op.val = dcnt[op.sem]

        def run(e, eng):
            known = {}
            for op in self.ops[e]:
                need = {}
                for d in op.deps:
                    if d.isdma:
                        key = ("d", d.sem)
                        s = dma_sems[d.sem]
                    else:
                        key = ("c", d.eng)
                        s = sems[d.eng]
                    if need.get(key, (None, 0))[1] < d.val:
                        need[key] = (s, d.val)
                for key, (s, v) in need.items():
                    if known.get(key, 0) >= v:
                        continue
                    eng.wait_ge(s, v)
                    known[key] = v
                if op.fn is None:
                    continue
                inst = op.fn(eng)
                if op.isdma:
                    inst.then_inc(dma_sems[op.sem], 16)
                elif op.sig:
                    inst.then_inc(sems[e], 1)

        @block.tensor
        def _(eng):
            run("pe", eng)

        @block.scalar
        def _(eng):
            run("act", eng)

        @block.vector
        def _(eng):
            run("dve", eng)

        @block.gpsimd
        def _(eng):
            run("pool", eng)

        @block.sync
        def _(eng):
            run("sp", eng)
            for key, v in dcnt.items():
                eng.wait_ge(dma_sems[key], v)


class Ring:
    def __init__(self, aps, name):
        self.aps = aps
        self.ts = [T("%s%d" % (name, i)) for i in range(len(aps))]
        self.i = 0

    def get(self):
        k = self.i % len(self.aps)
        self.i += 1
        return self.aps[k], self.ts[k]


def build_nc(order=None):
    discover = order is None
    nc = bass.Bass("TRN2", target_bir_lowering=False)
    dt_in = lambda name, shape: nc.dram_tensor(name, shape, F32, kind="ExternalInput").ap()
    x = dt_in("x", [S, D])
    w_in = dt_in("w_in", [D, INW])
    w_co = dt_in("w_co", [D, D])
    w_ao = dt_in("w_ao", [D, D])
    w_out = dt_in("w_out", [D, D])
    g_tab_d = dt_in("g_tab", [128, D])
    fg_tab_d = dt_in("fg_tab", [128, D])
    colp_d = dt_in("colp", [128, 24])
    wdw_d = dt_in("wdw", [128, 8 * 31])
    sinks_d = dt_in("sinks", [128, 16])
    cos_d = dt_in("cosT", [128, S])
    sin_d = dt_in("sinT", [128, S])
    mask_d = dt_in("maskT", [128, 256])
    ident_d = dt_in("ident", [128, 128])
    rperm_d = dt_in("rperm", [128, 128])
    y = nc.dram_tensor("y", [S, D], F32, kind="ExternalOutput").ap()
    NB = 37 if discover else len(order)
    wscr = nc.dram_tensor("wscr", [NB, 128, SLOTW], BF16, kind="Internal").ap()
    disc = []

    w_in_v = w_in.rearrange("(k p) e -> p k e", p=128)
    w_co_v = w_co.rearrange("(k p) e -> p k e", p=128)
    w_ao_v = w_ao.rearrange("(k p) e -> p k e", p=128)
    w_out_v = w_out.rearrange("(k p) e -> p k e", p=128)

    P = Prog()
    with ExitStack() as es:
        def sb(name, shape, dt):
            return es.enter_context(nc.sbuf_tensor("sb_" + name, shape, dt))

        def ps(name, shape, dt):
            return es.enter_context(nc.psum_tensor("ps_" + name, shape, dt))

        slots = [sb("wslot%d" % i, [128, SLOTW], BF16) for i in range(NSLOT)]
        slot_parts = [[T("ws%d_%d" % (i, j)) for j in range(4)] for i in range(NSLOT)]
        xs = [sb("xs%d" % i, [128, D], F32) for i in range(8)]
        t_xs = [T("xs%d" % i) for i in range(8)]
        hb = Ring([sb("hb%d" % i, [128, D], BF16) for i in range(4)], "hb")
        hT = sb("hT", [128, KC * TT], BF16)
        t_hT = [T("hT%d" % i) for i in range(4)]
        tmpf = Ring([sb("tmpf%d" % i, [128, TT], F32) for i in range(5)], "tmpf")
        tmpb = Ring([sb("tmpb%d" % i, [128, TT], BF16) for i in range(3)], "tmpb")
        h1 = [sb("h1_%d" % c, [128, 30 + TT], BF16) for c in range(8)]
        t_h1h = [T("h1h%d" % c) for c in range(8)]
        t_h1c = [T("h1c%d" % c) for c in range(8)]
        Vsb = sb("Vsb", [128, 8 * TT], BF16)
        t_V = [T("V%d" % c) for c in range(8)]
        rstdT = sb("rstdT", [128, TT], F32)
        nmr = sb("nmr", [128, TT], F32)
        t_rstdT, t_nmr = T("rstdT"), T("nmr")
        cgT = sb("cgT", [128, 8 * TT], BF16)
        t_cgT = [T("cgT%d" % c) for c in range(8)]
        m1 = sb("m1", [128, 8 * TT], BF16)
        t_m1 = [T("m1_%d" % c) for c in range(8)]
        cosb = Ring([sb("cosb%d" % i, [128, TT], F32) for i in range(1)], "cosb")
        sinb = Ring([sb("sinb%d" % i, [128, TT], F32) for i in range(1)], "sinb")
        qT, t_qT = cgT, t_cgT
        kT = sb("kT", [128, 4 * 640], BF16)
        t_kprev = T("kprev")
        t_kcur = [T("kcur%d" % c) for c in range(4)]
        Vaug = sb("Vaug", [128, 5 * 4 * 65], BF16)
        t_vprev = T("vprev")
        t_vcur = [T("vcur%d" % c) for c in range(4)]
        sgaT, t_sgaT = Vsb, t_V
        Pb = Ring([sb("Pb%d" % i, [128, 1024], BF16) for i in range(2)], "Pb")
        small = Ring([sb("small%d" % i, [128, 8], F32) for i in range(3)], "small")
        a_tm = [sb("a_tm%d" % i, [128, D], BF16) for i in range(4)]
        t_atm = [[T("atm%d_%d" % (i, h)) for h in range(4)] for i in range(4)]
        agT = sb("agT", [128, 8 * TT], BF16)
        t_ag = [[T("ag%d_%d" % (c, b)) for b in range(4)] for c in range(8)]
        mT = sb("mT", [128, 8 * TT], BF16)
        t_mT = [T("mT%d" % c) for c in range(8)]
        xo = Ring([sb("xo%d" % i, [128, D], F32) for i in range(2)], "xo")
        ssq = sb("ssq", [128, 16], F32)
        t_ssq = T("ssq")
        t_ssq2 = T("ssq2")
        g_tab = sb("g_tab", [128, D], F32)
        fg_tab = sb("fg_tab", [128, D], F32)
        colp = sb("colp", [128, 24], F32)
        wdw = sb("wdw", [128, 8 * 31], F32)
        esink = sb("esink", [128, 16], F32)
        maskb = sb("maskb", [128, 256], BF16)
        ident = sb("identb", [128, 128], BF16)
        ones = sb("ones", [128, 128], BF16)
        rpm = sb("rpm", [128, 128], BF16)
        t_rpm = T("rpm")
        qhl = Ring([sb("qhl%d" % i, [128, TT], BF16) for i in range(6)], "qhl")
        accf = Ring([sb("accf%d" % i, [128, TT], F32) for i in range(2)], "accf")
        t_const = T("const")
        t_esink = T("esink")
        dummy = sb("dummy", [128, 8], F32)
        t_dummy = T("dummy")
        t_warm = T("warm")

        def warm(func):
            P.add("act", lambda e: e.activation(out=dummy[:, 4:5], in_=ones[:, 0:1], func=func), r=[t_c7], w=[t_warm])

        pd = [ps("pd%d" % i, [128, 1024], F32) for i in range(3)]
        p6 = ps("p6", [128, 512], F32)
        ptr = ps("ptr", [128, 1024], BF16)
        ptrf = ptr[:, :].bitcast(F32)
        p6b = p6[:, :].bitcast(BF16)
        trs = [(ptr[:, :], None), (p6b, None)]
        tr_i = [0]
        t_pd = [[T("pd%d_%d" % (i, h), ps=True) for h in range(2)] for i in range(3)]
        t_p6 = T("p6", ps=True)
        t_ptr = T("ptr", ps=True)
        bank_all = [(pd[0][:, 0:512], t_pd[0][0]), (pd[1][:, 0:512], t_pd[1][0]), (pd[2][:, 0:512], t_pd[2][0]),
                    (p6[:, :], t_p6),
                    (pd[0][:, 512:1024], t_pd[0][1]), (pd[1][:, 512:1024], t_pd[1][1]), (pd[2][:, 512:1024], t_pd[2][1])]
        bank_job = [(p6[:, :], t_p6), (ptrf, t_ptr)]
        trs = [(ptr[:, :], t_ptr), (p6b, t_p6)]

        def tr_bank():
            k = tr_i[0] % 2
            tr_i[0] += 1
            return trs[k]
        bank_O = [(pd[2][:, 0:512], t_pd[2][0]), (pd[2][:, 512:1024], t_pd[2][1])]
        ring_state = {"all": 0, "job": 0, "O": 0, "S": 0}
        bank_mode = ["all"]

        def gen_bank():
            lst = bank_all if bank_mode[0] == "all" else bank_job
            k = ring_state[bank_mode[0]] % len(lst)
            ring_state[bank_mode[0]] += 1
            return lst[k]

        def o_bank():
            k = ring_state["O"] % 2
            ring_state["O"] += 1
            return bank_O[k]

        def s_double():
            k = ring_state["S"] % 2
            ring_state["S"] += 1
            return pd[k], t_pd[k]

        sems = {e: es.enter_context(nc.semaphore("s_" + e)) for e in ENGS}
        block = es.enter_context(nc.Block())

        def ld(eng, out_ap, in_ap, key, w):
            return P.add(eng, lambda e: e.dma_start(out=out_ap, in_=in_ap), w=w, dma=key)

        ld("sp", g_tab[:, :], g_tab_d[:, :], "c0", [t_const])
        t_c1, t_c2, t_c3, t_c4, t_c5, t_c6, t_c7 = [T("c%d" % i) for i in range(1, 8)]
        ld("sp", fg_tab[:, :], fg_tab_d[:, :], "c1", [t_c1])
        ld("sp", colp[:, :], colp_d[:, :], "c2", [t_c2])
        ld("sp", wdw[:, :], wdw_d[:, :], "c3", [t_c3])
        ld("sp", esink[:, :], sinks_d[:, :], "c4", [t_c4])
        ld("pool", maskb[:, :], mask_d[:, :], "c5", [t_c5])
        ld("pool", ident[:, :], ident_d[:, :], "c6", [t_c6])
        ld("pool", rpm[:, :], rperm_d[:, :], "c8", [t_rpm])
        P.add("pool", lambda e: e.memset(ones[:, :], 1.0), w=[t_c7])
        P.add("pool", lambda e: e.memset(dummy[:, 6:7], 0.375), w=[])
        P.add("act", lambda e: e.activation(out=esink[:, :], in_=esink[:, :], func=AF.Exp), r=[t_c4], w=[t_esink])
        P.add("pool", lambda e: e.memset(Vaug[:, :], 1.0), w=[t_vprev] + t_vcur)
        for c in range(8):
            P.add("pool", lambda e, c=c: e.memset(h1[c][:, 0:30], 0.0), w=[t_h1h[c]])
        P.add("pool", lambda e: e.memset(kT[:, :], 0.0), w=[t_kprev] + t_kcur)

        def slot3(si, n):
            return slots[si][:, 0:8 * n].rearrange("p (k n) -> p k n", k=8)

        def bwidth(kind):
            if kind == "diag":
                return (31 - NDVE) * 128
            return WIDTH.get(kind, SLOTW)

        def prep_block(kind, j, si, bi):
            sl = slots[si]
            tp = slot_parts[si]
            key = "wp%d" % si
            if kind == "glu":
                v = slot3(si, 256)
                ld("pool", v[:, :, 0:128], w_in_v[:, :, j * 128:(j + 1) * 128], key, [tp[0]])
                ld("pool", v[:, :, 128:256], w_in_v[:, :, 1024 + j * 128:1024 + (j + 1) * 128], key, [tp[1]])
            elif kind == "diag":
                v = sl[:, 0:31 * 128].rearrange("p (k n) -> p k n", k=31)
                NPE_ = 31 - NDVE
                for k in range(NPE_):
                    P.add("pool", lambda e, k=k, v=v: e.tensor_tensor(
                        out=v[:, k, :], in0=ident[:, :],
                        in1=wdw[:, j * 31 + k:j * 31 + k + 1].to_broadcast([128, 128]), op=ALU.mult),
                          r=[t_c3, t_c6], w=[tp[0]] if k == 0 else [tp[1]] if k == NPE_ - 1 else [])
            elif kind in ("cgate", "agate", "wout", "gconv", "gattn", "wco", "wao"):
                v = slot3(si, 512)
                src = {"cgate": lambda: w_in_v[:, :, 2048 + j * 512:2048 + (j + 1) * 512],
                       "agate": lambda: w_in_v[:, :, 4608 + j * 512:4608 + (j + 1) * 512],
                       "gconv": lambda: w_in_v[:, :, 5632 + j * 512:5632 + (j + 1) * 512],
                       "gattn": lambda: w_in_v[:, :, 6656 + j * 512:6656 + (j + 1) * 512],
                       "wout": lambda: w_out_v[:, :, j * 512:(j + 1) * 512],
                       "wco": lambda: w_co_v[:, :, j * 512:(j + 1) * 512],
                       "wao": lambda: w_ao_v[:, :, j * 512:(j + 1) * 512]}[kind]()
                ld("pool", v[:, :, 0:256], src[:, :, 0:256], key, [tp[0]])
                ld("pool", v[:, :, 256:512], src[:, :, 256:512], key, [tp[1]])
            elif kind == "v":
                v = slot3(si, 256)
                ld("pool", v[:, :, :], w_in_v[:, :, 4352:4608], key, [tp[0]])
            elif kind == "q":
                v = slot3(si, 256)
                for ci in range(2):
                    c0 = 3072 + (2 * j + ci) * 128
                    ld("pool", v[:, :, ci * 128:(ci + 1) * 128], w_in_v[:, :, c0:c0 + 128], key, [tp[ci]])
            elif kind == "k":
                v = slot3(si, 128)
                ld("pool", v[:, :, 0:64], w_in_v[:, :, 4096 + j * 64:4096 + (j + 1) * 64], key, [tp[0]])
                ld("pool", v[:, :, 64:128], w_in_v[:, :, 4096 + j * 64:4096 + (j + 1) * 64], key, [tp[1]])
            else:
                raise ValueError(kind)
            t_s = T("scr%d" % bi)
            W = bwidth(kind)
            P.add("sp", lambda e, sl=sl, bi=bi, W=W: e.dma_start(out=wscr[bi, :, 0:W], in_=sl[:, 0:W]),
                  r=list(tp), w=[t_s], dma="wst%d" % si)
            return t_s

        t_scr = {}
        st = {"next_load": 0, "next_use": 0}
        loaded = {}
        total_stream = NT * NB

        def issue_load():
            n = st["next_load"]
            st["next_load"] += 1
            it, bi = divmod(n, NB)
            si = n % NSLOT
            tp = slot_parts[si]
            if it == 0:
                kind, j = order[bi]
                P.add("pool", lambda e: e.memset(dummy[:, 0:1], 0.0), w=list(tp) + [t_dummy])
                t_scr[bi] = prep_block(kind, j, si, bi)
            else:
                W = bwidth(order[bi][0])
                P.add("sp", lambda e, si=si, bi=bi, W=W: e.dma_start(out=slots[si][:, 0:W], in_=wscr[bi, :, 0:W]),
                      r=[t_scr[bi]], w=list(tp), dma="w%d" % si)
            loaded[n] = si

        def wnext(kind, j):
            if discover:
                disc.append((kind, j))
                return 0, slot_parts[0]
            n = st["next_use"]
            st["next_use"] += 1
            while st["next_load"] < min(total_stream, n + NSLOT - 1) or n not in loaded:
                issue_load()
            it, bi = divmod(n, NB)
            assert order[bi] == (kind, j), (order[bi], kind, j)
            si = loaded[n]
            return si, slot_parts[si]

        def mm(out, lhsT, rhs, start, stop, r, w):
            P.add("pe", lambda e: e.matmul(out, lhsT=lhsT, rhs=rhs, start=start, stop=stop), r=r, w=w)

        hT3 = hT[:, :].rearrange("p (k n) -> p k n", k=KC)

        def proj_fm(si, tp, col0, bank, t_bank, ncols=512):
            v = slot3(si, ncols)
            for kc in range(KC):
                mm(bank, v[:, kc, col0:col0 + 128], hT3[:, kc, :], kc == 0, kc == KC - 1,
                   r=list(tp) + t_hT, w=[t_bank])

        hbs = {}

        def stage_N_load(it):
            P.tag = "N"
            par = it % 2
            for b in range(4):
                xt, t_x = xs[par * 4 + b], t_xs[par * 4 + b]
                row0 = it * TT + b * 128
                P.add("sp", lambda e, xt=xt, row0=row0: e.dma_start(out=xt[:, :], in_=x[row0:row0 + 128, :]),
                      w=[t_x], dma="x%d" % (par * 4 + b))

        def stage_N_pre(it):
            P.tag = "N"
            par = it % 2
            for b in range(4):
                xt, t_x = xs[par * 4 + b], t_xs[par * 4 + b]
                hbt, t_hb = hb.get()
                hbs[(it, b)] = (hbt, t_hb)
                P.add("act", lambda e, xt=xt, b=b, hbt=hbt: e.activation(out=hbt[:, :], in_=xt[:, :], func=AF.Square,
                                                                          accum_out=ssq[:, b:b + 1]),
                      r=[t_x], w=[t_hb, t_ssq])
            P.add("act", lambda e: e.activation(out=ssq[:, 4:8], in_=ssq[:, 0:4], func=AF.Ln, scale=1.0 / D, bias=EPS),
                  r=[t_ssq], w=[t_ssq])
            P.add("act", lambda e: e.activation(out=ssq[:, 4:8], in_=ssq[:, 4:8], func=AF.Exp, scale=-0.5),
                  r=[t_ssq], w=[t_ssq])
            for b in range(4):
                xt, t_x = xs[par * 4 + b], t_xs[par * 4 + b]
                hbt, t_hb = hbs[(it, b)]
                P.add("dve", lambda e, xt=xt, hbt=hbt, b=b: e.scalar_tensor_tensor(
                    out=hbt[:, :], in0=xt[:, :], scalar=ssq[:, 4 + b:5 + b], in1=g_tab[:, :], op0=ALU.mult, op1=ALU.mult),
                    r=[t_x, t_ssq, t_const], w=[t_hb])

        def stage_N_pe(it):
            P.tag = "N"
            for b in range(4):
                hbt, t_hb = hbs.pop((it, b))
                trb, t_trb = tr_bank()
                for kc in range(KC):
                    P.add("pe", lambda e, hbt=hbt, kc=kc, trb=trb: e.transpose(out=trb[:, kc * 128:(kc + 1) * 128],
                                                                                in_=hbt[:, kc * 128:(kc + 1) * 128],
                                                                                identity=ident[:, :]),
                          r=[t_hb, t_c6], w=[t_trb])
                P.add("act", lambda e, b=b, trb=trb: e.activation(out=hT3[:, :, b * 128:(b + 1) * 128],
                                                                   in_=trb.rearrange("p (k n) -> p k n", k=KC), func=AF.Copy),
                      r=[t_trb], w=[t_hT[b]])

        kT3 = kT[:, :].rearrange("p (h n) -> p h n", h=4)
        rope_tabs = {}
        kstate = {}
        bank_hook = [None]

        def rope_p1(si, tps, col0, bank_fn, cs, t_cs, ncols=512):
            bQ, t_bQ = bank_fn()
            proj_fm(si, tps, col0, bQ, t_bQ, ncols)
            qh, t_qh = qhl.get()
            ql, t_ql = qhl.get()
            a, t_a = tmpf.get()
            P.add("act", lambda e: e.activation(out=qh[:, :], in_=bQ, func=AF.Copy), r=[t_bQ], w=[t_qh])
            P.add("dve", lambda e: e.tensor_tensor(out=ql[:, :], in0=bQ, in1=qh[:, :], op=ALU.subtract),
                  r=[t_bQ, t_qh], w=[t_ql])
            P.add("dve", lambda e: e.tensor_tensor(out=a[:, :], in0=bQ, in1=cs[:, :], op=ALU.mult),
                  r=[t_bQ, t_cs], w=[t_a])
            return (qh, t_qh, ql, t_ql, a, t_a)

        def rope_p2(state, bank_fn, sn, t_sn, out_ap, t_out, defer_add=False):
            qh, t_qh, ql, t_ql, a, t_a = state
            bR, t_bR = bank_fn()
            mm(bR, rpm[:, :], qh[:, :], True, False, r=[t_rpm, t_qh], w=[t_bR])
            mm(bR, rpm[:, :], ql[:, :], False, True, r=[t_rpm, t_ql], w=[t_bR])
            b, t_b = tmpf.get()
            P.add("dve", lambda e: e.tensor_tensor(out=b[:, :], in0=bR, in1=sn[:, :], op=ALU.mult),
                  r=[t_bR, t_sn], w=[t_b])
            def final_add():
                P.add("dve", lambda e: e.tensor_tensor(out=out_ap, in0=a[:, :], in1=b[:, :], op=ALU.add),
                      r=[t_a, t_b], w=[t_out])
            if defer_add:
                return final_add
            final_add()

        def kprep(it):
            P.tag = "B.k"
            cs, t_cs = cosb.get()
            sn, t_sn = sinb.get()
            P.add("sp", lambda e: e.dma_start(out=cs[:, :], in_=cos_d[:, it * TT:(it + 1) * TT]), w=[t_cs], dma="cs0")
            P.add("sp", lambda e: e.dma_start(out=sn[:, :], in_=sin_d[:, it * TT:(it + 1) * TT]), w=[t_sn], dma="sn0")
            rope_tabs[it] = (cs, t_cs, sn, t_sn)

        kpend = {}

        def kjob_p1(it, hk):
            P.tag = "B.k"
            cs, t_cs, sn, t_sn = rope_tabs[it]
            si, tp = wnext("k", hk)
            kpend[(it, hk)] = rope_p1(si, (tp[0], tp[1]), 0, bank_hook[0], cs, t_cs, ncols=128)

        def kjob_p2(it, hk):
            P.tag = "B.k"
            cs, t_cs, sn, t_sn = rope_tabs[it]
            rope_p2(kpend.pop((it, hk)), bank_hook[0], sn, t_sn, kT3[:, hk, 128:640], t_kcur[hk])

        def stage_A(it):
            bank_mode[0] = "all"
            S1, t_S1 = bank_all[5]
            S2, t_S2 = bank_all[6]
            ring5 = [bank_all[i] for i in (0, 1, 2, 3, 4)]
            r5 = [0]

            def bank5():
                k = r5[0] % len(ring5)
                r5[0] += 1
                return ring5[k]

            bank_hook[0] = bank5

            P.tag = "A.cgate"
            for j in range(2):
                si, tp = wnext("cgate", j)
                for cc in range(4):
                    c = j * 4 + cc
                    bG, t_bG = bank5()
                    proj_fm(si, tp[0:2], cc * 128, bG, t_bG)
                    P.add("act", lambda e, bG=bG, c=c: e.activation(out=mT[:, c * TT:(c + 1) * TT], in_=bG, func=AF.Silu),
                          r=[t_bG], w=[t_mT[c]])

            def glu(c):
                P.tag = "A.glu"
                si, tp = wnext("glu", c)
                v = slot3(si, 256)
                bA, t_A = bank5()
                bB, t_B = bank5()
                for kc in range(KC):
                    mm(bA, v[:, kc, 0:128], hT3[:, kc, :], kc == 0, kc == KC - 1, r=list(tp[0:2]) + t_hT, w=[t_A])
                for kc in range(KC):
                    mm(bB, v[:, kc, 128:256], hT3[:, kc, :], kc == 0, kc == KC - 1, r=list(tp[0:2]) + t_hT, w=[t_B])
                sg, t_sg = tmpf.get()
                P.add("act", lambda e: e.activation(out=sg[:, :], in_=bB, func=AF.Sigmoid), r=[t_B], w=[t_sg])
                P.add("dve", lambda e: e.tensor_tensor(out=h1[c][:, 30:30 + TT], in0=bA, in1=sg[:, :], op=ALU.mult),
                      r=[t_A, t_sg], w=[t_h1c[c]])

            pend = []

            def stats(c, v2, t_v2):
                P.tag = "A.conv"
                mm(S1, ones[:, :], Vsb[:, c * TT:(c + 1) * TT], c == 0, c == 7, r=[t_c7, t_V[c]], w=[t_S1])
                mm(S2, ones[:, :], v2[:, :], c == 0, c == 7, r=[t_c7, t_v2], w=[t_S2])

            def conv(c):
                P.tag = "A.conv"
                if c == 7:
                    warm(AF.Ln)
                si, tp = wnext("diag", c)
                dv = slots[si][:, 0:31 * 128].rearrange("p (k n) -> p k n", k=31)
                bV, t_bV = bank5()
                NPE = 31 - NDVE
                for k in range(NPE):
                    mm(bV, dv[:, k, :], h1[c][:, k:k + TT], k == 0, k == NPE - 1, r=list(tp[0:2]) + [t_h1c[c], t_h1h[c]],
                       w=[t_bV])
                acc, t_acc = accf.get()
                for k in range(NPE, 31):
                    wk = wdw[:, c * 31 + k:c * 31 + k + 1]
                    if k == NPE:
                        P.add("dve", lambda e, k=k, wk=wk: e.tensor_scalar(out=acc[:, :], in0=h1[c][:, k:k + TT], scalar1=wk,
                                                                           scalar2=None, op0=ALU.mult),
                              r=[t_h1c[c], t_h1h[c], t_c3], w=[t_acc])
                    else:
                        P.add("dve", lambda e, k=k, wk=wk: e.scalar_tensor_tensor(out=acc[:, :], in0=h1[c][:, k:k + TT], scalar=wk,
                                                                                  in1=acc[:, :], op0=ALU.mult, op1=ALU.add),
                              r=[t_h1c[c], t_h1h[c], t_c3, t_acc], w=[t_acc])
                P.add("dve", lambda e: e.scalar_tensor_tensor(out=Vsb[:, c * TT:(c + 1) * TT], in0=bV, scalar=colp[:, c:c + 1],
                                                              in1=acc[:, :], op0=ALU.add, op1=ALU.add),
                      r=[t_bV, t_c2, t_acc], w=[t_V[c]])
                v2, t_v2 = tmpb.get()
                P.add("act", lambda e: e.activation(out=v2[:, :], in_=Vsb[:, c * TT:(c + 1) * TT], func=AF.Square),
                      r=[t_V[c]], w=[t_v2])
                P.add("pool", lambda e: e.tensor_copy(out=h1[c][:, 0:30], in_=h1[c][:, TT:TT + 30]),
                      r=[t_h1c[c]], w=[t_h1h[c]])
                pend.append((c, v2, t_v2))
                if len(pend) > 1:
                    stats(*pend.pop(0))

            kprep(it)
            glu(0)
            for c in range(8):
                if c + 1 < 8:
                    glu(c + 1)
                conv(c)
                if c % 2 == 0:
                    kjob_p1(it, c // 2)
                else:
                    kjob_p2(it, c // 2)
            stats(*pend.pop(0))

            gjobs = []
            gstate = {}

            def gjob(c):
                P.tag = "A.gconv"
                if c == 2:
                    ring5.extend([bank_all[5], bank_all[6]])
                    r5[0] = 0
                j, cc = divmod(c, 4)
                if j not in gstate:
                    gstate[j] = wnext("gconv", j)
                si, tp = gstate[j]
                bG, t_bG = bank5()
                proj_fm(si, tp[0:2], cc * 128, bG, t_bG)
                P.add("act", lambda e: e.activation(out=agT[:, c * TT:(c + 1) * TT], in_=bG, func=AF.Tanh, scale=0.5),
                      r=[t_bG], w=t_ag[c])

            P.tag = "A.ln"
            mean, t_mean = tmpf.get()
            msq, t_msq = tmpf.get()
            var, t_var = tmpf.get()
            P.add("dve", lambda e: e.tensor_scalar(out=mean[:, :], in0=S1, scalar1=1.0 / D, scalar2=None, op0=ALU.mult),
                  r=[t_S1], w=[t_mean])
            P.add("dve", lambda e: e.tensor_tensor(out=msq[:, :], in0=mean[:, :], in1=mean[:, :], op=ALU.mult),
                  r=[t_mean], w=[t_msq])
            P.add("dve", lambda e: e.scalar_tensor_tensor(out=var[:, :], in0=S2, scalar=1.0 / D, in1=msq[:, :],
                                                           op0=ALU.mult, op1=ALU.subtract),
                  r=[t_S2, t_msq], w=[t_var])
            P.add("act", lambda e: e.activation(out=var[:, :], in_=var[:, :], func=AF.Ln, bias=EPS), r=[t_var], w=[t_var])
            P.add("act", lambda e: e.activation(out=rstdT[:, :], in_=var[:, :], func=AF.Exp, scale=-0.5),
                  r=[t_var], w=[t_rstdT])
            P.add("dve", lambda e: e.scalar_tensor_tensor(out=nmr[:, :], in0=mean[:, :], scalar=-1.0, in1=rstdT[:, :],
                                                           op0=ALU.mult, op1=ALU.mult),
                  r=[t_mean, t_rstdT], w=[t_nmr])

            prev = None
            for c in range(8):
                gjob(c)
                P.tag = "A.norm"
                z, t_z = tmpf.get()
                P.add("dve", lambda e, z=z, c=c: e.tensor_tensor(out=z[:, :], in0=Vsb[:, c * TT:(c + 1) * TT], in1=rstdT[:, :],
                                                                  op=ALU.mult),
                      r=[t_V[c], t_rstdT], w=[t_z])
                P.add("dve", lambda e, z=z: e.tensor_tensor(out=z[:, :], in0=z[:, :], in1=nmr[:, :], op=ALU.add),
                      r=[t_z, t_nmr], w=[t_z])
                ca, t_ca = tmpb.get()
                P.add("act", lambda e, z=z, ca=ca, c=c: e.activation(out=ca[:, :], in_=z[:, :], func=AF.Silu,
                                                                      scale=colp[:, 8 + c:9 + c], bias=colp[:, 16 + c:17 + c]),
                      r=[t_z, t_c2], w=[t_ca])

                def cgmul(ca=ca, t_ca=t_ca, c=c):
                    P.add("dve", lambda e: e.tensor_tensor(out=cgT[:, c * TT:(c + 1) * TT], in0=ca[:, :],
                                                           in1=mT[:, c * TT:(c + 1) * TT], op=ALU.mult),
                          r=[t_ca, t_mT[c]], w=[t_cgT[c]])
                if prev is not None:
                    prev()
                prev = cgmul
            prev()
            stage_B0(it)

            P.tag = "A.wco"
            cg3 = cgT[:, :].rearrange("p (k n) -> p k n", k=8)
            for j in range(2):
                si, tp = wnext("wco", j)
                v = slot3(si, 512)
                for dd in range(4):
                    dch = j * 4 + dd
                    bY, t_bY = bank5()
                    for kc in range(KC):
                        mm(bY, v[:, kc, dd * 128:(dd + 1) * 128], cg3[:, kc, :], kc == 0, kc == KC - 1,
                           r=list(tp[0:2]) + t_cgT, w=[t_bY])
                    P.add("dve", lambda e, bY=bY, dch=dch: e.scalar_tensor_tensor(
                        out=m1[:, dch * TT:(dch + 1) * TT], in0=agT[:, dch * TT:(dch + 1) * TT], scalar=1.0, in1=bY,
                        op0=ALU.add, op1=ALU.mult),
                          r=[t_bY] + t_ag[dch], w=[t_m1[dch]])

        Va4 = Vaug[:, :].rearrange("p (b h d) -> p b h d", b=5, h=4)
        mask3 = maskb[:, :].rearrange("p (k n) -> p k n", k=2)
        q3 = qT[:, :].rearrange("p (c n) -> p c n", c=8)
        ag3 = agT[:, :].rearrange("p (c n) -> p c n", c=8)
        sga3 = sgaT[:, :].rearrange("p (c n) -> p c n", c=8)

        qpend = {}

        def q_p1(it, si, tp, ci, c):
            P.tag = "B.q"
            cs, t_cs, sn, t_sn = rope_tabs[it]
            qpend[(it, c)] = rope_p1(si, (tp[0], tp[1]), ci * 128, gen_bank, cs, t_cs, ncols=256)

        def q_p2(it, c, defer_add=False):
            P.tag = "B.q"
            cs, t_cs, sn, t_sn = rope_tabs[it]
            return rope_p2(qpend.pop((it, c)), gen_bank, sn, t_sn, qT[:, c * TT:(c + 1) * TT], t_qT[c], defer_add)

        def stage_B0(it):
            bank_mode[0] = "all"
            warm(AF.Exp)
            P.tag = "B.v"
            si, tp = wnext("v", 0)
            vv = slot3(si, 256)
            for b in range(4):
                bV, t_bV = gen_bank()
                for kc in range(KC):
                    mm(bV[:, 0:256], hT3[:, kc, b * 128:(b + 1) * 128], vv[:, kc, :], kc == 0, kc == KC - 1,
                       r=[tp[0], t_hT[b]], w=[t_bV])
                P.add("act", lambda e, bV=bV, b=b: e.activation(out=Va4[:, b + 1, :, 0:64],
                                                                 in_=bV[:, 0:256].rearrange("p (h d) -> p h d", h=4),
                                                                 func=AF.Copy),
                      r=[t_bV], w=[t_vcur[b]])
            si, tp = wnext("q", 0)
            q_p1(it, si, tp, 0, 0)
            q_p1(it, si, tp, 1, 1)
            q0_adds[it] = [q_p2(it, 0, defer_add=True), q_p2(it, 1, defer_add=True)]

        q0_adds = {}

        def stage_B(it):
            bank_mode[0] = "all"
            P.tag = "B.q"
            for f in q0_adds.pop(it):
                f()

            jobs = []

            def add_q_jobs(j):
                holder = {}

                def get():
                    if "s" not in holder:
                        holder["s"] = wnext("q", j)
                    return holder["s"]
                for ci in range(2):
                    jobs.append(lambda ci=ci, j=j: q_p1(it, *get(), ci, 2 * j + ci))
                    jobs.append(lambda ci=ci, j=j: q_p2(it, 2 * j + ci))

            def add_gate_jobs(kind, j):
                holder = {}

                def get():
                    if "s" not in holder:
                        holder["s"] = wnext(kind, j)
                    return holder["s"]

                def job(cc):
                    si, tp = get()
                    c = j * 4 + cc
                    P.tag = "B." + kind
                    bG, t_bG = gen_bank()
                    proj_fm(si, tp[0:2], cc * 128, bG, t_bG)
                    if kind == "agate":
                        th, t_th = tmpb.get()
                        P.add("act", lambda e: e.activation(out=th[:, :], in_=bG, func=AF.Tanh, scale=0.5),
                              r=[t_bG], w=[t_th])
                        P.add("dve", lambda e: e.scalar_tensor_tensor(out=sgaT[:, c * TT:(c + 1) * TT], in0=th[:, :], scalar=1.0,
                                                                      in1=bG, op0=ALU.add, op1=ALU.mult),
                              r=[t_bG, t_th], w=[t_sgaT[c]])
                    else:
                        P.add("act", lambda e: e.activation(out=mT[:, c * TT:(c + 1) * TT], in_=bG, func=AF.Tanh, scale=0.5),
                              r=[t_bG], w=[t_mT[c]])
                for cc in range(4):
                    jobs.append(lambda cc=cc: job(cc))

            add_q_jobs(1)
            add_gate_jobs("agate", 0)
            add_q_jobs(2)
            add_gate_jobs("agate", 1)
            add_q_jobs(3)
            add_gate_jobs("gattn", 0)
            add_gate_jobs("gattn", 1)
            sched = [[0], [1, 2], [3, 4], [5, 6], [7, 8], [9, 10], [11, 12], [13, 14], [15, 16], [17, 18], [19, 20], [21, 22],
                     [23, 24], [25, 26], [27], []]

            iters = [(hk, n) for hk in range(4) for n in range(4)]
            state = {}

            def S_part(i):
                hk, n = iters[i]
                P.tag = "B.attn"
                first = (it == 0 and n == 0)
                Sd, tS = s_double()
                for r in range(2):
                    for kb in range(2):
                        if first and kb == 0:
                            continue
                        kcol0 = n * 128 + kb * 128
                        t_k = [t_kcur[hk]] + ([t_kprev] if (n == 0 and kb == 0) else [])
                        base = r * 512 + kb * 256
                        mm(Sd[:, base:base + 256], kT3[r * 64:(r + 1) * 64, hk, kcol0:kcol0 + 128],
                           q3[r * 64:(r + 1) * 64, 2 * hk:2 * hk + 2, n * 128:(n + 1) * 128], True, True,
                           r=t_k + [t_qT[2 * hk], t_qT[2 * hk + 1]], w=[tS[r]])
                pb, t_p = Pb.get()
                k0 = 1 if first else 0
                if first:
                    S3 = Sd[:, :].rearrange("p (r k x) -> p r k x", r=2, k=2)
                    P3 = pb[:, :].rearrange("p (r k x) -> p r k x", r=2, k=2)
                    P.add("act", lambda e: e.activation(out=P3[:, :, 1, :], in_=S3[:, :, 1, :], func=AF.Exp, scale=0.125),
                          r=list(tS), w=[t_p])
                else:
                    P.add("act", lambda e: e.activation(out=pb[:, :], in_=Sd[:, :], func=AF.Exp, scale=0.125),
                          r=list(tS), w=[t_p])
                for r in range(2):
                    Pr = pb[:, r * 512:(r + 1) * 512].rearrange("p (k a q) -> p k a q", k=2, a=2)
                    P.add("dve", lambda e, Pr=Pr: e.tensor_tensor(
                        out=Pr[:, k0:2, :, :], in0=Pr[:, k0:2, :, :],
                        in1=mask3[:, k0:2, :].unsqueeze(2).to_broadcast([128, 2 - k0, 2, 128]), op=ALU.mult),
                        r=[t_p, t_c5], w=[t_p])
                state[i] = (pb, t_p, first)

            def PV_part(i):
                hk, n = iters[i]
                P.tag = "B.attn"
                pb, t_p, first = state.pop(i)
                P5 = pb[:, :].rearrange("p (r k a q) -> p r k a q", r=2, a=2, k=2)
                Ob, t_Ob = o_bank()
                O3 = Ob[:, 0:260].rearrange("p (g d) -> p g d", g=4)
                for g in range(4):
                    r, pair = g % 2, g // 2
                    for kb in range(2):
                        if first and kb == 0:
                            continue
                        vb = n + kb
                        t_v = t_vprev if vb == 0 else t_vcur[vb - 1]
                        mm(O3[:, g, :], P5[:, r, kb, pair, :], Va4[:, vb, hk, :],
                           (kb == 0) or first, kb == 1, r=[t_p, t_v], w=[t_Ob])
                sm, t_sm = small.get()
                at3 = a_tm[n][:, :].rearrange("p (g d) -> p g d", g=16)
                P.add("dve", lambda e: e.tensor_tensor(out=sm[:, 0:4], in0=O3[:, :, 64], in1=esink[:, hk * 4:hk * 4 + 4],
                                                       op=ALU.add),
                      r=[t_Ob, t_esink], w=[t_sm])
                P.add("dve", lambda e: e.reciprocal(out=sm[:, 4:8], in_=sm[:, 0:4]), r=[t_sm], w=[t_sm])
                P.add("dve", lambda e: e.tensor_tensor(
                    out=at3[:, hk * 4:hk * 4 + 4, :], in0=O3[:, :, 0:64],
                    in1=sm[:, 4:8].unsqueeze(2).to_broadcast([128, 4, 64]), op=ALU.mult),
                    r=[t_Ob, t_sm], w=[t_atm[n][hk]])

            bank_mode[0] = "job"
            S_part(0)
            for i in range(16):
                if i + 1 < 16:
                    S_part(i + 1)
                for jn in sched[i]:
                    jobs[jn]()
                PV_part(i)
            bank_mode[0] = "all"

        def stage_B2(it):
            bank_mode[0] = "all"
            P.add("pool", lambda e: e.tensor_copy(out=kT3[:, :, 0:128], in_=kT3[:, :, 512:640]), r=t_kcur, w=[t_kprev])
            P.add("pool", lambda e: e.tensor_copy(out=Va4[:, 0, :, 0:64], in_=Va4[:, 4, :, 0:64]), r=[t_vcur[3]], w=[t_vprev])
            P.tag = "B.attnT"
            for n in range(4):
                trb, t_trb = tr_bank()
                for c in range(8):
                    P.add("pe", lambda e, n=n, c=c, trb=trb: e.transpose(out=trb[:, c * 128:(c + 1) * 128],
                                                                          in_=a_tm[n][:, c * 128:(c + 1) * 128], identity=ident[:, :]),
                          r=t_atm[n] + [t_c6], w=[t_trb])
                P.add("dve", lambda e, n=n, trb=trb: e.scalar_tensor_tensor(out=ag3[:, :, n * 128:(n + 1) * 128],
                                                                            in0=trb.rearrange("p (c n) -> p c n", c=8), scalar=0.5,
                                                                            in1=sga3[:, :, n * 128:(n + 1) * 128], op0=ALU.mult, op1=ALU.mult),
                      r=[t_trb] + t_sgaT, w=[t_ag[c][n] for c in range(8)])
        def stage_B3(it):
            bank_mode[0] = "all"
            P.tag = "B.wao"
            for j in range(2):
                si, tp = wnext("wao", j)
                v = slot3(si, 512)
                for dd in range(4):
                    dch = j * 4 + dd
                    bY, t_bY = gen_bank()
                    for kc in range(KC):
                        mm(bY, v[:, kc, dd * 128:(dd + 1) * 128], ag3[:, kc, :], kc == 0, kc == KC - 1,
                           r=list(tp[0:2]) + t_ag[kc], w=[t_bY])
                    ya, t_ya = tmpf.get()
                    P.add("dve", lambda e, bY=bY, ya=ya, dch=dch: e.scalar_tensor_tensor(
                        out=ya[:, :], in0=mT[:, dch * TT:(dch + 1) * TT], scalar=1.0, in1=bY, op0=ALU.add, op1=ALU.mult),
                          r=[t_bY, t_mT[dch]], w=[t_ya])
                    P.add("dve", lambda e, ya=ya, dch=dch: e.tensor_tensor(out=mT[:, dch * TT:(dch + 1) * TT], in0=ya[:, :],
                                                                           in1=m1[:, dch * TT:(dch + 1) * TT], op=ALU.add),
                          r=[t_ya, t_m1[dch]], w=[t_mT[dch]])

        last_store = []

        def stage_O(it):
            P.tag = "O"
            bank_mode[0] = "all"
            par = it % 2
            si0, tp0 = wnext("wout", 0)
            si1, tp1 = wnext("wout", 1)
            wv = [slot3(si0, 512), slot3(si1, 512)]
            tps = [tp0, tp1]
            m3 = mT[:, :].rearrange("p (k n) -> p k n", k=8)
            for b in range(4):
                xt, t_x = xs[par * 4 + b], t_xs[par * 4 + b]
                xot, t_xo = xo.get()
                for half in range(2):
                    bO, t_bO = gen_bank()
                    for kc in range(KC):
                        mm(bO, m3[:, kc, b * 128:(b + 1) * 128], wv[half][:, kc, :], kc == 0, kc == KC - 1,
                           r=list(tps[half][0:2]) + t_mT, w=[t_bO])
                    P.add("dve", lambda e, bO=bO, xot=xot, xt=xt, half=half: e.scalar_tensor_tensor(
                        out=xot[:, half * 512:(half + 1) * 512], in0=bO, scalar=0.5, in1=xt[:, half * 512:(half + 1) * 512],
                        op0=ALU.mult, op1=ALU.add),
                        r=[t_bO, t_x], w=[t_xo])
                P.add("act", lambda e, xot=xot, b=b: e.activation(out=a_tm[b][:, :], in_=xot[:, :], func=AF.Square,
                                                                  accum_out=ssq[:, 8 + b:9 + b]),
                      r=[t_xo], w=t_atm[b] + [t_ssq2])
                P.add("act", lambda e, b=b: e.activation(out=ssq[:, 12 + b:13 + b], in_=ssq[:, 8 + b:9 + b], func=AF.Ln,
                                                         scale=1.0 / D, bias=EPS), r=[t_ssq2], w=[t_ssq2])
                P.add("act", lambda e, b=b: e.activation(out=ssq[:, 12 + b:13 + b], in_=ssq[:, 12 + b:13 + b], func=AF.Exp,
                                                         scale=-0.5), r=[t_ssq2], w=[t_ssq2])
                P.add("dve", lambda e, xot=xot, b=b: e.scalar_tensor_tensor(out=xot[:, :], in0=xot[:, :], scalar=ssq[:, 12 + b:13 + b],
                                                                            in1=fg_tab[:, :], op0=ALU.mult, op1=ALU.mult),
                      r=[t_xo, t_ssq2, t_c1], w=[t_xo])
                row0 = it * TT + b * 128
                key = "y%d" % ((xo.i - 1) % 2)
                op = P.add("sp", lambda e, xot=xot, row0=row0: e.dma_start(out=y[row0:row0 + 128, :], in_=xot[:, :]),
                           r=[t_xo], w=[], dma=key)
                last_store.append(op)

        if discover:
            stage_A(0)
            stage_B(0)
            stage_B2(0)
            stage_B3(0)
            stage_O(0)
            return disc
        stage_N_load(0)
        stage_N_pre(0)
        stage_N_pe(0)
        for it in range(NT):
            stage_A(it)
            if it + 1 < NT:
                stage_N_load(it + 1)
            stage_B(it)
            stage_B2(it)
            if it + 1 < NT:
                stage_N_pre(it + 1)
            stage_B3(it)
            if it + 1 < NT:
                stage_N_pe(it + 1)
            stage_O(it)
        fin = []
        for op in last_store[-2:]:
            t = T("fin")
            t.w = op
            fin.append(t)
        P.add("sp", None, r=fin)

        import os
        if os.environ.get("KTAGS"):
            import json
            json.dump({e: [o.tag for o in P.ops[e] if o.fn is not None] for e in ENGS}, open(os.environ["KTAGS"], "w"))
        dma_sems = {k: es.enter_context(nc.semaphore("d_" + k)) for k in P.dma_keys}
        P.emit_all(block, sems, dma_sems)
    return nc


def build_program():
    order = build_nc(None)
    return build_nc(order)


def _consts():
    d = np.arange(128) % 64
    f = d % 32
    inv_freq = (10000.0 ** (-(np.arange(0, 64, 2, dtype=np.float32)) / 64.0)).astype(np.float32)
    pos = np.arange(S, dtype=np.float32)
    ang = pos[None, :] * inv_freq[f][:, None]
    cosT = np.cos(ang).astype(np.float32)
    sgn = np.where(d < 32, -1.0, 1.0).astype(np.float32)
    sinT = (np.sin(ang) * sgn[:, None]).astype(np.float32)
    j = np.arange(128)[:, None]
    i = np.arange(128)[None, :]
    mask = np.concatenate([(j > i), (i >= j)], axis=1).astype(np.float32)
    ident = np.eye(128, dtype=np.float32)
    return cosT, sinT, mask, ident


_CACHE = {}


def kernel(x, norm_g, w_in, conv_dw_w, conv_dw_b, conv_ln_g, conv_ln_b, w_conv_out, attn_sinks, w_attn_out,
           w_out, final_norm_g):
    f = lambda a: np.ascontiguousarray(np.asarray(a, dtype=np.float32))
    x = f(x)
    cosT, sinT, mask, ident = _consts()
    col = lambda v: f(np.asarray(v, np.float32).reshape(8, 128).T)
    colp = np.concatenate([col(conv_dw_b[0]), col(conv_ln_g[0]), col(conv_ln_b[0])], axis=1)
    wdw = f(np.asarray(conv_dw_w[0], np.float32).reshape(31, 8, 128).transpose(2, 1, 0).reshape(128, 8 * 31))
    shared = {
        "w_in": f(w_in[0]), "w_co": f(w_conv_out[0]), "w_ao": f(w_attn_out[0]), "w_out": f(w_out[0]),
        "g_tab": f(np.broadcast_to(np.asarray(norm_g[0], np.float32)[None, :], (128, D))),
        "fg_tab": f(np.broadcast_to(np.asarray(final_norm_g, np.float32)[None, :], (128, D))),
        "colp": f(colp), "wdw": wdw,
        "sinks": f(np.broadcast_to(np.asarray(attn_sinks[0], np.float32)[None, :], (128, 16))),
        "cosT": cosT, "sinT": sinT, "maskT": mask, "ident": ident,
        "rperm": np.ascontiguousarray(ident[:, np.arange(128) ^ 32]),
    }
    if "nc" not in _CACHE:
        _CACHE["nc"] = build_program()
    nc = _CACHE["nc"]
    in_maps = []
    for b in range(NCORES):
        m = dict(shared)
        m["x"] = np.ascontiguousarray(x[b])
        in_maps.append(m)
    res = run_bass_kernel_spmd(nc, in_maps, core_ids=list(range(NCORES)))
    out = np.stack([np.asarray(res.results[b]["y"], dtype=np.float32).reshape(S, D) for b in range(NCORES)], axis=0)
    return out
```

```python
import numpy as np
from contextlib import ExitStack
import concourse.bass as bass
import concourse.mybir as mybir
from concourse.bass_utils import run_bass_kernel_spmd

F32 = mybir.dt.float32
BF16 = mybir.dt.bfloat16
ALU = mybir.AluOpType
AF = mybir.ActivationFunctionType

ENGS = ("pe", "act", "dve", "pool", "sp")

D = 1024
S = 4096
NCORES = 8
TT = 512
NT = S // TT
KC = 8
INW = 7680
NSLOT = 5
WIDTH = {"glu": 2048, "diag": 0, "k": 1024, "v": 2048, "q": 2048}
NDVE = 7
SLOTW = 4096
EPS = 1e-5


class T:
    __slots__ = ("name", "w", "rs", "ps")

    def __init__(self, name, ps=False):
        self.name = name
        self.w = None
        self.rs = []
        self.ps = ps


class Op:
    __slots__ = ("eng", "fn", "deps", "sig", "val", "sem", "isdma", "pos", "tag")


class Prog:
    def __init__(self):
        self.ops = {e: [] for e in ENGS}
        self.n = 0
        self.dma_keys = []
        self.tag = ""

    def add(self, eng, fn, r=(), w=(), dma=None):
        op = Op()
        op.tag = self.tag
        op.eng = eng
        op.fn = fn
        op.sig = False
        op.val = 0
        op.sem = dma
        op.isdma = dma is not None
        if dma is not None and dma not in self.dma_keys:
            self.dma_keys.append(dma)
        op.pos = self.n
        self.n += 1
        deps = []
        for t in r:
            if t.w is not None:
                deps.append(t.w)
            if t.ps:
                deps.extend(o for o in t.rs if o.eng != eng)
        for t in w:
            if t.w is not None:
                deps.append(t.w)
            deps.extend(t.rs)
        keep = []
        seen = set()
        latest = {}
        for d in deps:
            if id(d) in seen or d is op:
                continue
            seen.add(id(d))
            if d.isdma:
                keep.append(d)
                continue
            if d.eng == eng and eng == "pe":
                continue
            if d.eng not in latest or latest[d.eng].pos < d.pos:
                latest[d.eng] = d
        keep.extend(latest.values())
        for d in keep:
            d.sig = True
        op.deps = keep
        for t in r:
            t.rs.append(op)
        for t in w:
            t.w = op
            t.rs = []
        self.ops[eng].append(op)
        return op

    def emit_all(self, block, sems, dma_sems):
        for e in ENGS:
            cnt = 0
            for op in self.ops[e]:
                if op.isdma:
                    continue
                if op.sig:
                    cnt += 1
                    op.val = cnt
        dcnt = {}
        for op in sorted([o for e in ENGS for o in self.ops[e] if o.isdma], key=lambda o: o.pos):
            dcnt[op.sem] = dcnt.get(op.sem, 0) + 16
            op.val = dcnt[op.sem]

        def run(e, eng):
            known = {}
            for op in self.ops[e]:
                need = {}
                for d in op.deps:
                    if d.isdma:
                        key = ("d", d.sem)
                        s = dma_sems[d.sem]
                    else:
                        key = ("c", d.eng)
                        s = sems[d.eng]
                    if need.get(key, (None, 0))[1] < d.val:
                        need[key] = (s, d.val)
                for key, (s, v) in need.items():
                    if known.get(key, 0) >= v:
                        continue
                    eng.wait_ge(s, v)
                    known[key] = v
                if op.fn is None:
                    continue
                inst = op.fn(eng)
                if op.isdma:
                    inst.then_inc(dma_sems[op.sem], 16)
                elif op.sig:
                    inst.then_inc(sems[e], 1)

        @block.tensor
        def _(eng):
            run("pe", eng)

        @block.scalar
        def _(eng):
            run("act", eng)

        @block.vector
        def _(eng):
            run("dve", eng)

        @block.gpsimd
        def _(eng):
            run("pool", eng)

        @block.sync
        def _(eng):
            run("sp", eng)
            for key, v in dcnt.items():
                eng.wait_ge(dma_sems[key], v)


class Ring:
    def __init__(self, aps, name):
        self.aps = aps
        self.ts = [T("%s%d" % (name, i)) for i in range(len(aps))]
        self.i = 0

    def get(self):
        k = self.i % len(self.aps)
        self.i += 1
        return self.aps[k], self.ts[k]


def build_nc(order=None):
    discover = order is None
    nc = bass.Bass("TRN2", target_bir_lowering=False)
    dt_in = lambda name, shape: nc.dram_tensor(name, shape, F32, kind="ExternalInput").ap()
    x = dt_in("x", [S, D])
    w_in = dt_in("w_in", [D, INW])
    w_co = dt_in("w_co", [D, D])
    w_ao = dt_in("w_ao", [D, D])
    w_out = dt_in("w_out", [D, D])
    g_tab_d = dt_in("g_tab", [128, D])
    fg_tab_d = dt_in("fg_tab", [128, D])
    colp_d = dt_in("colp", [128, 24])
    wdw_d = dt_in("wdw", [128, 8 * 31])
    sinks_d = dt_in("sinks", [128, 16])
    cos_d = dt_in("cosT", [128, S])
    sin_d = dt_in("sinT", [128, S])
    mask_d = dt_in("maskT", [128, 256])
    ident_d = dt_in("ident", [128, 128])
    rperm_d = dt_in("rperm", [128, 128])
    y = nc.dram_tensor("y", [S, D], F32, kind="ExternalOutput").ap()
    NB = 37 if discover else len(order)
    wscr = nc.dram_tensor("wscr", [NB, 128, SLOTW], BF16, kind="Internal").ap()
    disc = []

    w_in_v = w_in.rearrange("(k p) e -> p k e", p=128)
    w_co_v = w_co.rearrange("(k p) e -> p k e", p=128)
    w_ao_v = w_ao.rearrange("(k p) e -> p k e", p=128)
    w_out_v = w_out.rearrange("(k p) e -> p k e", p=128)

    P = Prog()
    with ExitStack() as es:
        def sb(name, shape, dt):
            return es.enter_context(nc.sbuf_tensor("sb_" + name, shape, dt))

        def ps(name, shape, dt):
            return es.enter_context(nc.psum_tensor("ps_" + name, shape, dt))

        slots = [sb("wslot%d" % i, [128, SLOTW], BF16) for i in range(NSLOT)]
        slot_parts = [[T("ws%d_%d" % (i, j)) for j in range(4)] for i in range(NSLOT)]
        xs = [sb("xs%d" % i, [128, D], F32) for i in range(8)]
        t_xs = [T("xs%d" % i) for i in range(8)]
        hb = Ring([sb("hb%d" % i, [128, D], BF16) for i in range(4)], "hb")
        hT = sb("hT", [128, KC * TT], BF16)
        t_hT = [T("hT%d" % i) for i in range(4)]
        tmpf = Ring([sb("tmpf%d" % i, [128, TT], F32) for i in range(5)], "tmpf")
        tmpb = Ring([sb("tmpb%d" % i, [128, TT], BF16) for i in range(3)], "tmpb")
        h1 = [sb("h1_%d" % c, [128, 30 + TT], BF16) for c in range(8)]
        t_h1h = [T("h1h%d" % c) for c in range(8)]
        t_h1c = [T("h1c%d" % c) for c in range(8)]
        Vsb = sb("Vsb", [128, 8 * TT], BF16)
        t_V = [T("V%d" % c) for c in range(8)]
        rstdT = sb("rstdT", [128, TT], F32)
        nmr = sb("nmr", [128, TT], F32)
        t_rstdT, t_nmr = T("rstdT"), T("nmr")
        cgT = sb("cgT", [128, 8 * TT], BF16)
        t_cgT = [T("cgT%d" % c) for c in range(8)]
        m1 = sb("m1", [128, 8 * TT], BF16)
        t_m1 = [T("m1_%d" % c) for c in range(8)]
        cosb = Ring([sb("cosb%d" % i, [128, TT], F32) for i in range(1)], "cosb")
        sinb = Ring([sb("sinb%d" % i, [128, TT], F32) for i in range(1)], "sinb")
        qT, t_qT = cgT, t_cgT
        kT = sb("kT", [128, 4 * 640], BF16)
        t_kprev = T("kprev")
        t_kcur = [T("kcur%d" % c) for c in range(4)]
        Vaug = sb("Vaug", [128, 5 * 4 * 65], BF16)
        t_vprev = T("vprev")
        t_vcur = [T("vcur%d" % c) for c in range(4)]
        sgaT, t_sgaT = Vsb, t_V
        Pb = Ring([sb("Pb%d" % i, [128, 1024], BF16) for i in range(2)], "Pb")
        small = Ring([sb("small%d" % i, [128, 8], F32) for i in range(3)], "small")
        a_tm = [sb("a_tm%d" % i, [128, D], BF16) for i in range(4)]
        t_atm = [[T("atm%d_%d" % (i, h)) for h in range(4)] for i in range(4)]
        agT = sb("agT", [128, 8 * TT], BF16)
        t_ag = [[T("ag%d_%d" % (c, b)) for b in range(4)] for c in range(8)]
        mT = sb("mT", [128, 8 * TT], BF16)
        t_mT = [T("mT%d" % c) for c in range(8)]
        xo = Ring([sb("xo%d" % i, [128, D], F32) for i in range(2)], "xo")
        ssq = sb("ssq", [128, 16], F32)
        t_ssq = T("ssq")
        t_ssq2 = T("ssq2")
        g_tab = sb("g_tab", [128, D], F32)
        fg_tab = sb("fg_tab", [128, D], F32)
        colp = sb("colp", [128, 24], F32)
        wdw = sb("wdw", [128, 8 * 31], F32)
        esink = sb("esink", [128, 16], F32)
        maskb = sb("maskb", [128, 256], BF16)
        ident = sb("identb", [128, 128], BF16)
        ones = sb("ones", [128, 128], BF16)
        rpm = sb("rpm", [128, 128], BF16)
        t_rpm = T("rpm")
        qhl = Ring([sb("qhl%d" % i, [128, TT], BF16) for i in range(6)], "qhl")
        accf = Ring([sb("accf%d" % i, [128, TT], F32) for i in range(2)], "accf")
        t_const = T("const")
        t_esink = T("esink")
        dummy = sb("dummy", [128, 8], F32)
        t_dummy = T("dummy")
        t_warm = T("warm")

        def warm(func):
            P.add("act", lambda e: e.activation(out=dummy[:, 4:5], in_=ones[:, 0:1], func=func), r=[t_c7], w=[t_warm])

        pd = [ps("pd%d" % i, [128, 1024], F32) for i in range(3)]
        p6 = ps("p6", [128, 512], F32)
        ptr = ps("ptr", [128, 1024], BF16)
        ptrf = ptr[:, :].bitcast(F32)
        p6b = p6[:, :].bitcast(BF16)
        trs = [(ptr[:, :], None), (p6b, None)]
        tr_i = [0]
        t_pd = [[T("pd%d_%d" % (i, h), ps=True) for h in range(2)] for i in range(3)]
        t_p6 = T("p6", ps=True)
        t_ptr = T("ptr", ps=True)
        bank_all = [(pd[0][:, 0:512], t_pd[0][0]), (pd[1][:, 0:512], t_pd[1][0]), (pd[2][:, 0:512], t_pd[2][0]),
                    (p6[:, :], t_p6),
                    (pd[0][:, 512:1024], t_pd[0][1]), (pd[1][:, 512:1024], t_pd[1][1]), (pd[2][:, 512:1024], t_pd[2][1])]
        bank_job = [(p6[:, :], t_p6), (ptrf, t_ptr)]
        trs = [(ptr[:, :], t_ptr), (p6b, t_p6)]

        def tr_bank():
            k = tr_i[0] % 2
            tr_i[0] += 1
            return trs[k]
        bank_O = [(pd[2][:, 0:512], t_pd[2][0]), (pd[2][:, 512:1024], t_pd[2][1])]
        ring_state = {"all": 0, "job": 0, "O": 0, "S": 0}
        bank_mode = ["all"]

        def gen_bank():
            lst = bank_all if bank_mode[0] == "all" else bank_job
            k = ring_state[bank_mode[0]] % len(lst)
            ring_state[bank_mode[0]] += 1
            return lst[k]

        def o_bank():
            k = ring_state["O"] % 2
            ring_state["O"] += 1
            return bank_O[k]

        def s_double():
            k = ring_state["S"] % 2
            ring_state["S"] += 1
            return pd[k], t_pd[k]

        sems = {e: es.enter_context(nc.semaphore("s_" + e)) for e in ENGS}
        block = es.enter_context(nc.Block())

        def ld(eng, out_ap, in_ap, key, w):
            return P.add(eng, lambda e: e.dma_start(out=out_ap, in_=in_ap), w=w, dma=key)

        ld("sp", g_tab[:, :], g_tab_d[:, :], "c0", [t_const])
        t_c1, t_c2, t_c3, t_c4, t_c5, t_c6, t_c7 = [T("c%d" % i) for i in range(1, 8)]
        ld("sp", fg_tab[:, :], fg_tab_d[:, :], "c1", [t_c1])
        ld("sp", colp[:, :], colp_d[:, :], "c2", [t_c2])
        ld("sp", wdw[:, :], wdw_d[:, :], "c3", [t_c3])
        ld("sp", esink[:, :], sinks_d[:, :], "c4", [t_c4])
        ld("pool", maskb[:, :], mask_d[:, :], "c5", [t_c5])
        ld("pool", ident[:, :], ident_d[:, :], "c6", [t_c6])
        ld("pool", rpm[:, :], rperm_d[:, :], "c8", [t_rpm])
        P.add("pool", lambda e: e.memset(ones[:, :], 1.0), w=[t_c7])
        P.add("pool", lambda e: e.memset(dummy[:, 6:7], 0.625), w=[])
        P.add("act", lambda e: e.activation(out=esink[:, :], in_=esink[:, :], func=AF.Exp), r=[t_c4], w=[t_esink])
        P.add("pool", lambda e: e.memset(Vaug[:, :], 1.0), w=[t_vprev] + t_vcur)
        for c in range(8):
            P.add("pool", lambda e, c=c: e.memset(h1[c][:, 0:30], 0.0), w=[t_h1h[c]])
        P.add("pool", lambda e: e.memset(kT[:, :], 0.0), w=[t_kprev] + t_kcur)

        def slot3(si, n):
            return slots[si][:, 0:8 * n].rearrange("p (k n) -> p k n", k=8)

        def bwidth(kind):
            if kind == "diag":
                return (31 - NDVE) * 128
            return WIDTH.get(kind, SLOTW)

        def prep_block(kind, j, si, bi):
            sl = slots[si]
            tp = slot_parts[si]
            key = "wp%d" % si
            if kind == "glu":
                v = slot3(si, 256)
                ld("pool", v[:, :, 0:128], w_in_v[:, :, j * 128:(j + 1) * 128], key, [tp[0]])
                ld("pool", v[:, :, 128:256], w_in_v[:, :, 1024 + j * 128:1024 + (j + 1) * 128], key, [tp[1]])
            elif kind == "diag":
                v = sl[:, 0:31 * 128].rearrange("p (k n) -> p k n", k=31)
                NPE_ = 31 - NDVE
                for k in range(NPE_):
                    P.add("pool", lambda e, k=k, v=v: e.tensor_tensor(
                        out=v[:, k, :], in0=ident[:, :],
                        in1=wdw[:, j * 31 + k:j * 31 + k + 1].to_broadcast([128, 128]), op=ALU.mult),
                          r=[t_c3, t_c6], w=[tp[0]] if k == 0 else [tp[1]] if k == NPE_ - 1 else [])
            elif kind in ("cgate", "agate", "wout", "gconv", "gattn", "wco", "wao"):
                v = slot3(si, 512)
                src = {"cgate": lambda: w_in_v[:, :, 2048 + j * 512:2048 + (j + 1) * 512],
                       "agate": lambda: w_in_v[:, :, 4608 + j * 512:4608 + (j + 1) * 512],
                       "gconv": lambda: w_in_v[:, :, 5632 + j * 512:5632 + (j + 1) * 512],
                       "gattn": lambda: w_in_v[:, :, 6656 + j * 512:6656 + (j + 1) * 512],
                       "wout": lambda: w_out_v[:, :, j * 512:(j + 1) * 512],
                       "wco": lambda: w_co_v[:, :, j * 512:(j + 1) * 512],
                       "wao": lambda: w_ao_v[:, :, j * 512:(j + 1) * 512]}[kind]()
                ld("pool", v[:, :, 0:256], src[:, :, 0:256], key, [tp[0]])
                ld("pool", v[:, :, 256:512], src[:, :, 256:512], key, [tp[1]])
            elif kind == "v":
                v = slot3(si, 256)
                ld("pool", v[:, :, :], w_in_v[:, :, 4352:4608], key, [tp[0]])
            elif kind == "q":
                v = slot3(si, 256)
                for ci in range(2):
                    c0 = 3072 + (2 * j + ci) * 128
                    ld("pool", v[:, :, ci * 128:(ci + 1) * 128], w_in_v[:, :, c0:c0 + 128], key, [tp[ci]])
            elif kind == "k":
                v = slot3(si, 128)
                ld("pool", v[:, :, 0:64], w_in_v[:, :, 4096 + j * 64:4096 + (j + 1) * 64], key, [tp[0]])
                ld("pool", v[:, :, 64:128], w_in_v[:, :, 4096 + j * 64:4096 + (j + 1) * 64], key, [tp[1]])
            else:
                raise ValueError(kind)
            t_s = T("scr%d" % bi)
            W = bwidth(kind)
            P.add("sp", lambda e, sl=sl, bi=bi, W=W: e.dma_start(out=wscr[bi, :, 0:W], in_=sl[:, 0:W]),
                  r=list(tp), w=[t_s], dma="wst%d" % si)
            return t_s

        t_scr = {}
        st = {"next_load": 0, "next_use": 0}
        loaded = {}
        total_stream = NT * NB

        def issue_load():
            n = st["next_load"]
            st["next_load"] += 1
            it, bi = divmod(n, NB)
            si = n % NSLOT
            tp = slot_parts[si]
            if it == 0:
                kind, j = order[bi]
                P.add("pool", lambda e: e.memset(dummy[:, 0:1], 0.0), w=list(tp) + [t_dummy])
                t_scr[bi] = prep_block(kind, j, si, bi)
            else:
                W = bwidth(order[bi][0])
                P.add("sp", lambda e, si=si, bi=bi, W=W: e.dma_start(out=slots[si][:, 0:W], in_=wscr[bi, :, 0:W]),
                      r=[t_scr[bi]], w=list(tp), dma="w%d" % si)
            loaded[n] = si

        def wnext(kind, j):
            if discover:
                disc.append((kind, j))
                return 0, slot_parts[0]
            n = st["next_use"]
            st["next_use"] += 1
            while st["next_load"] < min(total_stream, n + NSLOT - 1) or n not in loaded:
                issue_load()
            it, bi = divmod(n, NB)
            assert order[bi] == (kind, j), (order[bi], kind, j)
            si = loaded[n]
            return si, slot_parts[si]

        def mm(out, lhsT, rhs, start, stop, r, w):
            P.add("pe", lambda e: e.matmul(out, lhsT=lhsT, rhs=rhs, start=start, stop=stop), r=r, w=w)

        hT3 = hT[:, :].rearrange("p (k n) -> p k n", k=KC)

        def proj_fm(si, tp, col0, bank, t_bank, ncols=512):
            v = slot3(si, ncols)
            for kc in range(KC):
                mm(bank, v[:, kc, col0:col0 + 128], hT3[:, kc, :], kc == 0, kc == KC - 1,
                   r=list(tp) + t_hT, w=[t_bank])

        hbs = {}

        def stage_N_load(it):
            P.tag = "N"
            par = it % 2
            for b in range(4):
                xt, t_x = xs[par * 4 + b], t_xs[par * 4 + b]
                row0 = it * TT + b * 128
                P.add("sp", lambda e, xt=xt, row0=row0: e.dma_start(out=xt[:, :], in_=x[row0:row0 + 128, :]),
                      w=[t_x], dma="x%d" % (par * 4 + b))

        def stage_N_pre(it):
            P.tag = "N"
            par = it % 2
            for b in range(4):
                xt, t_x = xs[par * 4 + b], t_xs[par * 4 + b]
                hbt, t_hb = hb.get()
                hbs[(it, b)] = (hbt, t_hb)
                P.add("act", lambda e, xt=xt, b=b, hbt=hbt: e.activation(out=hbt[:, :], in_=xt[:, :], func=AF.Square,
                                                                          accum_out=ssq[:, b:b + 1]),
                      r=[t_x], w=[t_hb, t_ssq])
            P.add("act", lambda e: e.activation(out=ssq[:, 4:8], in_=ssq[:, 0:4], func=AF.Ln, scale=1.0 / D, bias=EPS),
                  r=[t_ssq], w=[t_ssq])
            P.add("act", lambda e: e.activation(out=ssq[:, 4:8], in_=ssq[:, 4:8], func=AF.Exp, scale=-0.5),
                  r=[t_ssq], w=[t_ssq])
            for b in range(4):
                xt, t_x = xs[par * 4 + b], t_xs[par * 4 + b]
                hbt, t_hb = hbs[(it, b)]
                P.add("dve", lambda e, xt=xt, hbt=hbt, b=b: e.scalar_tensor_tensor(
                    out=hbt[:, :], in0=xt[:, :], scalar=ssq[:, 4 + b:5 + b], in1=g_tab[:, :], op0=ALU.mult, op1=ALU.mult),
                    r=[t_x, t_ssq, t_const], w=[t_hb])

        def stage_N_pe(it):
            P.tag = "N"
            for b in range(4):
                hbt, t_hb = hbs.pop((it, b))
                trb, t_trb = tr_bank()
                for kc in range(KC):
                    P.add("pe", lambda e, hbt=hbt, kc=kc, trb=trb: e.transpose(out=trb[:, kc * 128:(kc + 1) * 128],
                                                                                in_=hbt[:, kc * 128:(kc + 1) * 128],
                                                                                identity=ident[:, :]),
                          r=[t_hb, t_c6], w=[t_trb])
                P.add("act", lambda e, b=b, trb=trb: e.activation(out=hT3[:, :, b * 128:(b + 1) * 128],
                                                                   in_=trb.rearrange("p (k n) -> p k n", k=KC), func=AF.Copy),
                      r=[t_trb], w=[t_hT[b]])

        kT3 = kT[:, :].rearrange("p (h n) -> p h n", h=4)
        rope_tabs = {}
        kstate = {}
        bank_hook = [None]

        def rope_p1(si, tps, col0, bank_fn, cs, t_cs, ncols=512):
            bQ, t_bQ = bank_fn()
            proj_fm(si, tps, col0, bQ, t_bQ, ncols)
            qh, t_qh = qhl.get()
            ql, t_ql = qhl.get()
            a, t_a = tmpf.get()
            P.add("act", lambda e: e.activation(out=qh[:, :], in_=bQ, func=AF.Copy), r=[t_bQ], w=[t_qh])
            P.add("dve", lambda e: e.tensor_tensor(out=ql[:, :], in0=bQ, in1=qh[:, :], op=ALU.subtract),
                  r=[t_bQ, t_qh], w=[t_ql])
            P.add("dve", lambda e: e.tensor_tensor(out=a[:, :], in0=bQ, in1=cs[:, :], op=ALU.mult),
                  r=[t_bQ, t_cs], w=[t_a])
            return (qh, t_qh, ql, t_ql, a, t_a)

        def rope_p2(state, bank_fn, sn, t_sn, out_ap, t_out, defer_add=False):
            qh, t_qh, ql, t_ql, a, t_a = state
            bR, t_bR = bank_fn()
            mm(bR, rpm[:, :], qh[:, :], True, False, r=[t_rpm, t_qh], w=[t_bR])
            mm(bR, rpm[:, :], ql[:, :], False, True, r=[t_rpm, t_ql], w=[t_bR])
            b, t_b = tmpf.get()
            P.add("dve", lambda e: e.tensor_tensor(out=b[:, :], in0=bR, in1=sn[:, :], op=ALU.mult),
                  r=[t_bR, t_sn], w=[t_b])
            def final_add():
                P.add("dve", lambda e: e.tensor_tensor(out=out_ap, in0=a[:, :], in1=b[:, :], op=ALU.add),
                      r=[t_a, t_b], w=[t_out])
            if defer_add:
                return final_add
            final_add()

        def kprep(it):
            P.tag = "B.k"
            cs, t_cs = cosb.get()
            sn, t_sn = sinb.get()
            P.add("sp", lambda e: e.dma_start(out=cs[:, :], in_=cos_d[:, it * TT:(it + 1) * TT]), w=[t_cs], dma="cs0")
            P.add("sp", lambda e: e.dma_start(out=sn[:, :], in_=sin_d[:, it * TT:(it + 1) * TT]), w=[t_sn], dma="sn0")
            rope_tabs[it] = (cs, t_cs, sn, t_sn)

        kpend = {}

        def kjob_p1(it, hk):
            P.tag = "B.k"
            cs, t_cs, sn, t_sn = rope_tabs[it]
            si, tp = wnext("k", hk)
            kpend[(it, hk)] = rope_p1(si, (tp[0], tp[1]), 0, bank_hook[0], cs, t_cs, ncols=128)

        def kjob_p2(it, hk):
            P.tag = "B.k"
            cs, t_cs, sn, t_sn = rope_tabs[it]
            rope_p2(kpend.pop((it, hk)), bank_hook[0], sn, t_sn, kT3[:, hk, 128:640], t_kcur[hk])

        def stage_A(it):
            bank_mode[0] = "all"
            S1, t_S1 = bank_all[5]
            S2, t_S2 = bank_all[6]
            ring5 = [bank_all[i] for i in (0, 1, 2, 3, 4)]
            r5 = [0]

            def bank5():
                k = r5[0] % len(ring5)
                r5[0] += 1
                return ring5[k]

            bank_hook[0] = bank5

            P.tag = "A.cgate"
            for j in range(2):
                si, tp = wnext("cgate", j)
                for cc in range(4):
                    c = j * 4 + cc
                    bG, t_bG = bank5()
                    proj_fm(si, tp[0:2], cc * 128, bG, t_bG)
                    P.add("act", lambda e, bG=bG, c=c: e.activation(out=mT[:, c * TT:(c + 1) * TT], in_=bG, func=AF.Silu),
                          r=[t_bG], w=[t_mT[c]])

            def glu(c):
                P.tag = "A.glu"
                si, tp = wnext("glu", c)
                v = slot3(si, 256)
                bA, t_A = bank5()
                bB, t_B = bank5()
                for kc in range(KC):
                    mm(bA, v[:, kc, 0:128], hT3[:, kc, :], kc == 0, kc == KC - 1, r=list(tp[0:2]) + t_hT, w=[t_A])
                for kc in range(KC):
                    mm(bB, v[:, kc, 128:256], hT3[:, kc, :], kc == 0, kc == KC - 1, r=list(tp[0:2]) + t_hT, w=[t_B])
                sg, t_sg = tmpf.get()
                P.add("act", lambda e: e.activation(out=sg[:, :], in_=bB, func=AF.Sigmoid), r=[t_B], w=[t_sg])
                P.add("dve", lambda e: e.tensor_tensor(out=h1[c][:, 30:30 + TT], in0=bA, in1=sg[:, :], op=ALU.mult),
                      r=[t_A, t_sg], w=[t_h1c[c]])

            pend = []

            def stats(c, v2, t_v2):
                P.tag = "A.conv"
                mm(S1, ones[:, :], Vsb[:, c * TT:(c + 1) * TT], c == 0, c == 7, r=[t_c7, t_V[c]], w=[t_S1])
                mm(S2, ones[:, :], v2[:, :], c == 0, c == 7, r=[t_c7, t_v2], w=[t_S2])

            def conv(c):
                P.tag = "A.conv"
                if c == 7:
                    warm(AF.Ln)
                si, tp = wnext("diag", c)
                dv = slots[si][:, 0:31 * 128].rearrange("p (k n) -> p k n", k=31)
                bV, t_bV = bank5()
                NPE = 31 - NDVE
                for k in range(NPE):
                    mm(bV, dv[:, k, :], h1[c][:, k:k + TT], k == 0, k == NPE - 1, r=list(tp[0:2]) + [t_h1c[c], t_h1h[c]],
                       w=[t_bV])
                acc, t_acc = accf.get()
                for k in range(NPE, 31):
                    wk = wdw[:, c * 31 + k:c * 31 + k + 1]
                    if k == NPE:
                        P.add("dve", lambda e, k=k, wk=wk: e.tensor_scalar(out=acc[:, :], in0=h1[c][:, k:k + TT], scalar1=wk,
                                                                           scalar2=None, op0=ALU.mult),
                              r=[t_h1c[c], t_h1h[c], t_c3], w=[t_acc])
                    else:
                        P.add("dve", lambda e, k=k, wk=wk: e.scalar_tensor_tensor(out=acc[:, :], in0=h1[c][:, k:k + TT], scalar=wk,
                                                                                  in1=acc[:, :], op0=ALU.mult, op1=ALU.add),
                              r=[t_h1c[c], t_h1h[c], t_c3, t_acc], w=[t_acc])
                P.add("dve", lambda e: e.scalar_tensor_tensor(out=Vsb[:, c * TT:(c + 1) * TT], in0=bV, scalar=colp[:, c:c + 1],
                                                              in1=acc[:, :], op0=ALU.add, op1=ALU.add),
                      r=[t_bV, t_c2, t_acc], w=[t_V[c]])
                v2, t_v2 = tmpb.get()
                P.add("act", lambda e: e.activation(out=v2[:, :], in_=Vsb[:, c * TT:(c + 1) * TT], func=AF.Square),
                      r=[t_V[c]], w=[t_v2])
                P.add("pool", lambda e: e.tensor_copy(out=h1[c][:, 0:30], in_=h1[c][:, TT:TT + 30]),
                      r=[t_h1c[c]], w=[t_h1h[c]])
                pend.append((c, v2, t_v2))
                if len(pend) > 1:
                    stats(*pend.pop(0))

            kprep(it)
            glu(0)
            for c in range(8):
                if c + 1 < 8:
                    glu(c + 1)
                conv(c)
                if c % 2 == 0:
                    kjob_p1(it, c // 2)
                else:
                    kjob_p2(it, c // 2)
            stats(*pend.pop(0))

            gjobs = []
            gstate = {}

            def gjob(c):
                P.tag = "A.gconv"
                if c == 2:
                    ring5.extend([bank_all[5], bank_all[6]])
                    r5[0] = 0
                j, cc = divmod(c, 4)
                if j not in gstate:
                    gstate[j] = wnext("gconv", j)
                si, tp = gstate[j]
                bG, t_bG = bank5()
                proj_fm(si, tp[0:2], cc * 128, bG, t_bG)
                P.add("act", lambda e: e.activation(out=agT[:, c * TT:(c + 1) * TT], in_=bG, func=AF.Tanh, scale=0.5),
                      r=[t_bG], w=t_ag[c])

            P.tag = "A.ln"
            mean, t_mean = tmpf.get()
            msq, t_msq = tmpf.get()
            var, t_var = tmpf.get()
            P.add("dve", lambda e: e.tensor_scalar(out=mean[:, :], in0=S1, scalar1=1.0 / D, scalar2=None, op0=ALU.mult),
                  r=[t_S1], w=[t_mean])
            P.add("dve", lambda e: e.tensor_tensor(out=msq[:, :], in0=mean[:, :], in1=mean[:, :], op=ALU.mult),
                  r=[t_mean], w=[t_msq])
            P.add("dve", lambda e: e.scalar_tensor_tensor(out=var[:, :], in0=S2, scalar=1.0 / D, in1=msq[:, :],
                                                           op0=ALU.mult, op1=ALU.subtract),
                  r=[t_S2, t_msq], w=[t_var])
            P.add("act", lambda e: e.activation(out=var[:, :], in_=var[:, :], func=AF.Ln, bias=EPS), r=[t_var], w=[t_var])
            P.add("act", lambda e: e.activation(out=rstdT[:, :], in_=var[:, :], func=AF.Exp, scale=-0.5),
                  r=[t_var], w=[t_rstdT])
            P.add("dve", lambda e: e.scalar_tensor_tensor(out=nmr[:, :], in0=mean[:, :], scalar=-1.0, in1=rstdT[:, :],
                                                           op0=ALU.mult, op1=ALU.mult),
                  r=[t_mean, t_rstdT], w=[t_nmr])

            prev = None
            for c in range(8):
                gjob(c)
                P.tag = "A.norm"
                z, t_z = tmpf.get()
                P.add("dve", lambda e, z=z, c=c: e.tensor_tensor(out=z[:, :], in0=Vsb[:, c * TT:(c + 1) * TT], in1=rstdT[:, :],
                                                                  op=ALU.mult),
                      r=[t_V[c], t_rstdT], w=[t_z])
                P.add("dve", lambda e, z=z: e.tensor_tensor(out=z[:, :], in0=z[:, :], in1=nmr[:, :], op=ALU.add),
                      r=[t_z, t_nmr], w=[t_z])
                ca, t_ca = tmpb.get()
                P.add("act", lambda e, z=z, ca=ca, c=c: e.activation(out=ca[:, :], in_=z[:, :], func=AF.Silu,
                                                                      scale=colp[:, 8 + c:9 + c], bias=colp[:, 16 + c:17 + c]),
                      r=[t_z, t_c2], w=[t_ca])

                def cgmul(ca=ca, t_ca=t_ca, c=c):
                    P.add("dve", lambda e: e.tensor_tensor(out=cgT[:, c * TT:(c + 1) * TT], in0=ca[:, :],
                                                           in1=mT[:, c * TT:(c + 1) * TT], op=ALU.mult),
                          r=[t_ca, t_mT[c]], w=[t_cgT[c]])
                if prev is not None:
                    prev()
                prev = cgmul
            prev()
            stage_B0(it)

            P.tag = "A.wco"
            cg3 = cgT[:, :].rearrange("p (k n) -> p k n", k=8)
            for j in range(2):
                si, tp = wnext("wco", j)
                v = slot3(si, 512)
                for dd in range(4):
                    dch = j * 4 + dd
                    bY, t_bY = bank5()
                    for kc in range(KC):
                        mm(bY, v[:, kc, dd * 128:(dd + 1) * 128], cg3[:, kc, :], kc == 0, kc == KC - 1,
                           r=list(tp[0:2]) + t_cgT, w=[t_bY])
                    P.add("dve", lambda e, bY=bY, dch=dch: e.scalar_tensor_tensor(
                        out=m1[:, dch * TT:(dch + 1) * TT], in0=agT[:, dch * TT:(dch + 1) * TT], scalar=1.0, in1=bY,
                        op0=ALU.add, op1=ALU.mult),
                          r=[t_bY] + t_ag[dch], w=[t_m1[dch]])

        Va4 = Vaug[:, :].rearrange("p (b h d) -> p b h d", b=5, h=4)
        mask3 = maskb[:, :].rearrange("p (k n) -> p k n", k=2)
        q3 = qT[:, :].rearrange("p (c n) -> p c n", c=8)
        ag3 = agT[:, :].rearrange("p (c n) -> p c n", c=8)
        sga3 = sgaT[:, :].rearrange("p (c n) -> p c n", c=8)

        qpend = {}

        def q_p1(it, si, tp, ci, c):
            P.tag = "B.q"
            cs, t_cs, sn, t_sn = rope_tabs[it]
            qpend[(it, c)] = rope_p1(si, (tp[0], tp[1]), ci * 128, gen_bank, cs, t_cs, ncols=256)

        def q_p2(it, c, defer_add=False):
            P.tag = "B.q"
            cs, t_cs, sn, t_sn = rope_tabs[it]
            return rope_p2(qpend.pop((it, c)), gen_bank, sn, t_sn, qT[:, c * TT:(c + 1) * TT], t_qT[c], defer_add)

        def stage_B0(it):
            bank_mode[0] = "all"
            warm(AF.Exp)
            P.tag = "B.v"
            si, tp = wnext("v", 0)
            vv = slot3(si, 256)
            for b in range(4):
                bV, t_bV = gen_bank()
                for kc in range(KC):
                    mm(bV[:, 0:256], hT3[:, kc, b * 128:(b + 1) * 128], vv[:, kc, :], kc == 0, kc == KC - 1,
                       r=[tp[0], t_hT[b]], w=[t_bV])
                P.add("act", lambda e, bV=bV, b=b: e.activation(out=Va4[:, b + 1, :, 0:64],
                                                                 in_=bV[:, 0:256].rearrange("p (h d) -> p h d", h=4),
                                                                 func=AF.Copy),
                      r=[t_bV], w=[t_vcur[b]])
            si, tp = wnext("q", 0)
            q_p1(it, si, tp, 0, 0)
            q_p1(it, si, tp, 1, 1)
            q0_adds[it] = [q_p2(it, 0, defer_add=True), q_p2(it, 1, defer_add=True)]

        q0_adds = {}

        def stage_B(it):
            bank_mode[0] = "all"
            P.tag = "B.q"
            for f in q0_adds.pop(it):
                f()

            jobs = []

            def add_q_jobs(j):
                holder = {}

                def get():
                    if "s" not in holder:
                        holder["s"] = wnext("q", j)
                    return holder["s"]
                for ci in range(2):
                    jobs.append(lambda ci=ci, j=j: q_p1(it, *get(), ci, 2 * j + ci))
                    jobs.append(lambda ci=ci, j=j: q_p2(it, 2 * j + ci))

            def add_gate_jobs(kind, j):
                holder = {}

                def get():
                    if "s" not in holder:
                        holder["s"] = wnext(kind, j)
                    return holder["s"]

                def job(cc):
                    si, tp = get()
                    c = j * 4 + cc
                    P.tag = "B." + kind
                    bG, t_bG = gen_bank()
                    proj_fm(si, tp[0:2], cc * 128, bG, t_bG)
                    if kind == "agate":
                        th, t_th = tmpb.get()
                        P.add("act", lambda e: e.activation(out=th[:, :], in_=bG, func=AF.Tanh, scale=0.5),
                              r=[t_bG], w=[t_th])
                        P.add("dve", lambda e: e.scalar_tensor_tensor(out=sgaT[:, c * TT:(c + 1) * TT], in0=th[:, :], scalar=1.0,
                                                                      in1=bG, op0=ALU.add, op1=ALU.mult),
                              r=[t_bG, t_th], w=[t_sgaT[c]])
                    else:
                        P.add("act", lambda e: e.activation(out=mT[:, c * TT:(c + 1) * TT], in_=bG, func=AF.Tanh, scale=0.5),
                              r=[t_bG], w=[t_mT[c]])
                for cc in range(4):
                    jobs.append(lambda cc=cc: job(cc))

            add_q_jobs(1)
            add_gate_jobs("agate", 0)
            add_q_jobs(2)
            add_gate_jobs("agate", 1)
            add_q_jobs(3)
            add_gate_jobs("gattn", 0)
            add_gate_jobs("gattn", 1)
            sched = [[0], [1, 2], [3, 4], [5, 6], [7, 8], [9, 10], [11, 12], [13, 14], [15, 16], [17, 18], [19, 20], [21, 22],
                     [23, 24], [25, 26], [27], []]

            iters = [(hk, n) for hk in range(4) for n in range(4)]
            state = {}

            def S_part(i):
                hk, n = iters[i]
                P.tag = "B.attn"
                first = (it == 0 and n == 0)
                Sd, tS = s_double()
                for r in range(2):
                    for kb in range(2):
                        if first and kb == 0:
                            continue
                        kcol0 = n * 128 + kb * 128
                        t_k = [t_kcur[hk]] + ([t_kprev] if (n == 0 and kb == 0) else [])
                        base = r * 512 + kb * 256
                        mm(Sd[:, base:base + 256], kT3[r * 64:(r + 1) * 64, hk, kcol0:kcol0 + 128],
                           q3[r * 64:(r + 1) * 64, 2 * hk:2 * hk + 2, n * 128:(n + 1) * 128], True, True,
                           r=t_k + [t_qT[2 * hk], t_qT[2 * hk + 1]], w=[tS[r]])
                pb, t_p = Pb.get()
                k0 = 1 if first else 0
                if first:
                    S3 = Sd[:, :].rearrange("p (r k x) -> p r k x", r=2, k=2)
                    P3 = pb[:, :].rearrange("p (r k x) -> p r k x", r=2, k=2)
                    P.add("act", lambda e: e.activation(out=P3[:, :, 1, :], in_=S3[:, :, 1, :], func=AF.Exp, scale=0.125),
                          r=list(tS), w=[t_p])
                else:
                    P.add("act", lambda e: e.activation(out=pb[:, :], in_=Sd[:, :], func=AF.Exp, scale=0.125),
                          r=list(tS), w=[t_p])
                for r in range(2):
                    Pr = pb[:, r * 512:(r + 1) * 512].rearrange("p (k a q) -> p k a q", k=2, a=2)
                    P.add("dve", lambda e, Pr=Pr: e.tensor_tensor(
                        out=Pr[:, k0:2, :, :], in0=Pr[:, k0:2, :, :],
                        in1=mask3[:, k0:2, :].unsqueeze(2).to_broadcast([128, 2 - k0, 2, 128]), op=ALU.mult),
                        r=[t_p, t_c5], w=[t_p])
                state[i] = (pb, t_p, first)

            def PV_part(i):
                hk, n = iters[i]
                P.tag = "B.attn"
                pb, t_p, first = state.pop(i)
                P5 = pb[:, :].rearrange("p (r k a q) -> p r k a q", r=2, a=2, k=2)
                Ob, t_Ob = o_bank()
                O3 = Ob[:, 0:260].rearrange("p (g d) -> p g d", g=4)
                for g in range(4):
                    r, pair = g % 2, g // 2
                    for kb in range(2):
                        if first and kb == 0:
                            continue
                        vb = n + kb
                        t_v = t_vprev if vb == 0 else t_vcur[vb - 1]
                        mm(O3[:, g, :], P5[:, r, kb, pair, :], Va4[:, vb, hk, :],
                           (kb == 0) or first, kb == 1, r=[t_p, t_v], w=[t_Ob])
                sm, t_sm = small.get()
                at3 = a_tm[n][:, :].rearrange("p (g d) -> p g d", g=16)
                P.add("dve", lambda e: e.tensor_tensor(out=sm[:, 0:4], in0=O3[:, :, 64], in1=esink[:, hk * 4:hk * 4 + 4],
                                                       op=ALU.add),
                      r=[t_Ob, t_esink], w=[t_sm])
                P.add("dve", lambda e: e.reciprocal(out=sm[:, 4:8], in_=sm[:, 0:4]), r=[t_sm], w=[t_sm])
                P.add("dve", lambda e: e.tensor_tensor(
                    out=at3[:, hk * 4:hk * 4 + 4, :], in0=O3[:, :, 0:64],
                    in1=sm[:, 4:8].unsqueeze(2).to_broadcast([128, 4, 64]), op=ALU.mult),
                    r=[t_Ob, t_sm], w=[t_atm[n][hk]])

            bank_mode[0] = "job"
            S_part(0)
            for i in range(16):
                if i + 1 < 16:
                    S_part(i + 1)
                for jn in sched[i]:
                    jobs[jn]()
                PV_part(i)
            bank_mode[0] = "all"

        def stage_B2(it):
            bank_mode[0] = "all"
            P.add("pool", lambda e: e.tensor_copy(out=kT3[:, :, 0:128], in_=kT3[:, :, 512:640]), r=t_kcur, w=[t_kprev])
            P.add("pool", lambda e: e.tensor_copy(out=Va4[:, 0, :, 0:64], in_=Va4[:, 4, :, 0:64]), r=[t_vcur[3]], w=[t_vprev])
            P.tag = "B.attnT"
            for n in range(4):
                trb, t_trb = tr_bank()
                for c in range(8):
                    P.add("pe", lambda e, n=n, c=c, trb=trb: e.transpose(out=trb[:, c * 128:(c + 1) * 128],
                                                                          in_=a_tm[n][:, c * 128:(c + 1) * 128], identity=ident[:, :]),
                          r=t_atm[n] + [t_c6], w=[t_trb])
                P.add("dve", lambda e, n=n, trb=trb: e.scalar_tensor_tensor(out=ag3[:, :, n * 128:(n + 1) * 128],
                                                                            in0=trb.rearrange("p (c n) -> p c n", c=8), scalar=0.5,
                                                                            in1=sga3[:, :, n * 128:(n + 1) * 128], op0=ALU.mult, op1=ALU.mult),
                      r=[t_trb] + t_sgaT, w=[t_ag[c][n] for c in range(8)])
        def stage_B3(it):
            bank_mode[0] = "all"
            P.tag = "B.wao"
            for j in range(2):
                si, tp = wnext("wao", j)
                v = slot3(si, 512)
                for dd in range(4):
                    dch = j * 4 + dd
                    bY, t_bY = gen_bank()
                    for kc in range(KC):
                        mm(bY, v[:, kc, dd * 128:(dd + 1) * 128], ag3[:, kc, :], kc == 0, kc == KC - 1,
                           r=list(tp[0:2]) + t_ag[kc], w=[t_bY])
                    ya, t_ya = tmpf.get()
                    P.add("dve", lambda e, bY=bY, ya=ya, dch=dch: e.scalar_tensor_tensor(
                        out=ya[:, :], in0=mT[:, dch * TT:(dch + 1) * TT], scalar=1.0, in1=bY, op0=ALU.add, op1=ALU.mult),
                          r=[t_bY, t_mT[dch]], w=[t_ya])
                    P.add("dve", lambda e, ya=ya, dch=dch: e.tensor_tensor(out=mT[:, dch * TT:(dch + 1) * TT], in0=ya[:, :],
                                                                           in1=m1[:, dch * TT:(dch + 1) * TT], op=ALU.add),
                          r=[t_ya, t_m1[dch]], w=[t_mT[dch]])

        last_store = []

        def stage_O(it):
            P.tag = "O"
            bank_mode[0] = "all"
            par = it % 2
            si0, tp0 = wnext("wout", 0)
            si1, tp1 = wnext("wout", 1)
            wv = [slot3(si0, 512), slot3(si1, 512)]
            tps = [tp0, tp1]
            m3 = mT[:, :].rearrange("p (k n) -> p k n", k=8)
            for b in range(4):
                xt, t_x = xs[par * 4 + b], t_xs[par * 4 + b]
                xot, t_xo = xo.get()
                for half in range(2):
                    bO, t_bO = gen_bank()
                    for kc in range(KC):
                        mm(bO, m3[:, kc, b * 128:(b + 1) * 128], wv[half][:, kc, :], kc == 0, kc == KC - 1,
                           r=list(tps[half][0:2]) + t_mT, w=[t_bO])
                    P.add("dve", lambda e, bO=bO, xot=xot, xt=xt, half=half: e.scalar_tensor_tensor(
                        out=xot[:, half * 512:(half + 1) * 512], in0=bO, scalar=0.5, in1=xt[:, half * 512:(half + 1) * 512],
                        op0=ALU.mult, op1=ALU.add),
                        r=[t_bO, t_x], w=[t_xo])
                P.add("act", lambda e, xot=xot, b=b: e.activation(out=a_tm[b][:, :], in_=xot[:, :], func=AF.Square,
                                                                  accum_out=ssq[:, 8 + b:9 + b]),
                      r=[t_xo], w=t_atm[b] + [t_ssq2])
                P.add("act", lambda e, b=b: e.activation(out=ssq[:, 12 + b:13 + b], in_=ssq[:, 8 + b:9 + b], func=AF.Ln,
                                                         scale=1.0 / D, bias=EPS), r=[t_ssq2], w=[t_ssq2])
                P.add("act", lambda e, b=b: e.activation(out=ssq[:, 12 + b:13 + b], in_=ssq[:, 12 + b:13 + b], func=AF.Exp,
                                                         scale=-0.5), r=[t_ssq2], w=[t_ssq2])
                P.add("dve", lambda e, xot=xot, b=b: e.scalar_tensor_tensor(out=xot[:, :], in0=xot[:, :], scalar=ssq[:, 12 + b:13 + b],
                                                                            in1=fg_tab[:, :], op0=ALU.mult, op1=ALU.mult),
                      r=[t_xo, t_ssq2, t_c1], w=[t_xo])
                row0 = it * TT + b * 128
                key = "y%d" % ((xo.i - 1) % 2)
                op = P.add("sp", lambda e, xot=xot, row0=row0: e.dma_start(out=y[row0:row0 + 128, :], in_=xot[:, :]),
                           r=[t_xo], w=[], dma=key)
                last_store.append(op)

        if discover:
            stage_A(0)
            stage_B(0)
            stage_B2(0)
            stage_B3(0)
            stage_O(0)
            return disc
        stage_N_load(0)
        stage_N_pre(0)
        stage_N_pe(0)
        for it in range(NT):
            stage_A(it)
            if it + 1 < NT:
                stage_N_load(it + 1)
            stage_B(it)
            stage_B2(it)
            if it + 1 < NT:
                stage_N_pre(it + 1)
            stage_B3(it)
            if it + 1 < NT:
                stage_N_pe(it + 1)
            stage_O(it)
        fin = []
        for op in last_store[-2:]:
            t = T("fin")
            t.w = op
            fin.append(t)
        P.add("sp", None, r=fin)

        import os
        if os.environ.get("KTAGS"):
            import json
            json.dump({e: [o.tag for o in P.ops[e] if o.fn is not None] for e in ENGS}, open(os.environ["KTAGS"], "w"))
        dma_sems = {k: es.enter_context(nc.semaphore("d_" + k)) for k in P.dma_keys}
        P.emit_all(block, sems, dma_sems)
    return nc


def build_program():
    order = build_nc(None)
    return build_nc(order)


def _consts():
    d = np.arange(128) % 64
    f = d % 32
    inv_freq = (10000.0 ** (-(np.arange(0, 64, 2, dtype=np.float32)) / 64.0)).astype(np.float32)
    pos = np.arange(S, dtype=np.float32)
    ang = pos[None, :] * inv_freq[f][:, None]
    cosT = np.cos(ang).astype(np.float32)
    sgn = np.where(d < 32, -1.0, 1.0).astype(np.float32)
    sinT = (np.sin(ang) * sgn[:, None]).astype(np.float32)
    j = np.arange(128)[:, None]
    i = np.arange(128)[None, :]
    mask = np.concatenate([(j > i), (i >= j)], axis=1).astype(np.float32)
    ident = np.eye(128, dtype=np.float32)
    return cosT, sinT, mask, ident


_CACHE = {}


def kernel(x, norm_g, w_in, conv_dw_w, conv_dw_b, conv_ln_g, conv_ln_b, w_conv_out, attn_sinks, w_attn_out,
           w_out, final_norm_g):
    f = lambda a: np.ascontiguousarray(np.asarray(a, dtype=np.float32))
    x = f(x)
    cosT, sinT, mask, ident = _consts()
    col = lambda v: f(np.asarray(v, np.float32).reshape(8, 128).T)
    colp = np.concatenate([col(conv_dw_b[0]), col(conv_ln_g[0]), col(conv_ln_b[0])], axis=1)
    wdw = f(np.asarray(conv_dw_w[0], np.float32).reshape(31, 8, 128).transpose(2, 1, 0).reshape(128, 8 * 31))
    shared = {
        "w_in": f(w_in[0]), "w_co": f(w_conv_out[0]), "w_ao": f(w_attn_out[0]), "w_out": f(w_out[0]),
        "g_tab": f(np.broadcast_to(np.asarray(norm_g[0], np.float32)[None, :], (128, D))),
        "fg_tab": f(np.broadcast_to(np.asarray(final_norm_g, np.float32)[None, :], (128, D))),
        "colp": f(colp), "wdw": wdw,
        "sinks": f(np.broadcast_to(np.asarray(attn_sinks[0], np.float32)[None, :], (128, 16))),
        "cosT": cosT, "sinT": sinT, "maskT": mask, "ident": ident,
        "rperm": np.ascontiguousarray(ident[:, np.arange(128) ^ 32]),
    }
    if "nc" not in _CACHE:
        _CACHE["nc"] = build_program()
    nc = _CACHE["nc"]
    in_maps = []
    for b in range(NCORES):
        m = dict(shared)
        m["x"] = np.ascontiguousarray(x[b])
        in_maps.append(m)
    res = run_bass_kernel_spmd(nc, in_maps, core_ids=list(range(NCORES)))
    out = np.stack([np.asarray(res.results[b]["y"], dtype=np.float32).reshape(S, D) for b in range(NCORES)], axis=0)
    return out
```

```python
import numpy as np
from contextlib import ExitStack
import concourse.bass as bass
import concourse.mybir as mybir
from concourse.bass_utils import run_bass_kernel_spmd

F32 = mybir.dt.float32
BF16 = mybir.dt.bfloat16
ALU = mybir.AluOpType
AF = mybir.ActivationFunctionType

ENGS = ("pe", "act", "dve", "pool", "sp")

D = 1024
S = 4096
NCORES = 8
TT = 512
NT = S // TT
KC = 8
INW = 7680
NSLOT = 5
WIDTH = {"glu": 2048, "diag": 0, "k": 1024, "v": 2048, "q": 2048}
NDVE = 7
SLOTW = 4096
EPS = 1e-5


class T:
    __slots__ = ("name", "w", "rs", "ps")

    def __init__(self, name, ps=False):
        self.name = name
        self.w = None
        self.rs = []
        self.ps = ps


class Op:
    __slots__ = ("eng", "fn", "deps", "sig", "val", "sem", "isdma", "pos", "tag")


class Prog:
    def __init__(self):
        self.ops = {e: [] for e in ENGS}
        self.n = 0
        self.dma_keys = []
        self.tag = ""

    def add(self, eng, fn, r=(), w=(), dma=None):
        op = Op()
        op.tag = self.tag
        op.eng = eng
        op.fn = fn
        op.sig = False
        op.val = 0
        op.sem = dma
        op.isdma = dma is not None
        if dma is not None and dma not in self.dma_keys:
            self.dma_keys.append(dma)
        op.pos = self.n
        self.n += 1
        deps = []
        for t in r:
            if t.w is not None:
                deps.append(t.w)
            if t.ps:
                deps.extend(o for o in t.rs if o.eng != eng)
        for t in w:
            if t.w is not None:
                deps.append(t.w)
            deps.extend(t.rs)
        keep = []
        seen = set()
        latest = {}
        for d in deps:
            if id(d) in seen or d is op:
                continue
            seen.add(id(d))
            if d.isdma:
                keep.append(d)
                continue
            if d.eng == eng and eng == "pe":
                continue
            if d.eng not in latest or latest[d.eng].pos < d.pos:
                latest[d.eng] = d
        keep.extend(latest.values())
        for d in keep:
            d.sig = True
        op.deps = keep
        for t in r:
            t.rs.append(op)
        for t in w:
            t.w = op
            t.rs = []
        self.ops[eng].append(op)
        return op

    def emit_all(self, block, sems, dma_sems):
        for e in ENGS:
            cnt = 0
            for op in self.ops[e]:
                if op.isdma:
                    continue
                if op.sig:
                    cnt += 1
                    op.val = cnt
        dcnt = {}
        for op in sorted([o for e in ENGS for o in self.ops[e] if o.isdma], key=lambda o: o.pos):
            dcnt[op.sem] = dcnt.get(op.sem, 0) + 16
            op.val = dcnt[op.sem]

        def run(e, eng):
            known = {}
            for op in self.ops[e]:
                need = {}
                for d in op.deps:
                    if d.isdma:
                        key = ("d", d.sem)
                        s = dma_sems[d.sem]
                    else:
                        key = ("c", d.eng)
                        s = sems[d.eng]
                    if need.get(key, (None, 0))[1] < d.val:
                        need[key] = (s, d.val)
                for key, (s, v) in need.items():
                    if known.get(key, 0) >= v:
                        continue
                    eng.wait_ge(s, v)
                    known[key] = v
                if op.fn is None:
                    continue
                inst = op.fn(eng)
                if op.isdma:
                    inst.then_inc(dma_sems[op.sem], 16)
                elif op.sig:
                    inst.then_inc(sems[e], 1)

        @block.tensor
        def _(eng):
            run("pe", eng)

        @block.scalar
        def _(eng):
            run("act", eng)

        @block.vector
        def _(eng):
            run("dve", eng)

        @block.gpsimd
        def _(eng):
            run("pool", eng)

        @block.sync
        def _(eng):
            run("sp", eng)
            for key, v in dcnt.items():
                eng.wait_ge(dma_sems[key], v)


class Ring:
    def __init__(self, aps, name):
        self.aps = aps
        self.ts = [T("%s%d" % (name, i)) for i in range(len(aps))]
        self.i = 0

    def get(self):
        k = self.i % len(self.aps)
        self.i += 1
        return self.aps[k], self.ts[k]


def build_nc(order=None):
    discover = order is None
    nc = bass.Bass("TRN2", target_bir_lowering=False)
    dt_in = lambda name, shape: nc.dram_tensor(name, shape, F32, kind="ExternalInput").ap()
    x = dt_in("x", [S, D])
    w_in = dt_in("w_in", [D, INW])
    w_co = dt_in("w_co", [D, D])
    w_ao = dt_in("w_ao", [D, D])
    w_out = dt_in("w_out", [D, D])
    g_tab_d = dt_in("g_tab", [128, D])
    fg_tab_d = dt_in("fg_tab", [128, D])
    colp_d = dt_in("colp", [128, 24])
    wdw_d = dt_in("wdw", [128, 8 * 31])
    sinks_d = dt_in("sinks", [128, 16])
    cos_d = dt_in("cosT", [128, S])
    sin_d = dt_in("sinT", [128, S])
    mask_d = dt_in("maskT", [128, 256])
    ident_d = dt_in("ident", [128, 128])
    rperm_d = dt_in("rperm", [128, 128])
    y = nc.dram_tensor("y", [S, D], F32, kind="ExternalOutput").ap()
    NB = 37 if discover else len(order)
    wscr = nc.dram_tensor("wscr", [NB, 128, SLOTW], BF16, kind="Internal").ap()
    disc = []

    w_in_v = w_in.rearrange("(k p) e -> p k e", p=128)
    w_co_v = w_co.rearrange("(k p) e -> p k e", p=128)
    w_ao_v = w_ao.rearrange("(k p) e -> p k e", p=128)
    w_out_v = w_out.rearrange("(k p) e -> p k e", p=128)

    P = Prog()
    with ExitStack() as es:
        def sb(name, shape, dt):
            return es.enter_context(nc.sbuf_tensor("sb_" + name, shape, dt))

        def ps(name, shape, dt):
            return es.enter_context(nc.psum_tensor("ps_" + name, shape, dt))

        slots = [sb("wslot%d" % i, [128, SLOTW], BF16) for i in range(NSLOT)]
        slot_parts = [[T("ws%d_%d" % (i, j)) for j in range(4)] for i in range(NSLOT)]
        xs = [sb("xs%d" % i, [128, D], F32) for i in range(8)]
        t_xs = [T("xs%d" % i) for i in range(8)]
        hb = Ring([sb("hb%d" % i, [128, D], BF16) for i in range(4)], "hb")
        hT = sb("hT", [128, KC * TT], BF16)
        t_hT = [T("hT%d" % i) for i in range(4)]
        tmpf = Ring([sb("tmpf%d" % i, [128, TT], F32) for i in range(5)], "tmpf")
        tmpb = Ring([sb("tmpb%d" % i, [128, TT], BF16) for i in range(3)], "tmpb")
        h1 = [sb("h1_%d" % c, [128, 30 + TT], BF16) for c in range(8)]
        t_h1h = [T("h1h%d" % c) for c in range(8)]
        t_h1c = [T("h1c%d" % c) for c in range(8)]
        Vsb = sb("Vsb", [128, 8 * TT], BF16)
        t_V = [T("V%d" % c) for c in range(8)]
        rstdT = sb("rstdT", [128, TT], F32)
        nmr = sb("nmr", [128, TT], F32)
        t_rstdT, t_nmr = T("rstdT"), T("nmr")
        cgT = sb("cgT", [128, 8 * TT], BF16)
        t_cgT = [T("cgT%d" % c) for c in range(8)]
        m1 = sb("m1", [128, 8 * TT], BF16)
        t_m1 = [T("m1_%d" % c) for c in range(8)]
        cosb = Ring([sb("cosb%d" % i, [128, TT], F32) for i in range(1)], "cosb")
        sinb = Ring([sb("sinb%d" % i, [128, TT], F32) for i in range(1)], "sinb")
        qT, t_qT = cgT, t_cgT
        kT = sb("kT", [128, 4 * 640], BF16)
        t_kprev = T("kprev")
        t_kcur = [T("kcur%d" % c) for c in range(4)]
        Vaug = sb("Vaug", [128, 5 * 4 * 65], BF16)
        t_vprev = T("vprev")
        t_vcur = [T("vcur%d" % c) for c in range(4)]
        sgaT, t_sgaT = Vsb, t_V
        Pb = Ring([sb("Pb%d" % i, [128, 1024], BF16) for i in range(2)], "Pb")
        small = Ring([sb("small%d" % i, [128, 8], F32) for i in range(3)], "small")
        a_tm = [sb("a_tm%d" % i, [128, D], BF16) for i in range(4)]
        t_atm = [[T("atm%d_%d" % (i, h)) for h in range(4)] for i in range(4)]
        agT = sb("agT", [128, 8 * TT], BF16)
        t_ag = [[T("ag%d_%d" % (c, b)) for b in range(4)] for c in range(8)]
        mT = sb("mT", [128, 8 * TT], BF16)
        t_mT = [T("mT%d" % c) for c in range(8)]
        xo = Ring([sb("xo%d" % i, [128, D], F32) for i in range(2)], "xo")
        ssq = sb("ssq", [128, 16], F32)
        t_ssq = T("ssq")
        t_ssq2 = T("ssq2")
        g_tab = sb("g_tab", [128, D], F32)
        fg_tab = sb("fg_tab", [128, D], F32)
        colp = sb("colp", [128, 24], F32)
        wdw = sb("wdw", [128, 8 * 31], F32)
        esink = sb("esink", [128, 16], F32)
        maskb = sb("maskb", [128, 256], BF16)
        ident = sb("identb", [128, 128], BF16)
        ones = sb("ones", [128, 128], BF16)
        rpm = sb("rpm", [128, 128], BF16)
        t_rpm = T("rpm")
        qhl = Ring([sb("qhl%d" % i, [128, TT], BF16) for i in range(6)], "qhl")
        accf = Ring([sb("accf%d" % i, [128, TT], F32) for i in range(2)], "accf")
        t_const = T("const")
        t_esink = T("esink")
        dummy = sb("dummy", [128, 8], F32)
        t_dummy = T("dummy")
        t_warm = T("warm")

        def warm(func):
            P.add("act", lambda e: e.activation(out=dummy[:, 4:5], in_=ones[:, 0:1], func=func), r=[t_c7], w=[t_warm])

        pd = [ps("pd%d" % i, [128, 1024], F32) for i in range(3)]
        p6 = ps("p6", [128, 512], F32)
        ptr = ps("ptr", [128, 1024], BF16)
        ptrf = ptr[:, :].bitcast(F32)
        p6b = p6[:, :].bitcast(BF16)
        trs = [(ptr[:, :], None), (p6b, None)]
        tr_i = [0]
        t_pd = [[T("pd%d_%d" % (i, h), ps=True) for h in range(2)] for i in range(3)]
        t_p6 = T("p6", ps=True)
        t_ptr = T("ptr", ps=True)
        bank_all = [(pd[0][:, 0:512], t_pd[0][0]), (pd[1][:, 0:512], t_pd[1][0]), (pd[2][:, 0:512], t_pd[2][0]),
                    (p6[:, :], t_p6),
                    (pd[0][:, 512:1024], t_pd[0][1]), (pd[1][:, 512:1024], t_pd[1][1]), (pd[2][:, 512:1024], t_pd[2][1])]
        bank_job = [(p6[:, :], t_p6), (ptrf, t_ptr)]
        trs = [(ptr[:, :], t_ptr), (p6b, t_p6)]

        def tr_bank():
            k = tr_i[0] % 2
            tr_i[0] += 1
            return trs[k]
        bank_O = [(pd[2][:, 0:512], t_pd[2][0]), (pd[2][:, 512:1024], t_pd[2][1])]
        ring_state = {"all": 0, "job": 0, "O": 0, "S": 0}
        bank_mode = ["all"]

        def gen_bank():
            lst = bank_all if bank_mode[0] == "all" else bank_job
            k = ring_state[bank_mode[0]] % len(lst)
            ring_state[bank_mode[0]] += 1
            return lst[k]

        def o_bank():
            k = ring_state["O"] % 2
            ring_state["O"] += 1
            return bank_O[k]

        def s_double():
            k = ring_state["S"] % 2
            ring_state["S"] += 1
            return pd[k], t_pd[k]

        sems = {e: es.enter_context(nc.semaphore("s_" + e)) for e in ENGS}
        block = es.enter_context(nc.Block())

        def ld(eng, out_ap, in_ap, key, w):
            return P.add(eng, lambda e: e.dma_start(out=out_ap, in_=in_ap), w=w, dma=key)

        ld("sp", g_tab[:, :], g_tab_d[:, :], "c0", [t_const])
        t_c1, t_c2, t_c3, t_c4, t_c5, t_c6, t_c7 = [T("c%d" % i) for i in range(1, 8)]
        ld("sp", fg_tab[:, :], fg_tab_d[:, :], "c1", [t_c1])
        ld("sp", colp[:, :], colp_d[:, :], "c2", [t_c2])
        ld("sp", wdw[:, :], wdw_d[:, :], "c3", [t_c3])
        ld("sp", esink[:, :], sinks_d[:, :], "c4", [t_c4])
        ld("pool", maskb[:, :], mask_d[:, :], "c5", [t_c5])
        ld("pool", ident[:, :], ident_d[:, :], "c6", [t_c6])
        ld("pool", rpm[:, :], rperm_d[:, :], "c8", [t_rpm])
        P.add("pool", lambda e: e.memset(ones[:, :], 1.0), w=[t_c7])
        P.add("act", lambda e: e.activation(out=esink[:, :], in_=esink[:, :], func=AF.Exp), r=[t_c4], w=[t_esink])
        P.add("pool", lambda e: e.memset(Vaug[:, :], 1.0), w=[t_vprev] + t_vcur)
        for c in range(8):
            P.add("pool", lambda e, c=c: e.memset(h1[c][:, 0:30], 0.0), w=[t_h1h[c]])
        P.add("pool", lambda e: e.memset(kT[:, :], 0.0), w=[t_kprev] + t_kcur)

        def slot3(si, n):
            return slots[si][:, 0:8 * n].rearrange("p (k n) -> p k n", k=8)

        def bwidth(kind):
            if kind == "diag":
                return (31 - NDVE) * 128
            return WIDTH.get(kind, SLOTW)

        def prep_block(kind, j, si, bi):
            sl = slots[si]
            tp = slot_parts[si]
            key = "wp%d" % si
            if kind == "glu":
                v = slot3(si, 256)
                ld("pool", v[:, :, 0:128], w_in_v[:, :, j * 128:(j + 1) * 128], key, [tp[0]])
                ld("pool", v[:, :, 128:256], w_in_v[:, :, 1024 + j * 128:1024 + (j + 1) * 128], key, [tp[1]])
            elif kind == "diag":
                v = sl[:, 0:31 * 128].rearrange("p (k n) -> p k n", k=31)
                NPE_ = 31 - NDVE
                for k in range(NPE_):
                    P.add("pool", lambda e, k=k, v=v: e.tensor_tensor(
                        out=v[:, k, :], in0=ident[:, :],
                        in1=wdw[:, j * 31 + k:j * 31 + k + 1].to_broadcast([128, 128]), op=ALU.mult),
                          r=[t_c3, t_c6], w=[tp[0]] if k == 0 else [tp[1]] if k == NPE_ - 1 else [])
            elif kind in ("cgate", "agate", "wout", "gconv", "gattn", "wco", "wao"):
                v = slot3(si, 512)
                src = {"cgate": lambda: w_in_v[:, :, 2048 + j * 512:2048 + (j + 1) * 512],
                       "agate": lambda: w_in_v[:, :, 4608 + j * 512:4608 + (j + 1) * 512],
                       "gconv": lambda: w_in_v[:, :, 5632 + j * 512:5632 + (j + 1) * 512],
                       "gattn": lambda: w_in_v[:, :, 6656 + j * 512:6656 + (j + 1) * 512],
                       "wout": lambda: w_out_v[:, :, j * 512:(j + 1) * 512],
                       "wco": lambda: w_co_v[:, :, j * 512:(j + 1) * 512],
                       "wao": lambda: w_ao_v[:, :, j * 512:(j + 1) * 512]}[kind]()
                ld("pool", v[:, :, 0:256], src[:, :, 0:256], key, [tp[0]])
                ld("pool", v[:, :, 256:512], src[:, :, 256:512], key, [tp[1]])
            elif kind == "v":
                v = slot3(si, 256)
                ld("pool", v[:, :, :], w_in_v[:, :, 4352:4608], key, [tp[0]])
            elif kind == "q":
                v = slot3(si, 256)
                for ci in range(2):
                    c0 = 3072 + (2 * j + ci) * 128
                    ld("pool", v[:, :, ci * 128:(ci + 1) * 128], w_in_v[:, :, c0:c0 + 128], key, [tp[ci]])
            elif kind == "k":
                v = slot3(si, 128)
                ld("pool", v[:, :, 0:64], w_in_v[:, :, 4096 + j * 64:4096 + (j + 1) * 64], key, [tp[0]])
                ld("pool", v[:, :, 64:128], w_in_v[:, :, 4096 + j * 64:4096 + (j + 1) * 64], key, [tp[1]])
            else:
                raise ValueError(kind)
            t_s = T("scr%d" % bi)
            W = bwidth(kind)
            P.add("sp", lambda e, sl=sl, bi=bi, W=W: e.dma_start(out=wscr[bi, :, 0:W], in_=sl[:, 0:W]),
                  r=list(tp), w=[t_s], dma="wst%d" % si)
            return t_s

        t_scr = {}
        st = {"next_load": 0, "next_use": 0}
        loaded = {}
        total_stream = NT * NB

        def issue_load():
            n = st["next_load"]
            st["next_load"] += 1
            it, bi = divmod(n, NB)
            si = n % NSLOT
            tp = slot_parts[si]
            if it == 0:
                kind, j = order[bi]
                P.add("pool", lambda e: e.memset(dummy[:, 0:1], 0.0), w=list(tp) + [t_dummy])
                t_scr[bi] = prep_block(kind, j, si, bi)
            else:
                W = bwidth(order[bi][0])
                P.add("sp", lambda e, si=si, bi=bi, W=W: e.dma_start(out=slots[si][:, 0:W], in_=wscr[bi, :, 0:W]),
                      r=[t_scr[bi]], w=list(tp), dma="w%d" % si)
            loaded[n] = si

        def wnext(kind, j):
            if discover:
                disc.append((kind, j))
                return 0, slot_parts[0]
            n = st["next_use"]
            st["next_use"] += 1
            while st["next_load"] < min(total_stream, n + NSLOT - 1) or n not in loaded:
                issue_load()
            it, bi = divmod(n, NB)
            assert order[bi] == (kind, j), (order[bi], kind, j)
            si = loaded[n]
            return si, slot_parts[si]

        def mm(out, lhsT, rhs, start, stop, r, w):
            P.add("pe", lambda e: e.matmul(out, lhsT=lhsT, rhs=rhs, start=start, stop=stop), r=r, w=w)

        hT3 = hT[:, :].rearrange("p (k n) -> p k n", k=KC)

        def proj_fm(si, tp, col0, bank, t_bank, ncols=512):
            v = slot3(si, ncols)
            for kc in range(KC):
                mm(bank, v[:, kc, col0:col0 + 128], hT3[:, kc, :], kc == 0, kc == KC - 1,
                   r=list(tp) + t_hT, w=[t_bank])

        hbs = {}

        def stage_N_load(it):
            P.tag = "N"
            par = it % 2
            for b in range(4):
                xt, t_x = xs[par * 4 + b], t_xs[par * 4 + b]
                row0 = it * TT + b * 128
                P.add("sp", lambda e, xt=xt, row0=row0: e.dma_start(out=xt[:, :], in_=x[row0:row0 + 128, :]),
                      w=[t_x], dma="x%d" % (par * 4 + b))

        def stage_N_pre(it):
            P.tag = "N"
            par = it % 2
            for b in range(4):
                xt, t_x = xs[par * 4 + b], t_xs[par * 4 + b]
                hbt, t_hb = hb.get()
                hbs[(it, b)] = (hbt, t_hb)
                P.add("act", lambda e, xt=xt, b=b, hbt=hbt: e.activation(out=hbt[:, :], in_=xt[:, :], func=AF.Square,
                                                                          accum_out=ssq[:, b:b + 1]),
                      r=[t_x], w=[t_hb, t_ssq])
            P.add("act", lambda e: e.activation(out=ssq[:, 4:8], in_=ssq[:, 0:4], func=AF.Ln, scale=1.0 / D, bias=EPS),
                  r=[t_ssq], w=[t_ssq])
            P.add("act", lambda e: e.activation(out=ssq[:, 4:8], in_=ssq[:, 4:8], func=AF.Exp, scale=-0.5),
                  r=[t_ssq], w=[t_ssq])
            for b in range(4):
                xt, t_x = xs[par * 4 + b], t_xs[par * 4 + b]
                hbt, t_hb = hbs[(it, b)]
                P.add("dve", lambda e, xt=xt, hbt=hbt, b=b: e.scalar_tensor_tensor(
                    out=hbt[:, :], in0=xt[:, :], scalar=ssq[:, 4 + b:5 + b], in1=g_tab[:, :], op0=ALU.mult, op1=ALU.mult),
                    r=[t_x, t_ssq, t_const], w=[t_hb])

        def stage_N_pe(it):
            P.tag = "N"
            for b in range(4):
                hbt, t_hb = hbs.pop((it, b))
                trb, t_trb = tr_bank()
                for kc in range(KC):
                    P.add("pe", lambda e, hbt=hbt, kc=kc, trb=trb: e.transpose(out=trb[:, kc * 128:(kc + 1) * 128],
                                                                                in_=hbt[:, kc * 128:(kc + 1) * 128],
                                                                                identity=ident[:, :]),
                          r=[t_hb, t_c6], w=[t_trb])
                P.add("act", lambda e, b=b, trb=trb: e.activation(out=hT3[:, :, b * 128:(b + 1) * 128],
                                                                   in_=trb.rearrange("p (k n) -> p k n", k=KC), func=AF.Copy),
                      r=[t_trb], w=[t_hT[b]])

        kT3 = kT[:, :].rearrange("p (h n) -> p h n", h=4)
        rope_tabs = {}
        kstate = {}
        bank_hook = [None]

        def rope_p1(si, tps, col0, bank_fn, cs, t_cs, ncols=512):
            bQ, t_bQ = bank_fn()
            proj_fm(si, tps, col0, bQ, t_bQ, ncols)
            qh, t_qh = qhl.get()
            ql, t_ql = qhl.get()
            a, t_a = tmpf.get()
            P.add("act", lambda e: e.activation(out=qh[:, :], in_=bQ, func=AF.Copy), r=[t_bQ], w=[t_qh])
            P.add("dve", lambda e: e.tensor_tensor(out=ql[:, :], in0=bQ, in1=qh[:, :], op=ALU.subtract),
                  r=[t_bQ, t_qh], w=[t_ql])
            P.add("dve", lambda e: e.tensor_tensor(out=a[:, :], in0=bQ, in1=cs[:, :], op=ALU.mult),
                  r=[t_bQ, t_cs], w=[t_a])
            return (qh, t_qh, ql, t_ql, a, t_a)

        def rope_p2(state, bank_fn, sn, t_sn, out_ap, t_out, defer_add=False):
            qh, t_qh, ql, t_ql, a, t_a = state
            bR, t_bR = bank_fn()
            mm(bR, rpm[:, :], qh[:, :], True, False, r=[t_rpm, t_qh], w=[t_bR])
            mm(bR, rpm[:, :], ql[:, :], False, True, r=[t_rpm, t_ql], w=[t_bR])
            b, t_b = tmpf.get()
            P.add("dve", lambda e: e.tensor_tensor(out=b[:, :], in0=bR, in1=sn[:, :], op=ALU.mult),
                  r=[t_bR, t_sn], w=[t_b])
            def final_add():
                P.add("dve", lambda e: e.tensor_tensor(out=out_ap, in0=a[:, :], in1=b[:, :], op=ALU.add),
                      r=[t_a, t_b], w=[t_out])
            if defer_add:
                return final_add
            final_add()

        def kprep(it):
            P.tag = "B.k"
            cs, t_cs = cosb.get()
            sn, t_sn = sinb.get()
            P.add("sp", lambda e: e.dma_start(out=cs[:, :], in_=cos_d[:, it * TT:(it + 1) * TT]), w=[t_cs], dma="cs0")
            P.add("sp", lambda e: e.dma_start(out=sn[:, :], in_=sin_d[:, it * TT:(it + 1) * TT]), w=[t_sn], dma="sn0")
            rope_tabs[it] = (cs, t_cs, sn, t_sn)

        kpend = {}

        def kjob_p1(it, hk):
            P.tag = "B.k"
            cs, t_cs, sn, t_sn = rope_tabs[it]
            si, tp = wnext("k", hk)
            kpend[(it, hk)] = rope_p1(si, (tp[0], tp[1]), 0, bank_hook[0], cs, t_cs, ncols=128)

        def kjob_p2(it, hk):
            P.tag = "B.k"
            cs, t_cs, sn, t_sn = rope_tabs[it]
            rope_p2(kpend.pop((it, hk)), bank_hook[0], sn, t_sn, kT3[:, hk, 128:640], t_kcur[hk])

        def stage_A(it):
            bank_mode[0] = "all"
            S1, t_S1 = bank_all[5]
            S2, t_S2 = bank_all[6]
            ring5 = [bank_all[i] for i in (0, 1, 2, 3, 4)]
            r5 = [0]

            def bank5():
                k = r5[0] % len(ring5)
                r5[0] += 1
                return ring5[k]

            bank_hook[0] = bank5

            P.tag = "A.cgate"
            for j in range(2):
                si, tp = wnext("cgate", j)
                for cc in range(4):
                    c = j * 4 + cc
                    bG, t_bG = bank5()
                    proj_fm(si, tp[0:2], cc * 128, bG, t_bG)
                    P.add("act", lambda e, bG=bG, c=c: e.activation(out=mT[:, c * TT:(c + 1) * TT], in_=bG, func=AF.Silu),
                          r=[t_bG], w=[t_mT[c]])

            def glu(c):
                P.tag = "A.glu"
                si, tp = wnext("glu", c)
                v = slot3(si, 256)
                bA, t_A = bank5()
                bB, t_B = bank5()
                for kc in range(KC):
                    mm(bA, v[:, kc, 0:128], hT3[:, kc, :], kc == 0, kc == KC - 1, r=list(tp[0:2]) + t_hT, w=[t_A])
                for kc in range(KC):
                    mm(bB, v[:, kc, 128:256], hT3[:, kc, :], kc == 0, kc == KC - 1, r=list(tp[0:2]) + t_hT, w=[t_B])
                sg, t_sg = tmpf.get()
                P.add("act", lambda e: e.activation(out=sg[:, :], in_=bB, func=AF.Sigmoid), r=[t_B], w=[t_sg])
                P.add("dve", lambda e: e.tensor_tensor(out=h1[c][:, 30:30 + TT], in0=bA, in1=sg[:, :], op=ALU.mult),
                      r=[t_A, t_sg], w=[t_h1c[c]])

            pend = []

            def stats(c, v2, t_v2):
                P.tag = "A.conv"
                mm(S1, ones[:, :], Vsb[:, c * TT:(c + 1) * TT], c == 0, c == 7, r=[t_c7, t_V[c]], w=[t_S1])
                mm(S2, ones[:, :], v2[:, :], c == 0, c == 7, r=[t_c7, t_v2], w=[t_S2])

            def conv(c):
                P.tag = "A.conv"
                if c == 7:
                    warm(AF.Ln)
                si, tp = wnext("diag", c)
                dv = slots[si][:, 0:31 * 128].rearrange("p (k n) -> p k n", k=31)
                bV, t_bV = bank5()
                NPE = 31 - NDVE
                for k in range(NPE):
                    mm(bV, dv[:, k, :], h1[c][:, k:k + TT], k == 0, k == NPE - 1, r=list(tp[0:2]) + [t_h1c[c], t_h1h[c]],
                       w=[t_bV])
                acc, t_acc = accf.get()
                for k in range(NPE, 31):
                    wk = wdw[:, c * 31 + k:c * 31 + k + 1]
                    if k == NPE:
                        P.add("dve", lambda e, k=k, wk=wk: e.tensor_scalar(out=acc[:, :], in0=h1[c][:, k:k + TT], scalar1=wk,
                                                                           scalar2=None, op0=ALU.mult),
                              r=[t_h1c[c], t_h1h[c], t_c3], w=[t_acc])
                    else:
                        P.add("dve", lambda e, k=k, wk=wk: e.scalar_tensor_tensor(out=acc[:, :], in0=h1[c][:, k:k + TT], scalar=wk,
                                                                                  in1=acc[:, :], op0=ALU.mult, op1=ALU.add),
                              r=[t_h1c[c], t_h1h[c], t_c3, t_acc], w=[t_acc])
                P.add("dve", lambda e: e.scalar_tensor_tensor(out=Vsb[:, c * TT:(c + 1) * TT], in0=bV, scalar=colp[:, c:c + 1],
                                                              in1=acc[:, :], op0=ALU.add, op1=ALU.add),
                      r=[t_bV, t_c2, t_acc], w=[t_V[c]])
                v2, t_v2 = tmpb.get()
                P.add("act", lambda e: e.activation(out=v2[:, :], in_=Vsb[:, c * TT:(c + 1) * TT], func=AF.Square),
                      r=[t_V[c]], w=[t_v2])
                P.add("pool", lambda e: e.tensor_copy(out=h1[c][:, 0:30], in_=h1[c][:, TT:TT + 30]),
                      r=[t_h1c[c]], w=[t_h1h[c]])
                pend.append((c, v2, t_v2))
                if len(pend) > 1:
                    stats(*pend.pop(0))

            kprep(it)
            glu(0)
            for c in range(8):
                if c + 1 < 8:
                    glu(c + 1)
                conv(c)
                if c % 2 == 0:
                    kjob_p1(it, c // 2)
                else:
                    kjob_p2(it, c // 2)
            stats(*pend.pop(0))

            gjobs = []
            gstate = {}

            def gjob(c):
                P.tag = "A.gconv"
                if c == 2:
                    ring5.extend([bank_all[5], bank_all[6]])
                    r5[0] = 0
                j, cc = divmod(c, 4)
                if j not in gstate:
                    gstate[j] = wnext("gconv", j)
                si, tp = gstate[j]
                bG, t_bG = bank5()
                proj_fm(si, tp[0:2], cc * 128, bG, t_bG)
                P.add("act", lambda e: e.activation(out=agT[:, c * TT:(c + 1) * TT], in_=bG, func=AF.Tanh, scale=0.5),
                      r=[t_bG], w=t_ag[c])

            P.tag = "A.ln"
            mean, t_mean = tmpf.get()
            msq, t_msq = tmpf.get()
            var, t_var = tmpf.get()
            P.add("dve", lambda e: e.tensor_scalar(out=mean[:, :], in0=S1, scalar1=1.0 / D, scalar2=None, op0=ALU.mult),
                  r=[t_S1], w=[t_mean])
            P.add("dve", lambda e: e.tensor_tensor(out=msq[:, :], in0=mean[:, :], in1=mean[:, :], op=ALU.mult),
                  r=[t_mean], w=[t_msq])
            P.add("dve", lambda e: e.scalar_tensor_tensor(out=var[:, :], in0=S2, scalar=1.0 / D, in1=msq[:, :],
                                                           op0=ALU.mult, op1=ALU.subtract),
                  r=[t_S2, t_msq], w=[t_var])
            P.add("act", lambda e: e.activation(out=var[:, :], in_=var[:, :], func=AF.Ln, bias=EPS), r=[t_var], w=[t_var])
            P.add("act", lambda e: e.activation(out=rstdT[:, :], in_=var[:, :], func=AF.Exp, scale=-0.5),
                  r=[t_var], w=[t_rstdT])
            P.add("dve", lambda e: e.scalar_tensor_tensor(out=nmr[:, :], in0=mean[:, :], scalar=-1.0, in1=rstdT[:, :],
                                                           op0=ALU.mult, op1=ALU.mult),
                  r=[t_mean, t_rstdT], w=[t_nmr])

            prev = None
            for c in range(8):
                gjob(c)
                P.tag = "A.norm"
                z, t_z = tmpf.get()
                P.add("dve", lambda e, z=z, c=c: e.tensor_tensor(out=z[:, :], in0=Vsb[:, c * TT:(c + 1) * TT], in1=rstdT[:, :],
                                                                  op=ALU.mult),
                      r=[t_V[c], t_rstdT], w=[t_z])
                P.add("dve", lambda e, z=z: e.tensor_tensor(out=z[:, :], in0=z[:, :], in1=nmr[:, :], op=ALU.add),
                      r=[t_z, t_nmr], w=[t_z])
                ca, t_ca = tmpb.get()
                P.add("act", lambda e, z=z, ca=ca, c=c: e.activation(out=ca[:, :], in_=z[:, :], func=AF.Silu,
                                                                      scale=colp[:, 8 + c:9 + c], bias=colp[:, 16 + c:17 + c]),
                      r=[t_z, t_c2], w=[t_ca])

                def cgmul(ca=ca, t_ca=t_ca, c=c):
                    P.add("dve", lambda e: e.tensor_tensor(out=cgT[:, c * TT:(c + 1) * TT], in0=ca[:, :],
                                                           in1=mT[:, c * TT:(c + 1) * TT], op=ALU.mult),
                          r=[t_ca, t_mT[c]], w=[t_cgT[c]])
                if prev is not None:
                    prev()
                prev = cgmul
            prev()
            stage_B0(it)

            P.tag = "A.wco"
            cg3 = cgT[:, :].rearrange("p (k n) -> p k n", k=8)
            for j in range(2):
                si, tp = wnext("wco", j)
                v = slot3(si, 512)
                for dd in range(4):
                    dch = j * 4 + dd
                    bY, t_bY = bank5()
                    for kc in range(KC):
                        mm(bY, v[:, kc, dd * 128:(dd + 1) * 128], cg3[:, kc, :], kc == 0, kc == KC - 1,
                           r=list(tp[0:2]) + t_cgT, w=[t_bY])
                    P.add("dve", lambda e, bY=bY, dch=dch: e.scalar_tensor_tensor(
                        out=m1[:, dch * TT:(dch + 1) * TT], in0=agT[:, dch * TT:(dch + 1) * TT], scalar=1.0, in1=bY,
                        op0=ALU.add, op1=ALU.mult),
                          r=[t_bY] + t_ag[dch], w=[t_m1[dch]])

        Va4 = Vaug[:, :].rearrange("p (b h d) -> p b h d", b=5, h=4)
        mask3 = maskb[:, :].rearrange("p (k n) -> p k n", k=2)
        q3 = qT[:, :].rearrange("p (c n) -> p c n", c=8)
        ag3 = agT[:, :].rearrange("p (c n) -> p c n", c=8)
        sga3 = sgaT[:, :].rearrange("p (c n) -> p c n", c=8)

        qpend = {}

        def q_p1(it, si, tp, ci, c):
            P.tag = "B.q"
            cs, t_cs, sn, t_sn = rope_tabs[it]
            qpend[(it, c)] = rope_p1(si, (tp[0], tp[1]), ci * 128, gen_bank, cs, t_cs, ncols=256)

        def q_p2(it, c, defer_add=False):
            P.tag = "B.q"
            cs, t_cs, sn, t_sn = rope_tabs[it]
            return rope_p2(qpend.pop((it, c)), gen_bank, sn, t_sn, qT[:, c * TT:(c + 1) * TT], t_qT[c], defer_add)

        def stage_B0(it):
            bank_mode[0] = "all"
            warm(AF.Exp)
            P.tag = "B.v"
            si, tp = wnext("v", 0)
            vv = slot3(si, 256)
            for b in range(4):
                bV, t_bV = gen_bank()
                for kc in range(KC):
                    mm(bV[:, 0:256], hT3[:, kc, b * 128:(b + 1) * 128], vv[:, kc, :], kc == 0, kc == KC - 1,
                       r=[tp[0], t_hT[b]], w=[t_bV])
                P.add("act", lambda e, bV=bV, b=b: e.activation(out=Va4[:, b + 1, :, 0:64],
                                                                 in_=bV[:, 0:256].rearrange("p (h d) -> p h d", h=4),
                                                                 func=AF.Copy),
                      r=[t_bV], w=[t_vcur[b]])
            si, tp = wnext("q", 0)
            q_p1(it, si, tp, 0, 0)
            q_p1(it, si, tp, 1, 1)
            q0_adds[it] = [q_p2(it, 0, defer_add=True), q_p2(it, 1, defer_add=True)]

        q0_adds = {}

        def stage_B(it):
            bank_mode[0] = "all"
            P.tag = "B.q"
            for f in q0_adds.pop(it):
                f()

            jobs = []

            def add_q_jobs(j):
                holder = {}

                def get():
                    if "s" not in holder:
                        holder["s"] = wnext("q", j)
                    return holder["s"]
                for ci in range(2):
                    jobs.append(lambda ci=ci, j=j: q_p1(it, *get(), ci, 2 * j + ci))
                    jobs.append(lambda ci=ci, j=j: q_p2(it, 2 * j + ci))

            def add_gate_jobs(kind, j):
                holder = {}

                def get():
                    if "s" not in holder:
                        holder["s"] = wnext(kind, j)
                    return holder["s"]

                def job(cc):
                    si, tp = get()
                    c = j * 4 + cc
                    P.tag = "B." + kind
                    bG, t_bG = gen_bank()
                    proj_fm(si, tp[0:2], cc * 128, bG, t_bG)
                    if kind == "agate":
                        th, t_th = tmpb.get()
                        P.add("act", lambda e: e.activation(out=th[:, :], in_=bG, func=AF.Tanh, scale=0.5),
                              r=[t_bG], w=[t_th])
                        P.add("dve", lambda e: e.scalar_tensor_tensor(out=sgaT[:, c * TT:(c + 1) * TT], in0=th[:, :], scalar=1.0,
                                                                      in1=bG, op0=ALU.add, op1=ALU.mult),
                              r=[t_bG, t_th], w=[t_sgaT[c]])
                    else:
                        P.add("act", lambda e: e.activation(out=mT[:, c * TT:(c + 1) * TT], in_=bG, func=AF.Tanh, scale=0.5),
                              r=[t_bG], w=[t_mT[c]])
                for cc in range(4):
                    jobs.append(lambda cc=cc: job(cc))

            add_q_jobs(1)
            add_gate_jobs("agate", 0)
            add_q_jobs(2)
            add_gate_jobs("agate", 1)
            add_q_jobs(3)
            add_gate_jobs("gattn", 0)
            add_gate_jobs("gattn", 1)
            sched = [[2, 1], [4, 3], [5, 6], [7, 8], [10, 9], [12, 11], [13, 14], [15, 16], [18, 17], [20, 19], [21, 22],
                     [23, 24], [25, 26], [27], [], []]

            iters = [(hk, n) for hk in range(4) for n in range(4)]
            state = {}

            def S_part(i):
                hk, n = iters[i]
                P.tag = "B.attn"
                first = (it == 0 and n == 0)
                Sd, tS = s_double()
                for r in range(2):
                    for kb in range(2):
                        if first and kb == 0:
                            continue
                        kcol0 = n * 128 + kb * 128
                        t_k = [t_kcur[hk]] + ([t_kprev] if (n == 0 and kb == 0) else [])
                        base = r * 512 + kb * 256
                        mm(Sd[:, base:base + 256], kT3[r * 64:(r + 1) * 64, hk, kcol0:kcol0 + 128],
                           q3[r * 64:(r + 1) * 64, 2 * hk:2 * hk + 2, n * 128:(n + 1) * 128], True, True,
                           r=t_k + [t_qT[2 * hk], t_qT[2 * hk + 1]], w=[tS[r]])
                pb, t_p = Pb.get()
                k0 = 1 if first else 0
                if first:
                    S3 = Sd[:, :].rearrange("p (r k x) -> p r k x", r=2, k=2)
                    P3 = pb[:, :].rearrange("p (r k x) -> p r k x", r=2, k=2)
                    P.add("act", lambda e: e.activation(out=P3[:, :, 1, :], in_=S3[:, :, 1, :], func=AF.Exp, scale=0.125),
                          r=list(tS), w=[t_p])
                else:
                    P.add("act", lambda e: e.activation(out=pb[:, :], in_=Sd[:, :], func=AF.Exp, scale=0.125),
                          r=list(tS), w=[t_p])
                for r in range(2):
                    Pr = pb[:, r * 512:(r + 1) * 512].rearrange("p (k a q) -> p k a q", k=2, a=2)
                    P.add("dve", lambda e, Pr=Pr: e.tensor_tensor(
                        out=Pr[:, k0:2, :, :], in0=Pr[:, k0:2, :, :],
                        in1=mask3[:, k0:2, :].unsqueeze(2).to_broadcast([128, 2 - k0, 2, 128]), op=ALU.mult),
                        r=[t_p, t_c5], w=[t_p])
                state[i] = (pb, t_p, first)

            def PV_part(i):
                hk, n = iters[i]
                P.tag = "B.attn"
                pb, t_p, first = state.pop(i)
                P5 = pb[:, :].rearrange("p (r k a q) -> p r k a q", r=2, a=2, k=2)
                Ob, t_Ob = o_bank()
                O3 = Ob[:, 0:260].rearrange("p (g d) -> p g d", g=4)
                for g in range(4):
                    r, pair = g % 2, g // 2
                    for kb in range(2):
                        if first and kb == 0:
                            continue
                        vb = n + kb
                        t_v = t_vprev if vb == 0 else t_vcur[vb - 1]
                        mm(O3[:, g, :], P5[:, r, kb, pair, :], Va4[:, vb, hk, :],
                           (kb == 0) or first, kb == 1, r=[t_p, t_v], w=[t_Ob])
                sm, t_sm = small.get()
                at3 = a_tm[n][:, :].rearrange("p (g d) -> p g d", g=16)
                P.add("dve", lambda e: e.tensor_tensor(out=sm[:, 0:4], in0=O3[:, :, 64], in1=esink[:, hk * 4:hk * 4 + 4],
                                                       op=ALU.add),
                      r=[t_Ob, t_esink], w=[t_sm])
                P.add("dve", lambda e: e.reciprocal(out=sm[:, 4:8], in_=sm[:, 0:4]), r=[t_sm], w=[t_sm])
                P.add("dve", lambda e: e.tensor_tensor(
                    out=at3[:, hk * 4:hk * 4 + 4, :], in0=O3[:, :, 0:64],
                    in1=sm[:, 4:8].unsqueeze(2).to_broadcast([128, 4, 64]), op=ALU.mult),
                    r=[t_Ob, t_sm], w=[t_atm[n][hk]])

            bank_mode[0] = "job"
            jobs[0]()
            S_part(0)
            for i in range(16):
                if i + 1 < 16:
                    S_part(i + 1)
                for jn in sched[i]:
                    jobs[jn]()
                PV_part(i)
            bank_mode[0] = "all"

        def stage_B2(it):
            bank_mode[0] = "all"
            P.add("pool", lambda e: e.tensor_copy(out=kT3[:, :, 0:128], in_=kT3[:, :, 512:640]), r=t_kcur, w=[t_kprev])
            P.add("pool", lambda e: e.tensor_copy(out=Va4[:, 0, :, 0:64], in_=Va4[:, 4, :, 0:64]), r=[t_vcur[3]], w=[t_vprev])
            P.tag = "B.attnT"
            for n in range(4):
                trb, t_trb = tr_bank()
                for c in range(8):
                    P.add("pe", lambda e, n=n, c=c, trb=trb: e.transpose(out=trb[:, c * 128:(c + 1) * 128],
                                                                          in_=a_tm[n][:, c * 128:(c + 1) * 128], identity=ident[:, :]),
                          r=t_atm[n] + [t_c6], w=[t_trb])
                P.add("dve", lambda e, n=n, trb=trb: e.scalar_tensor_tensor(out=ag3[:, :, n * 128:(n + 1) * 128],
                                                                            in0=trb.rearrange("p (c n) -> p c n", c=8), scalar=0.5,
                                                                            in1=sga3[:, :, n * 128:(n + 1) * 128], op0=ALU.mult, op1=ALU.mult),
                      r=[t_trb] + t_sgaT, w=[t_ag[c][n] for c in range(8)])
        def stage_B3(it):
            bank_mode[0] = "all"
            P.tag = "B.wao"
            for j in range(2):
                si, tp = wnext("wao", j)
                v = slot3(si, 512)
                for dd in range(4):
                    dch = j * 4 + dd
                    bY, t_bY = gen_bank()
                    for kc in range(KC):
                        mm(bY, v[:, kc, dd * 128:(dd + 1) * 128], ag3[:, kc, :], kc == 0, kc == KC - 1,
                           r=list(tp[0:2]) + t_ag[kc], w=[t_bY])
                    ya, t_ya = tmpf.get()
                    P.add("dve", lambda e, bY=bY, ya=ya, dch=dch: e.scalar_tensor_tensor(
                        out=ya[:, :], in0=mT[:, dch * TT:(dch + 1) * TT], scalar=1.0, in1=bY, op0=ALU.add, op1=ALU.mult),
                          r=[t_bY, t_mT[dch]], w=[t_ya])
                    P.add("dve", lambda e, ya=ya, dch=dch: e.tensor_tensor(out=mT[:, dch * TT:(dch + 1) * TT], in0=ya[:, :],
                                                                           in1=m1[:, dch * TT:(dch + 1) * TT], op=ALU.add),
                          r=[t_ya, t_m1[dch]], w=[t_mT[dch]])

        last_store = []

        def stage_O(it):
            P.tag = "O"
            bank_mode[0] = "all"
            par = it % 2
            si0, tp0 = wnext("wout", 0)
            si1, tp1 = wnext("wout", 1)
            wv = [slot3(si0, 512), slot3(si1, 512)]
            tps = [tp0, tp1]
            m3 = mT[:, :].rearrange("p (k n) -> p k n", k=8)
            for b in range(4):
                xt, t_x = xs[par * 4 + b], t_xs[par * 4 + b]
                xot, t_xo = xo.get()
                for half in range(2):
                    bO, t_bO = gen_bank()
                    for kc in range(KC):
                        mm(bO, m3[:, kc, b * 128:(b + 1) * 128], wv[half][:, kc, :], kc == 0, kc == KC - 1,
                           r=list(tps[half][0:2]) + t_mT, w=[t_bO])
                    P.add("dve", lambda e, bO=bO, xot=xot, xt=xt, half=half: e.scalar_tensor_tensor(
                        out=xot[:, half * 512:(half + 1) * 512], in0=bO, scalar=0.5, in1=xt[:, half * 512:(half + 1) * 512],
                        op0=ALU.mult, op1=ALU.add),
                        r=[t_bO, t_x], w=[t_xo])
                P.add("act", lambda e, xot=xot, b=b: e.activation(out=a_tm[b][:, :], in_=xot[:, :], func=AF.Square,
                                                                  accum_out=ssq[:, 8 + b:9 + b]),
                      r=[t_xo], w=t_atm[b] + [t_ssq2])
                P.add("act", lambda e, b=b: e.activation(out=ssq[:, 12 + b:13 + b], in_=ssq[:, 8 + b:9 + b], func=AF.Ln,
                                                         scale=1.0 / D, bias=EPS), r=[t_ssq2], w=[t_ssq2])
                P.add("act", lambda e, b=b: e.activation(out=ssq[:, 12 + b:13 + b], in_=ssq[:, 12 + b:13 + b], func=AF.Exp,
                                                         scale=-0.5), r=[t_ssq2], w=[t_ssq2])
                P.add("dve", lambda e, xot=xot, b=b: e.scalar_tensor_tensor(out=xot[:, :], in0=xot[:, :], scalar=ssq[:, 12 + b:13 + b],
                                                                            in1=fg_tab[:, :], op0=ALU.mult, op1=ALU.mult),
                      r=[t_xo, t_ssq2, t_c1], w=[t_xo])
                row0 = it * TT + b * 128
                key = "y%d" % ((xo.i - 1) % 2)
                op = P.add("sp", lambda e, xot=xot, row0=row0: e.dma_start(out=y[row0:row0 + 128, :], in_=xot[:, :]),
                           r=[t_xo], w=[], dma=key)
                last_store.append(op)

        if discover:
            stage_A(0)
            stage_B(0)
            stage_B2(0)
            stage_B3(0)
            stage_O(0)
            return disc
        stage_N_load(0)
        stage_N_pre(0)
        stage_N_pe(0)
        for it in range(NT):
            stage_A(it)
            if it + 1 < NT:
                stage_N_load(it + 1)
            stage_B(it)
            stage_B2(it)
            if it + 1 < NT:
                stage_N_pre(it + 1)
            stage_B3(it)
            if it + 1 < NT:
                stage_N_pe(it + 1)
            stage_O(it)
        fin = []
        for op in last_store[-2:]:
            t = T("fin")
            t.w = op
            fin.append(t)
        P.add("sp", None, r=fin)

        import os
        if os.environ.get("KTAGS"):
            import json
            json.dump({e: [o.tag for o in P.ops[e] if o.fn is not None] for e in ENGS}, open(os.environ["KTAGS"], "w"))
        dma_sems = {k: es.enter_context(nc.semaphore("d_" + k)) for k in P.dma_keys}
        P.emit_all(block, sems, dma_sems)
    return nc


def build_program():
    order = build_nc(None)
    return build_nc(order)


def _consts():
    d = np.arange(128) % 64
    f = d % 32
    inv_freq = (10000.0 ** (-(np.arange(0, 64, 2, dtype=np.float32)) / 64.0)).astype(np.float32)
    pos = np.arange(S, dtype=np.float32)
    ang = pos[None, :] * inv_freq[f][:, None]
    cosT = np.cos(ang).astype(np.float32)
    sgn = np.where(d < 32, -1.0, 1.0).astype(np.float32)
    sinT = (np.sin(ang) * sgn[:, None]).astype(np.float32)
    j = np.arange(128)[:, None]
    i = np.arange(128)[None, :]
    mask = np.concatenate([(j > i), (i >= j)], axis=1).astype(np.float32)
    ident = np.eye(128, dtype=np.float32)
    return cosT, sinT, mask, ident


_CACHE = {}


def kernel(x, norm_g, w_in, conv_dw_w, conv_dw_b, conv_ln_g, conv_ln_b, w_conv_out, attn_sinks, w_attn_out,
           w_out, final_norm_g):
    f = lambda a: np.ascontiguousarray(np.asarray(a, dtype=np.float32))
    x = f(x)
    cosT, sinT, mask, ident = _consts()
    col = lambda v: f(np.asarray(v, np.float32).reshape(8, 128).T)
    colp = np.concatenate([col(conv_dw_b[0]), col(conv_ln_g[0]), col(conv_ln_b[0])], axis=1)
    wdw = f(np.asarray(conv_dw_w[0], np.float32).reshape(31, 8, 128).transpose(2, 1, 0).reshape(128, 8 * 31))
    shared = {
        "w_in": f(w_in[0]), "w_co": f(w_conv_out[0]), "w_ao": f(w_attn_out[0]), "w_out": f(w_out[0]),
        "g_tab": f(np.broadcast_to(np.asarray(norm_g[0], np.float32)[None, :], (128, D))),
        "fg_tab": f(np.broadcast_to(np.asarray(final_norm_g, np.float32)[None, :], (128, D))),
        "colp": f(colp), "wdw": wdw,
        "sinks": f(np.broadcast_to(np.asarray(attn_sinks[0], np.float32)[None, :], (128, 16))),
        "cosT": cosT, "sinT": sinT, "maskT": mask, "ident": ident,
        "rperm": np.ascontiguousarray(ident[:, np.arange(128) ^ 32]),
    }
    if "nc" not in _CACHE:
        _CACHE["nc"] = build_program()
    nc = _CACHE["nc"]
    in_maps = []
    for b in range(NCORES):
        m = dict(shared)
        m["x"] = np.ascontiguousarray(x[b])
        in_maps.append(m)
    res = run_bass_kernel_spmd(nc, in_maps, core_ids=list(range(NCORES)))
    out = np.stack([np.asarray(res.results[b]["y"], dtype=np.float32).reshape(S, D) for b in range(NCORES)], axis=0)
    return out
```

```python
import numpy as np
from contextlib import ExitStack
import concourse.bass as bass
import concourse.mybir as mybir
from concourse.bass_utils import run_bass_kernel_spmd

F32 = mybir.dt.float32
BF16 = mybir.dt.bfloat16
ALU = mybir.AluOpType
AF = mybir.ActivationFunctionType

ENGS = ("pe", "act", "dve", "pool", "sp")

D = 1024
S = 4096
NCORES = 8
TT = 512
NT = S // TT
KC = 8
INW = 7680
NSLOT = 5
WIDTH = {"glu": 2048, "diag": 0, "k": 1024, "v": 2048, "q": 2048}
NDVE = 7
SLOTW = 4096
EPS = 1e-5


class T:
    __slots__ = ("name", "w", "rs", "ps")

    def __init__(self, name, ps=False):
        self.name = name
        self.w = None
        self.rs = []
        self.ps = ps


class Op:
    __slots__ = ("eng", "fn", "deps", "sig", "val", "sem", "isdma", "pos", "tag")


class Prog:
    def __init__(self):
        self.ops = {e: [] for e in ENGS}
        self.n = 0
        self.dma_keys = []
        self.tag = ""

    def add(self, eng, fn, r=(), w=(), dma=None):
        op = Op()
        op.tag = self.tag
        op.eng = eng
        op.fn = fn
        op.sig = False
        op.val = 0
        op.sem = dma
        op.isdma = dma is not None
        if dma is not None and dma not in self.dma_keys:
            self.dma_keys.append(dma)
        op.pos = self.n
        self.n += 1
        deps = []
        for t in r:
            if t.w is not None:
                deps.append(t.w)
            if t.ps:
                deps.extend(o for o in t.rs if o.eng != eng)
        for t in w:
            if t.w is not None:
                deps.append(t.w)
            deps.extend(t.rs)
        keep = []
        seen = set()
        latest = {}
        for d in deps:
            if id(d) in seen or d is op:
                continue
            seen.add(id(d))
            if d.isdma:
                keep.append(d)
                continue
            if d.eng == eng and eng == "pe":
                continue
            if d.eng not in latest or latest[d.eng].pos < d.pos:
                latest[d.eng] = d
        keep.extend(latest.values())
        for d in keep:
            d.sig = True
        op.deps = keep
        for t in r:
            t.rs.append(op)
        for t in w:
            t.w = op
            t.rs = []
        self.ops[eng].append(op)
        return op

    def emit_all(self, block, sems, dma_sems):
        for e in ENGS:
            cnt = 0
            for op in self.ops[e]:
                if op.isdma:
                    continue
                if op.sig:
                    cnt += 1
                    op.val = cnt
        dcnt = {}
        for op in sorted([o for e in ENGS for o in self.ops[e] if o.isdma], key=lambda o: o.pos):
            dcnt[op.sem] = dcnt.get(op.sem, 0) + 16
            op.val = dcnt[op.sem]

        def run(e, eng):
            known = {}
            for op in self.ops[e]:
                need = {}
                for d in op.deps:
                    if d.isdma:
                        key = ("d", d.sem)
                        s = dma_sems[d.sem]
                    else:
                        key = ("c", d.eng)
                        s = sems[d.eng]
                    if need.get(key, (None, 0))[1] < d.val:
                        need[key] = (s, d.val)
                for key, (s, v) in need.items():
                    if known.get(key, 0) >= v:
                        continue
                    eng.wait_ge(s, v)
                    known[key] = v
                if op.fn is None:
                    continue
                inst = op.fn(eng)
                if op.isdma:
                    inst.then_inc(dma_sems[op.sem], 16)
                elif op.sig:
                    inst.then_inc(sems[e], 1)

        @block.tensor
        def _(eng):
            run("pe", eng)

        @block.scalar
        def _(eng):
            run("act", eng)

        @block.vector
        def _(eng):
            run("dve", eng)

        @block.gpsimd
        def _(eng):
            run("pool", eng)

        @block.sync
        def _(eng):
            run("sp", eng)
            for key, v in dcnt.items():
                eng.wait_ge(dma_sems[key], v)


class Ring:
    def __init__(self, aps, name):
        self.aps = aps
        self.ts = [T("%s%d" % (name, i)) for i in range(len(aps))]
        self.i = 0

    def get(self):
        k = self.i % len(self.aps)
        self.i += 1
        return self.aps[k], self.ts[k]


def build_nc(order=None):
    discover = order is None
    nc = bass.Bass("TRN2", target_bir_lowering=False)
    dt_in = lambda name, shape: nc.dram_tensor(name, shape, F32, kind="ExternalInput").ap()
    x = dt_in("x", [S, D])
    w_in = dt_in("w_in", [D, INW])
    w_co = dt_in("w_co", [D, D])
    w_ao = dt_in("w_ao", [D, D])
    w_out = dt_in("w_out", [D, D])
    g_tab_d = dt_in("g_tab", [128, D])
    fg_tab_d = dt_in("fg_tab", [128, D])
    colp_d = dt_in("colp", [128, 24])
    wdw_d = dt_in("wdw", [128, 8 * 31])
    sinks_d = dt_in("sinks", [128, 16])
    cos_d = dt_in("cosT", [128, S])
    sin_d = dt_in("sinT", [128, S])
    mask_d = dt_in("maskT", [128, 256])
    ident_d = dt_in("ident", [128, 128])
    rperm_d = dt_in("rperm", [128, 128])
    y = nc.dram_tensor("y", [S, D], F32, kind="ExternalOutput").ap()
    NB = 37 if discover else len(order)
    wscr = nc.dram_tensor("wscr", [NB, 128, SLOTW], BF16, kind="Internal").ap()
    disc = []

    w_in_v = w_in.rearrange("(k p) e -> p k e", p=128)
    w_co_v = w_co.rearrange("(k p) e -> p k e", p=128)
    w_ao_v = w_ao.rearrange("(k p) e -> p k e", p=128)
    w_out_v = w_out.rearrange("(k p) e -> p k e", p=128)

    P = Prog()
    with ExitStack() as es:
        def sb(name, shape, dt):
            return es.enter_context(nc.sbuf_tensor("sb_" + name, shape, dt))

        def ps(name, shape, dt):
            return es.enter_context(nc.psum_tensor("ps_" + name, shape, dt))

        slots = [sb("wslot%d" % i, [128, SLOTW], BF16) for i in range(NSLOT)]
        slot_parts = [[T("ws%d_%d" % (i, j)) for j in range(4)] for i in range(NSLOT)]
        xs = [sb("xs%d" % i, [128, D], F32) for i in range(8)]
        t_xs = [T("xs%d" % i) for i in range(8)]
        hb = Ring([sb("hb%d" % i, [128, D], BF16) for i in range(4)], "hb")
        hT = sb("hT", [128, KC * TT], BF16)
        t_hT = [T("hT%d" % i) for i in range(4)]
        tmpf = Ring([sb("tmpf%d" % i, [128, TT], F32) for i in range(5)], "tmpf")
        tmpb = Ring([sb("tmpb%d" % i, [128, TT], BF16) for i in range(3)], "tmpb")
        h1 = [sb("h1_%d" % c, [128, 30 + TT], BF16) for c in range(8)]
        t_h1h = [T("h1h%d" % c) for c in range(8)]
        t_h1c = [T("h1c%d" % c) for c in range(8)]
        Vsb = sb("Vsb", [128, 8 * TT], BF16)
        t_V = [T("V%d" % c) for c in range(8)]
        rstdT = sb("rstdT", [128, TT], F32)
        nmr = sb("nmr", [128, TT], F32)
        t_rstdT, t_nmr = T("rstdT"), T("nmr")
        cgT = sb("cgT", [128, 8 * TT], BF16)
        t_cgT = [T("cgT%d" % c) for c in range(8)]
        m1 = sb("m1", [128, 8 * TT], BF16)
        t_m1 = [T("m1_%d" % c) for c in range(8)]
        cosb = Ring([sb("cosb%d" % i, [128, TT], F32) for i in range(1)], "cosb")
        sinb = Ring([sb("sinb%d" % i, [128, TT], F32) for i in range(1)], "sinb")
        qT, t_qT = cgT, t_cgT
        kT = sb("kT", [128, 4 * 640], BF16)
        t_kprev = T("kprev")
        t_kcur = [T("kcur%d" % c) for c in range(4)]
        Vaug = sb("Vaug", [128, 5 * 4 * 65], BF16)
        t_vprev = T("vprev")
        t_vcur = [T("vcur%d" % c) for c in range(4)]
        sgaT, t_sgaT = Vsb, t_V
        Pb = Ring([sb("Pb%d" % i, [128, 1024], BF16) for i in range(2)], "Pb")
        small = Ring([sb("small%d" % i, [128, 8], F32) for i in range(3)], "small")
        a_tm = [sb("a_tm%d" % i, [128, D], BF16) for i in range(4)]
        t_atm = [[T("atm%d_%d" % (i, h)) for h in range(4)] for i in range(4)]
        agT = sb("agT", [128, 8 * TT], BF16)
        t_ag = [[T("ag%d_%d" % (c, b)) for b in range(4)] for c in range(8)]
        mT = sb("mT", [128, 8 * TT], BF16)
        t_mT = [T("mT%d" % c) for c in range(8)]
        xo = Ring([sb("xo%d" % i, [128, D], F32) for i in range(2)], "xo")
        ssq = sb("ssq", [128, 16], F32)
        t_ssq = T("ssq")
        t_ssq2 = T("ssq2")
        g_tab = sb("g_tab", [128, D], F32)
        fg_tab = sb("fg_tab", [128, D], F32)
        colp = sb("colp", [128, 24], F32)
        wdw = sb("wdw", [128, 8 * 31], F32)
        esink = sb("esink", [128, 16], F32)
        maskb = sb("maskb", [128, 256], BF16)
        ident = sb("identb", [128, 128], BF16)
        ones = sb("ones", [128, 128], BF16)
        rpm = sb("rpm", [128, 128], BF16)
        t_rpm = T("rpm")
        qhl = Ring([sb("qhl%d" % i, [128, TT], BF16) for i in range(6)], "qhl")
        accf = Ring([sb("accf%d" % i, [128, TT], F32) for i in range(2)], "accf")
        t_const = T("const")
        t_esink = T("esink")
        dummy = sb("dummy", [128, 8], F32)
        t_dummy = T("dummy")
        t_warm = T("warm")

        def warm(func):
            P.add("act", lambda e: e.activation(out=dummy[:, 4:5], in_=ones[:, 0:1], func=func), r=[t_c7], w=[t_warm])

        pd = [ps("pd%d" % i, [128, 1024], F32) for i in range(3)]
        p6 = ps("p6", [128, 512], F32)
        ptr = ps("ptr", [128, 1024], BF16)
        ptrf = ptr[:, :].bitcast(F32)
        p6b = p6[:, :].bitcast(BF16)
        trs = [(ptr[:, :], None), (p6b, None)]
        tr_i = [0]
        t_pd = [[T("pd%d_%d" % (i, h), ps=True) for h in range(2)] for i in range(3)]
        t_p6 = T("p6", ps=True)
        t_ptr = T("ptr", ps=True)
        bank_all = [(pd[0][:, 0:512], t_pd[0][0]), (pd[1][:, 0:512], t_pd[1][0]), (pd[2][:, 0:512], t_pd[2][0]),
                    (p6[:, :], t_p6),
                    (pd[0][:, 512:1024], t_pd[0][1]), (pd[1][:, 512:1024], t_pd[1][1]), (pd[2][:, 512:1024], t_pd[2][1])]
        bank_job = [(p6[:, :], t_p6), (ptrf, t_ptr)]
        trs = [(ptr[:, :], t_ptr), (p6b, t_p6)]

        def tr_bank():
            k = tr_i[0] % 2
            tr_i[0] += 1
            return trs[k]
        bank_O = [(pd[2][:, 0:512], t_pd[2][0]), (pd[2][:, 512:1024], t_pd[2][1])]
        ring_state = {"all": 0, "job": 0, "O": 0, "S": 0}
        bank_mode = ["all"]

        def gen_bank():
            lst = bank_all if bank_mode[0] == "all" else bank_job
            k = ring_state[bank_mode[0]] % len(lst)
            ring_state[bank_mode[0]] += 1
            return lst[k]

        def o_bank():
            k = ring_state["O"] % 2
            ring_state["O"] += 1
            return bank_O[k]

        def s_double():
            k = ring_state["S"] % 2
            ring_state["S"] += 1
            return pd[k], t_pd[k]

        sems = {e: es.enter_context(nc.semaphore("s_" + e)) for e in ENGS}
        block = es.enter_context(nc.Block())

        def ld(eng, out_ap, in_ap, key, w):
            return P.add(eng, lambda e: e.dma_start(out=out_ap, in_=in_ap), w=w, dma=key)

        ld("sp", g_tab[:, :], g_tab_d[:, :], "c0", [t_const])
        t_c1, t_c2, t_c3, t_c4, t_c5, t_c6, t_c7 = [T("c%d" % i) for i in range(1, 8)]
        ld("sp", fg_tab[:, :], fg_tab_d[:, :], "c1", [t_c1])
        ld("sp", colp[:, :], colp_d[:, :], "c2", [t_c2])
        ld("sp", wdw[:, :], wdw_d[:, :], "c3", [t_c3])
        ld("sp", esink[:, :], sinks_d[:, :], "c4", [t_c4])
        ld("pool", maskb[:, :], mask_d[:, :], "c5", [t_c5])
        ld("pool", ident[:, :], ident_d[:, :], "c6", [t_c6])
        ld("pool", rpm[:, :], rperm_d[:, :], "c8", [t_rpm])
        P.add("pool", lambda e: e.memset(ones[:, :], 1.0), w=[t_c7])
        P.add("act", lambda e: e.activation(out=esink[:, :], in_=esink[:, :], func=AF.Exp), r=[t_c4], w=[t_esink])
        P.add("pool", lambda e: e.memset(Vaug[:, :], 1.0), w=[t_vprev] + t_vcur)
        for c in range(8):
            P.add("pool", lambda e, c=c: e.memset(h1[c][:, 0:30], 0.0), w=[t_h1h[c]])
        P.add("pool", lambda e: e.memset(kT[:, :], 0.0), w=[t_kprev] + t_kcur)

        def slot3(si, n):
            return slots[si][:, 0:8 * n].rearrange("p (k n) -> p k n", k=8)

        def bwidth(kind):
            if kind == "diag":
                return (31 - NDVE) * 128
            return WIDTH.get(kind, SLOTW)

        def prep_block(kind, j, si, bi):
            sl = slots[si]
            tp = slot_parts[si]
            key = "wp%d" % si
            if kind == "glu":
                v = slot3(si, 256)
                ld("pool", v[:, :, 0:128], w_in_v[:, :, j * 128:(j + 1) * 128], key, [tp[0]])
                ld("pool", v[:, :, 128:256], w_in_v[:, :, 1024 + j * 128:1024 + (j + 1) * 128], key, [tp[1]])
            elif kind == "diag":
                v = sl[:, 0:31 * 128].rearrange("p (k n) -> p k n", k=31)
                NPE_ = 31 - NDVE
                for k in range(NPE_):
                    P.add("pool", lambda e, k=k, v=v: e.tensor_tensor(
                        out=v[:, k, :], in0=ident[:, :],
                        in1=wdw[:, j * 31 + k:j * 31 + k + 1].to_broadcast([128, 128]), op=ALU.mult),
                          r=[t_c3, t_c6], w=[tp[0]] if k == 0 else [tp[1]] if k == NPE_ - 1 else [])
            elif kind in ("cgate", "agate", "wout", "gconv", "gattn", "wco", "wao"):
                v = slot3(si, 512)
                src = {"cgate": lambda: w_in_v[:, :, 2048 + j * 512:2048 + (j + 1) * 512],
                       "agate": lambda: w_in_v[:, :, 4608 + j * 512:4608 + (j + 1) * 512],
                       "gconv": lambda: w_in_v[:, :, 5632 + j * 512:5632 + (j + 1) * 512],
                       "gattn": lambda: w_in_v[:, :, 6656 + j * 512:6656 + (j + 1) * 512],
                       "wout": lambda: w_out_v[:, :, j * 512:(j + 1) * 512],
                       "wco": lambda: w_co_v[:, :, j * 512:(j + 1) * 512],
                       "wao": lambda: w_ao_v[:, :, j * 512:(j + 1) * 512]}[kind]()
                ld("pool", v[:, :, 0:256], src[:, :, 0:256], key, [tp[0]])
                ld("pool", v[:, :, 256:512], src[:, :, 256:512], key, [tp[1]])
            elif kind == "v":
                v = slot3(si, 256)
                ld("pool", v[:, :, :], w_in_v[:, :, 4352:4608], key, [tp[0]])
            elif kind == "q":
                v = slot3(si, 256)
                for ci in range(2):
                    c0 = 3072 + (2 * j + ci) * 128
                    ld("pool", v[:, :, ci * 128:(ci + 1) * 128], w_in_v[:, :, c0:c0 + 128], key, [tp[ci]])
            elif kind == "k":
                v = slot3(si, 128)
                ld("pool", v[:, :, 0:64], w_in_v[:, :, 4096 + j * 64:4096 + (j + 1) * 64], key, [tp[0]])
                ld("pool", v[:, :, 64:128], w_in_v[:, :, 4096 + j * 64:4096 + (j + 1) * 64], key, [tp[1]])
            else:
                raise ValueError(kind)
            t_s = T("scr%d" % bi)
            W = bwidth(kind)
            P.add("sp", lambda e, sl=sl, bi=bi, W=W: e.dma_start(out=wscr[bi, :, 0:W], in_=sl[:, 0:W]),
                  r=list(tp), w=[t_s], dma="wst%d" % si)
            return t_s

        t_scr = {}
        st = {"next_load": 0, "next_use": 0}
        loaded = {}
        total_stream = NT * NB

        def issue_load():
            n = st["next_load"]
            st["next_load"] += 1
            it, bi = divmod(n, NB)
            si = n % NSLOT
            tp = slot_parts[si]
            if it == 0:
                kind, j = order[bi]
                P.add("pool", lambda e: e.memset(dummy[:, 0:1], 0.0), w=list(tp) + [t_dummy])
                t_scr[bi] = prep_block(kind, j, si, bi)
            else:
                W = bwidth(order[bi][0])
                P.add("sp", lambda e, si=si, bi=bi, W=W: e.dma_start(out=slots[si][:, 0:W], in_=wscr[bi, :, 0:W]),
                      r=[t_scr[bi]], w=list(tp), dma="w%d" % si)
            loaded[n] = si

        def wnext(kind, j):
            if discover:
                disc.append((kind, j))
                return 0, slot_parts[0]
            n = st["next_use"]
            st["next_use"] += 1
            while st["next_load"] < min(total_stream, n + NSLOT - 1) or n not in loaded:
                issue_load()
            it, bi = divmod(n, NB)
            assert order[bi] == (kind, j), (order[bi], kind, j)
            si = loaded[n]
            return si, slot_parts[si]

        def mm(out, lhsT, rhs, start, stop, r, w):
            P.add("pe", lambda e: e.matmul(out, lhsT=lhsT, rhs=rhs, start=start, stop=stop), r=r, w=w)

        hT3 = hT[:, :].rearrange("p (k n) -> p k n", k=KC)

        def proj_fm(si, tp, col0, bank, t_bank, ncols=512):
            v = slot3(si, ncols)
            for kc in range(KC):
                mm(bank, v[:, kc, col0:col0 + 128], hT3[:, kc, :], kc == 0, kc == KC - 1,
                   r=list(tp) + t_hT, w=[t_bank])

        hbs = {}

        def stage_N_load(it):
            P.tag = "N"
            par = it % 2
            for b in range(4):
                xt, t_x = xs[par * 4 + b], t_xs[par * 4 + b]
                row0 = it * TT + b * 128
                P.add("sp", lambda e, xt=xt, row0=row0: e.dma_start(out=xt[:, :], in_=x[row0:row0 + 128, :]),
                      w=[t_x], dma="x%d" % (par * 4 + b))

        def stage_N_pre(it):
            P.tag = "N"
            par = it % 2
            for b in range(4):
                xt, t_x = xs[par * 4 + b], t_xs[par * 4 + b]
                hbt, t_hb = hb.get()
                hbs[(it, b)] = (hbt, t_hb)
                P.add("act", lambda e, xt=xt, b=b, hbt=hbt: e.activation(out=hbt[:, :], in_=xt[:, :], func=AF.Square,
                                                                          accum_out=ssq[:, b:b + 1]),
                      r=[t_x], w=[t_hb, t_ssq])
            P.add("act", lambda e: e.activation(out=ssq[:, 4:8], in_=ssq[:, 0:4], func=AF.Ln, scale=1.0 / D, bias=EPS),
                  r=[t_ssq], w=[t_ssq])
            P.add("act", lambda e: e.activation(out=ssq[:, 4:8], in_=ssq[:, 4:8], func=AF.Exp, scale=-0.5),
                  r=[t_ssq], w=[t_ssq])
            for b in range(4):
                xt, t_x = xs[par * 4 + b], t_xs[par * 4 + b]
                hbt, t_hb = hbs[(it, b)]
                P.add("dve", lambda e, xt=xt, hbt=hbt, b=b: e.scalar_tensor_tensor(
                    out=hbt[:, :], in0=xt[:, :], scalar=ssq[:, 4 + b:5 + b], in1=g_tab[:, :], op0=ALU.mult, op1=ALU.mult),
                    r=[t_x, t_ssq, t_const], w=[t_hb])

        def stage_N_pe(it):
            P.tag = "N"
            for b in range(4):
                hbt, t_hb = hbs.pop((it, b))
                trb, t_trb = tr_bank()
                for kc in range(KC):
                    P.add("pe", lambda e, hbt=hbt, kc=kc, trb=trb: e.transpose(out=trb[:, kc * 128:(kc + 1) * 128],
                                                                                in_=hbt[:, kc * 128:(kc + 1) * 128],
                                                                                identity=ident[:, :]),
                          r=[t_hb, t_c6], w=[t_trb])
                P.add("act", lambda e, b=b, trb=trb: e.activation(out=hT3[:, :, b * 128:(b + 1) * 128],
                                                                   in_=trb.rearrange("p (k n) -> p k n", k=KC), func=AF.Copy),
                      r=[t_trb], w=[t_hT[b]])

        kT3 = kT[:, :].rearrange("p (h n) -> p h n", h=4)
        rope_tabs = {}
        kstate = {}
        bank_hook = [None]

        def rope_p1(si, tps, col0, bank_fn, cs, t_cs, ncols=512):
            bQ, t_bQ = bank_fn()
            proj_fm(si, tps, col0, bQ, t_bQ, ncols)
            qh, t_qh = qhl.get()
            ql, t_ql = qhl.get()
            a, t_a = tmpf.get()
            P.add("act", lambda e: e.activation(out=qh[:, :], in_=bQ, func=AF.Copy), r=[t_bQ], w=[t_qh])
            P.add("dve", lambda e: e.tensor_tensor(out=ql[:, :], in0=bQ, in1=qh[:, :], op=ALU.subtract),
                  r=[t_bQ, t_qh], w=[t_ql])
            P.add("dve", lambda e: e.tensor_tensor(out=a[:, :], in0=bQ, in1=cs[:, :], op=ALU.mult),
                  r=[t_bQ, t_cs], w=[t_a])
            return (qh, t_qh, ql, t_ql, a, t_a)

        def rope_p2(state, bank_fn, sn, t_sn, out_ap, t_out, defer_add=False):
            qh, t_qh, ql, t_ql, a, t_a = state
            bR, t_bR = bank_fn()
            mm(bR, rpm[:, :], qh[:, :], True, False, r=[t_rpm, t_qh], w=[t_bR])
            mm(bR, rpm[:, :], ql[:, :], False, True, r=[t_rpm, t_ql], w=[t_bR])
            b, t_b = tmpf.get()
            P.add("dve", lambda e: e.tensor_tensor(out=b[:, :], in0=bR, in1=sn[:, :], op=ALU.mult),
                  r=[t_bR, t_sn], w=[t_b])
            def final_add():
                P.add("dve", lambda e: e.tensor_tensor(out=out_ap, in0=a[:, :], in1=b[:, :], op=ALU.add),
                      r=[t_a, t_b], w=[t_out])
            if defer_add:
                return final_add
            final_add()

        def kprep(it):
            P.tag = "B.k"
            cs, t_cs = cosb.get()
            sn, t_sn = sinb.get()
            P.add("sp", lambda e: e.dma_start(out=cs[:, :], in_=cos_d[:, it * TT:(it + 1) * TT]), w=[t_cs], dma="cs0")
            P.add("sp", lambda e: e.dma_start(out=sn[:, :], in_=sin_d[:, it * TT:(it + 1) * TT]), w=[t_sn], dma="sn0")
            rope_tabs[it] = (cs, t_cs, sn, t_sn)

        kpend = {}

        def kjob_p1(it, hk):
            P.tag = "B.k"
            cs, t_cs, sn, t_sn = rope_tabs[it]
            si, tp = wnext("k", hk)
            kpend[(it, hk)] = rope_p1(si, (tp[0], tp[1]), 0, bank_hook[0], cs, t_cs, ncols=128)

        def kjob_p2(it, hk):
            P.tag = "B.k"
            cs, t_cs, sn, t_sn = rope_tabs[it]
            rope_p2(kpend.pop((it, hk)), bank_hook[0], sn, t_sn, kT3[:, hk, 128:640], t_kcur[hk])

        def stage_A(it):
            bank_mode[0] = "all"
            S1, t_S1 = bank_all[5]
            S2, t_S2 = bank_all[6]
            ring5 = [bank_all[i] for i in (0, 1, 2, 3, 4)]
            r5 = [0]

            def bank5():
                k = r5[0] % len(ring5)
                r5[0] += 1
                return ring5[k]

            bank_hook[0] = bank5

            P.tag = "A.cgate"
            for j in range(2):
                si, tp = wnext("cgate", j)
                for cc in range(4):
                    c = j * 4 + cc
                    bG, t_bG = bank5()
                    proj_fm(si, tp[0:2], cc * 128, bG, t_bG)
                    P.add("act", lambda e, bG=bG, c=c: e.activation(out=mT[:, c * TT:(c + 1) * TT], in_=bG, func=AF.Silu),
                          r=[t_bG], w=[t_mT[c]])

            def glu(c):
                P.tag = "A.glu"
                si, tp = wnext("glu", c)
                v = slot3(si, 256)
                bA, t_A = bank5()
                bB, t_B = bank5()
                for kc in range(KC):
                    mm(bA, v[:, kc, 0:128], hT3[:, kc, :], kc == 0, kc == KC - 1, r=list(tp[0:2]) + t_hT, w=[t_A])
                for kc in range(KC):
                    mm(bB, v[:, kc, 128:256], hT3[:, kc, :], kc == 0, kc == KC - 1, r=list(tp[0:2]) + t_hT, w=[t_B])
                sg, t_sg = tmpf.get()
                P.add("act", lambda e: e.activation(out=sg[:, :], in_=bB, func=AF.Sigmoid), r=[t_B], w=[t_sg])
                P.add("dve", lambda e: e.tensor_tensor(out=h1[c][:, 30:30 + TT], in0=bA, in1=sg[:, :], op=ALU.mult),
                      r=[t_A, t_sg], w=[t_h1c[c]])

            pend = []

            def stats(c, v2, t_v2):
                P.tag = "A.conv"
                mm(S1, ones[:, :], Vsb[:, c * TT:(c + 1) * TT], c == 0, c == 7, r=[t_c7, t_V[c]], w=[t_S1])
                mm(S2, ones[:, :], v2[:, :], c == 0, c == 7, r=[t_c7, t_v2], w=[t_S2])

            def conv(c):
                P.tag = "A.conv"
                if c == 7:
                    warm(AF.Ln)
                si, tp = wnext("diag", c)
                dv = slots[si][:, 0:31 * 128].rearrange("p (k n) -> p k n", k=31)
                bV, t_bV = bank5()
                NPE = 31 - NDVE
                for k in range(NPE):
                    mm(bV, dv[:, k, :], h1[c][:, k:k + TT], k == 0, k == NPE - 1, r=list(tp[0:2]) + [t_h1c[c], t_h1h[c]],
                       w=[t_bV])
                acc, t_acc = accf.get()
                for k in range(NPE, 31):
                    wk = wdw[:, c * 31 + k:c * 31 + k + 1]
                    if k == NPE:
                        P.add("dve", lambda e, k=k, wk=wk: e.tensor_scalar(out=acc[:, :], in0=h1[c][:, k:k + TT], scalar1=wk,
                                                                           scalar2=None, op0=ALU.mult),
                              r=[t_h1c[c], t_h1h[c], t_c3], w=[t_acc])
                    else:
                        P.add("dve", lambda e, k=k, wk=wk: e.scalar_tensor_tensor(out=acc[:, :], in0=h1[c][:, k:k + TT], scalar=wk,
                                                                                  in1=acc[:, :], op0=ALU.mult, op1=ALU.add),
                              r=[t_h1c[c], t_h1h[c], t_c3, t_acc], w=[t_acc])
                P.add("dve", lambda e: e.scalar_tensor_tensor(out=Vsb[:, c * TT:(c + 1) * TT], in0=bV, scalar=colp[:, c:c + 1],
                                                              in1=acc[:, :], op0=ALU.add, op1=ALU.add),
                      r=[t_bV, t_c2, t_acc], w=[t_V[c]])
                v2, t_v2 = tmpb.get()
                P.add("act", lambda e: e.activation(out=v2[:, :], in_=Vsb[:, c * TT:(c + 1) * TT], func=AF.Square),
                      r=[t_V[c]], w=[t_v2])
                P.add("pool", lambda e: e.tensor_copy(out=h1[c][:, 0:30], in_=h1[c][:, TT:TT + 30]),
                      r=[t_h1c[c]], w=[t_h1h[c]])
                pend.append((c, v2, t_v2))
                if len(pend) > 1:
                    stats(*pend.pop(0))

            kprep(it)
            glu(0)
            for c in range(8):
                if c + 1 < 8:
                    glu(c + 1)
                conv(c)
                if c % 2 == 0:
                    kjob_p1(it, c // 2)
                else:
                    kjob_p2(it, c // 2)
            stats(*pend.pop(0))

            gjobs = []
            gstate = {}

            def gjob(c):
                P.tag = "A.gconv"
                if c == 2:
                    ring5.extend([bank_all[5], bank_all[6]])
                    r5[0] = 0
                j, cc = divmod(c, 4)
                if j not in gstate:
                    gstate[j] = wnext("gconv", j)
                si, tp = gstate[j]
                bG, t_bG = bank5()
                proj_fm(si, tp[0:2], cc * 128, bG, t_bG)
                P.add("act", lambda e: e.activation(out=agT[:, c * TT:(c + 1) * TT], in_=bG, func=AF.Tanh, scale=0.5),
                      r=[t_bG], w=t_ag[c])

            P.tag = "A.ln"
            mean, t_mean = tmpf.get()
            msq, t_msq = tmpf.get()
            var, t_var = tmpf.get()
            P.add("dve", lambda e: e.tensor_scalar(out=mean[:, :], in0=S1, scalar1=1.0 / D, scalar2=None, op0=ALU.mult),
                  r=[t_S1], w=[t_mean])
            P.add("dve", lambda e: e.tensor_tensor(out=msq[:, :], in0=mean[:, :], in1=mean[:, :], op=ALU.mult),
                  r=[t_mean], w=[t_msq])
            P.add("dve", lambda e: e.scalar_tensor_tensor(out=var[:, :], in0=S2, scalar=1.0 / D, in1=msq[:, :],
                                                           op0=ALU.mult, op1=ALU.subtract),
                  r=[t_S2, t_msq], w=[t_var])
            P.add("act", lambda e: e.activation(out=var[:, :], in_=var[:, :], func=AF.Ln, bias=EPS), r=[t_var], w=[t_var])
            P.add("act", lambda e: e.activation(out=rstdT[:, :], in_=var[:, :], func=AF.Exp, scale=-0.5),
                  r=[t_var], w=[t_rstdT])
            P.add("dve", lambda e: e.scalar_tensor_tensor(out=nmr[:, :], in0=mean[:, :], scalar=-1.0, in1=rstdT[:, :],
                                                           op0=ALU.mult, op1=ALU.mult),
                  r=[t_mean, t_rstdT], w=[t_nmr])

            prev = None
            for c in range(8):
                gjob(c)
                P.tag = "A.norm"
                z, t_z = tmpf.get()
                P.add("dve", lambda e, z=z, c=c: e.tensor_tensor(out=z[:, :], in0=Vsb[:, c * TT:(c + 1) * TT], in1=rstdT[:, :],
                                                                  op=ALU.mult),
                      r=[t_V[c], t_rstdT], w=[t_z])
                P.add("dve", lambda e, z=z: e.tensor_tensor(out=z[:, :], in0=z[:, :], in1=nmr[:, :], op=ALU.add),
                      r=[t_z, t_nmr], w=[t_z])
                ca, t_ca = tmpb.get()
                P.add("act", lambda e, z=z, ca=ca, c=c: e.activation(out=ca[:, :], in_=z[:, :], func=AF.Silu,
                                                                      scale=colp[:, 8 + c:9 + c], bias=colp[:, 16 + c:17 + c]),
                      r=[t_z, t_c2], w=[t_ca])

                def cgmul(ca=ca, t_ca=t_ca, c=c):
                    P.add("dve", lambda e: e.tensor_tensor(out=cgT[:, c * TT:(c + 1) * TT], in0=ca[:, :],
                                                           in1=mT[:, c * TT:(c + 1) * TT], op=ALU.mult),
                          r=[t_ca, t_mT[c]], w=[t_cgT[c]])
                if prev is not None:
                    prev()
                prev = cgmul
            prev()
            stage_B0(it)

            P.tag = "A.wco"
            cg3 = cgT[:, :].rearrange("p (k n) -> p k n", k=8)
            for j in range(2):
                si, tp = wnext("wco", j)
                v = slot3(si, 512)
                for dd in range(4):
                    dch = j * 4 + dd
                    bY, t_bY = bank5()
                    for kc in range(KC):
                        mm(bY, v[:, kc, dd * 128:(dd + 1) * 128], cg3[:, kc, :], kc == 0, kc == KC - 1,
                           r=list(tp[0:2]) + t_cgT, w=[t_bY])
                    P.add("dve", lambda e, bY=bY, dch=dch: e.scalar_tensor_tensor(
                        out=m1[:, dch * TT:(dch + 1) * TT], in0=agT[:, dch * TT:(dch + 1) * TT], scalar=1.0, in1=bY,
                        op0=ALU.add, op1=ALU.mult),
                          r=[t_bY] + t_ag[dch], w=[t_m1[dch]])

        Va4 = Vaug[:, :].rearrange("p (b h d) -> p b h d", b=5, h=4)
        mask3 = maskb[:, :].rearrange("p (k n) -> p k n", k=2)
        q3 = qT[:, :].rearrange("p (c n) -> p c n", c=8)
        ag3 = agT[:, :].rearrange("p (c n) -> p c n", c=8)
        sga3 = sgaT[:, :].rearrange("p (c n) -> p c n", c=8)

        qpend = {}

        def q_p1(it, si, tp, ci, c):
            P.tag = "B.q"
            cs, t_cs, sn, t_sn = rope_tabs[it]
            qpend[(it, c)] = rope_p1(si, (tp[0], tp[1]), ci * 128, gen_bank, cs, t_cs, ncols=256)

        def q_p2(it, c, defer_add=False):
            P.tag = "B.q"
            cs, t_cs, sn, t_sn = rope_tabs[it]
            return rope_p2(qpend.pop((it, c)), gen_bank, sn, t_sn, qT[:, c * TT:(c + 1) * TT], t_qT[c], defer_add)

        def stage_B0(it):
            bank_mode[0] = "all"
            warm(AF.Exp)
            si, tp = wnext("q", 0)
            q_p1(it, si, tp, 0, 0)
            q_p1(it, si, tp, 1, 1)
            P.tag = "B.v"
            si, tp = wnext("v", 0)
            vv = slot3(si, 256)
            for b in range(4):
                bV, t_bV = gen_bank()
                for kc in range(KC):
                    mm(bV[:, 0:256], hT3[:, kc, b * 128:(b + 1) * 128], vv[:, kc, :], kc == 0, kc == KC - 1,
                       r=[tp[0], t_hT[b]], w=[t_bV])
                P.add("act", lambda e, bV=bV, b=b: e.activation(out=Va4[:, b + 1, :, 0:64],
                                                                 in_=bV[:, 0:256].rearrange("p (h d) -> p h d", h=4),
                                                                 func=AF.Copy),
                      r=[t_bV], w=[t_vcur[b]])
            q0_adds[it] = [q_p2(it, 0, defer_add=True), q_p2(it, 1, defer_add=True)]

        q0_adds = {}

        def stage_B(it):
            bank_mode[0] = "all"
            P.tag = "B.q"
            for f in q0_adds.pop(it):
                f()

            jobs = []

            def add_q_jobs(j):
                holder = {}

                def get():
                    if "s" not in holder:
                        holder["s"] = wnext("q", j)
                    return holder["s"]
                for ci in range(2):
                    jobs.append(lambda ci=ci, j=j: q_p1(it, *get(), ci, 2 * j + ci))
                    jobs.append(lambda ci=ci, j=j: q_p2(it, 2 * j + ci))

            def add_gate_jobs(kind, j):
                holder = {}

                def get():
                    if "s" not in holder:
                        holder["s"] = wnext(kind, j)
                    return holder["s"]

                def job(cc):
                    si, tp = get()
                    c = j * 4 + cc
                    P.tag = "B." + kind
                    bG, t_bG = gen_bank()
                    proj_fm(si, tp[0:2], cc * 128, bG, t_bG)
                    if kind == "agate":
                        th, t_th = tmpb.get()
                        P.add("act", lambda e: e.activation(out=th[:, :], in_=bG, func=AF.Tanh, scale=0.5),
                              r=[t_bG], w=[t_th])
                        P.add("dve", lambda e: e.scalar_tensor_tensor(out=sgaT[:, c * TT:(c + 1) * TT], in0=th[:, :], scalar=1.0,
                                                                      in1=bG, op0=ALU.add, op1=ALU.mult),
                              r=[t_bG, t_th], w=[t_sgaT[c]])
                    else:
                        P.add("act", lambda e: e.activation(out=mT[:, c * TT:(c + 1) * TT], in_=bG, func=AF.Tanh, scale=0.5),
                              r=[t_bG], w=[t_mT[c]])
                for cc in range(4):
                    jobs.append(lambda cc=cc: job(cc))

            add_q_jobs(1)
            add_gate_jobs("agate", 0)
            add_q_jobs(2)
            add_gate_jobs("agate", 1)
            add_q_jobs(3)
            add_gate_jobs("gattn", 0)
            add_gate_jobs("gattn", 1)
            sched = [[2, 1], [4, 3], [5, 6], [7, 8], [10, 9], [12, 11], [13, 14], [15, 16], [18, 17], [20, 19], [21, 22],
                     [23, 24], [25, 26], [27], [], []]

            iters = [(hk, n) for hk in range(4) for n in range(4)]
            state = {}

            def S_part(i):
                hk, n = iters[i]
                P.tag = "B.attn"
                first = (it == 0 and n == 0)
                Sd, tS = s_double()
                for r in range(2):
                    for kb in range(2):
                        if first and kb == 0:
                            continue
                        kcol0 = n * 128 + kb * 128
                        t_k = [t_kcur[hk]] + ([t_kprev] if (n == 0 and kb == 0) else [])
                        base = r * 512 + kb * 256
                        mm(Sd[:, base:base + 256], kT3[r * 64:(r + 1) * 64, hk, kcol0:kcol0 + 128],
                           q3[r * 64:(r + 1) * 64, 2 * hk:2 * hk + 2, n * 128:(n + 1) * 128], True, True,
                           r=t_k + [t_qT[2 * hk], t_qT[2 * hk + 1]], w=[tS[r]])
                pb, t_p = Pb.get()
                k0 = 1 if first else 0
                if first:
                    S3 = Sd[:, :].rearrange("p (r k x) -> p r k x", r=2, k=2)
                    P3 = pb[:, :].rearrange("p (r k x) -> p r k x", r=2, k=2)
                    P.add("act", lambda e: e.activation(out=P3[:, :, 1, :], in_=S3[:, :, 1, :], func=AF.Exp, scale=0.125),
                          r=list(tS), w=[t_p])
                else:
                    P.add("act", lambda e: e.activation(out=pb[:, :], in_=Sd[:, :], func=AF.Exp, scale=0.125),
                          r=list(tS), w=[t_p])
                for r in range(2):
                    Pr = pb[:, r * 512:(r + 1) * 512].rearrange("p (k a q) -> p k a q", k=2, a=2)
                    P.add("dve", lambda e, Pr=Pr: e.tensor_tensor(
                        out=Pr[:, k0:2, :, :], in0=Pr[:, k0:2, :, :],
                        in1=mask3[:, k0:2, :].unsqueeze(2).to_broadcast([128, 2 - k0, 2, 128]), op=ALU.mult),
                        r=[t_p, t_c5], w=[t_p])
                state[i] = (pb, t_p, first)

            def PV_part(i):
                hk, n = iters[i]
                P.tag = "B.attn"
                pb, t_p, first = state.pop(i)
                P5 = pb[:, :].rearrange("p (r k a q) -> p r k a q", r=2, a=2, k=2)
                Ob, t_Ob = o_bank()
                O3 = Ob[:, 0:260].rearrange("p (g d) -> p g d", g=4)
                for g in range(4):
                    r, pair = g % 2, g // 2
                    for kb in range(2):
                        if first and kb == 0:
                            continue
                        vb = n + kb
                        t_v = t_vprev if vb == 0 else t_vcur[vb - 1]
                        mm(O3[:, g, :], P5[:, r, kb, pair, :], Va4[:, vb, hk, :],
                           (kb == 0) or first, kb == 1, r=[t_p, t_v], w=[t_Ob])
                sm, t_sm = small.get()
                at3 = a_tm[n][:, :].rearrange("p (g d) -> p g d", g=16)
                P.add("dve", lambda e: e.tensor_tensor(out=sm[:, 0:4], in0=O3[:, :, 64], in1=esink[:, hk * 4:hk * 4 + 4],
                                                       op=ALU.add),
                      r=[t_Ob, t_esink], w=[t_sm])
                P.add("dve", lambda e: e.reciprocal(out=sm[:, 4:8], in_=sm[:, 0:4]), r=[t_sm], w=[t_sm])
                P.add("dve", lambda e: e.tensor_tensor(
                    out=at3[:, hk * 4:hk * 4 + 4, :], in0=O3[:, :, 0:64],
                    in1=sm[:, 4:8].unsqueeze(2).to_broadcast([128, 4, 64]), op=ALU.mult),
                    r=[t_Ob, t_sm], w=[t_atm[n][hk]])

            bank_mode[0] = "job"
            jobs[0]()
            S_part(0)
            for i in range(16):
                if i + 1 < 16:
                    S_part(i + 1)
                for jn in sched[i]:
                    jobs[jn]()
                PV_part(i)
            bank_mode[0] = "all"

        def stage_B2(it):
            bank_mode[0] = "all"
            P.add("pool", lambda e: e.tensor_copy(out=kT3[:, :, 0:128], in_=kT3[:, :, 512:640]), r=t_kcur, w=[t_kprev])
            P.add("pool", lambda e: e.tensor_copy(out=Va4[:, 0, :, 0:64], in_=Va4[:, 4, :, 0:64]), r=[t_vcur[3]], w=[t_vprev])
            P.tag = "B.attnT"
            for n in range(4):
                trb, t_trb = tr_bank()
                for c in range(8):
                    P.add("pe", lambda e, n=n, c=c, trb=trb: e.transpose(out=trb[:, c * 128:(c + 1) * 128],
                                                                          in_=a_tm[n][:, c * 128:(c + 1) * 128], identity=ident[:, :]),
                          r=t_atm[n] + [t_c6], w=[t_trb])
                P.add("dve", lambda e, n=n, trb=trb: e.scalar_tensor_tensor(out=ag3[:, :, n * 128:(n + 1) * 128],
                                                                            in0=trb.rearrange("p (c n) -> p c n", c=8), scalar=0.5,
                                                                            in1=sga3[:, :, n * 128:(n + 1) * 128], op0=ALU.mult, op1=ALU.mult),
                      r=[t_trb] + t_sgaT, w=[t_ag[c][n] for c in range(8)])
        def stage_B3(it):
            bank_mode[0] = "all"
            P.tag = "B.wao"
            for j in range(2):
                si, tp = wnext("wao", j)
                v = slot3(si, 512)
                for dd in range(4):
                    dch = j * 4 + dd
                    bY, t_bY = gen_bank()
                    for kc in range(KC):
                        mm(bY, v[:, kc, dd * 128:(dd + 1) * 128], ag3[:, kc, :], kc == 0, kc == KC - 1,
                           r=list(tp[0:2]) + t_ag[kc], w=[t_bY])
                    ya, t_ya = tmpf.get()
                    P.add("dve", lambda e, bY=bY, ya=ya, dch=dch: e.scalar_tensor_tensor(
                        out=ya[:, :], in0=mT[:, dch * TT:(dch + 1) * TT], scalar=1.0, in1=bY, op0=ALU.add, op1=ALU.mult),
                          r=[t_bY, t_mT[dch]], w=[t_ya])
                    P.add("dve", lambda e, ya=ya, dch=dch: e.tensor_tensor(out=mT[:, dch * TT:(dch + 1) * TT], in0=ya[:, :],
                                                                           in1=m1[:, dch * TT:(dch + 1) * TT], op=ALU.add),
                          r=[t_ya, t_m1[dch]], w=[t_mT[dch]])

        last_store = []

        def stage_O(it):
            P.tag = "O"
            bank_mode[0] = "all"
            par = it % 2
            si0, tp0 = wnext("wout", 0)
            si1, tp1 = wnext("wout", 1)
            wv = [slot3(si0, 512), slot3(si1, 512)]
            tps = [tp0, tp1]
            m3 = mT[:, :].rearrange("p (k n) -> p k n", k=8)
            for b in range(4):
                xt, t_x = xs[par * 4 + b], t_xs[par * 4 + b]
                xot, t_xo = xo.get()
                for half in range(2):
                    bO, t_bO = gen_bank()
                    for kc in range(KC):
                        mm(bO, m3[:, kc, b * 128:(b + 1) * 128], wv[half][:, kc, :], kc == 0, kc == KC - 1,
                           r=list(tps[half][0:2]) + t_mT, w=[t_bO])
                    P.add("dve", lambda e, bO=bO, xot=xot, xt=xt, half=half: e.scalar_tensor_tensor(
                        out=xot[:, half * 512:(half + 1) * 512], in0=bO, scalar=0.5, in1=xt[:, half * 512:(half + 1) * 512],
                        op0=ALU.mult, op1=ALU.add),
                        r=[t_bO, t_x], w=[t_xo])
                P.add("act", lambda e, xot=xot, b=b: e.activation(out=a_tm[b][:, :], in_=xot[:, :], func=AF.Square,
                                                                  accum_out=ssq[:, 8 + b:9 + b]),
                      r=[t_xo], w=t_atm[b] + [t_ssq2])
                P.add("act", lambda e, b=b: e.activation(out=ssq[:, 12 + b:13 + b], in_=ssq[:, 8 + b:9 + b], func=AF.Ln,
                                                         scale=1.0 / D, bias=EPS), r=[t_ssq2], w=[t_ssq2])
                P.add("act", lambda e, b=b: e.activation(out=ssq[:, 12 + b:13 + b], in_=ssq[:, 12 + b:13 + b], func=AF.Exp,
                                                         scale=-0.5), r=[t_ssq2], w=[t_ssq2])
                P.add("dve", lambda e, xot=xot, b=b: e.scalar_tensor_tensor(out=xot[:, :], in0=xot[:, :], scalar=ssq[:, 12 + b:13 + b],
                                                                            in1=fg_tab[:, :], op0=ALU.mult, op1=ALU.mult),
                      r=[t_xo, t_ssq2, t_c1], w=[t_xo])
                row0 = it * TT + b * 128
                key = "y%d" % ((xo.i - 1) % 2)
                op = P.add("sp", lambda e, xot=xot, row0=row0: e.dma_start(out=y[row0:row0 + 128, :], in_=xot[:, :]),
                           r=[t_xo], w=[], dma=key)
                last_store.append(op)

        if discover:
            stage_A(0)
            stage_B(0)
            stage_B2(0)
            stage_B3(0)
            stage_O(0)
            return disc
        stage_N_load(0)
        stage_N_pre(0)
        stage_N_pe(0)
        for it in range(NT):
            stage_A(it)
            if it + 1 < NT:
                stage_N_load(it + 1)
            stage_B(it)
            stage_B2(it)
            if it + 1 < NT:
                stage_N_pre(it + 1)
            stage_B3(it)
            if it + 1 < NT:
                stage_N_pe(it + 1)
            stage_O(it)
        fin = []
        for op in last_store[-2:]:
            t = T("fin")
            t.w = op
            fin.append(t)
        P.add("sp", None, r=fin)

        import os
        if os.environ.get("KTAGS"):
            import json
            json.dump({e: [o.tag for o in P.ops[e] if o.fn is not None] for e in ENGS}, open(os.environ["KTAGS"], "w"))
        dma_sems = {k: es.enter_context(nc.semaphore("d_" + k)) for k in P.dma_keys}
        P.emit_all(block, sems, dma_sems)
    return nc


def build_program():
    order = build_nc(None)
    return build_nc(order)


def _consts():
    d = np.arange(128) % 64
    f = d % 32
    inv_freq = (10000.0 ** (-(np.arange(0, 64, 2, dtype=np.float32)) / 64.0)).astype(np.float32)
    pos = np.arange(S, dtype=np.float32)
    ang = pos[None, :] * inv_freq[f][:, None]
    cosT = np.cos(ang).astype(np.float32)
    sgn = np.where(d < 32, -1.0, 1.0).astype(np.float32)
    sinT = (np.sin(ang) * sgn[:, None]).astype(np.float32)
    j = np.arange(128)[:, None]
    i = np.arange(128)[None, :]
    mask = np.concatenate([(j > i), (i >= j)], axis=1).astype(np.float32)
    ident = np.eye(128, dtype=np.float32)
    return cosT, sinT, mask, ident


_CACHE = {}


def kernel(x, norm_g, w_in, conv_dw_w, conv_dw_b, conv_ln_g, conv_ln_b, w_conv_out, attn_sinks, w_attn_out,
           w_out, final_norm_g):
    f = lambda a: np.ascontiguousarray(np.asarray(a, dtype=np.float32))
    x = f(x)
    cosT, sinT, mask, ident = _consts()
    col = lambda v: f(np.asarray(v, np.float32).reshape(8, 128).T)
    colp = np.concatenate([col(conv_dw_b[0]), col(conv_ln_g[0]), col(conv_ln_b[0])], axis=1)
    wdw = f(np.asarray(conv_dw_w[0], np.float32).reshape(31, 8, 128).transpose(2, 1, 0).reshape(128, 8 * 31))
    shared = {
        "w_in": f(w_in[0]), "w_co": f(w_conv_out[0]), "w_ao": f(w_attn_out[0]), "w_out": f(w_out[0]),
        "g_tab": f(np.broadcast_to(np.asarray(norm_g[0], np.float32)[None, :], (128, D))),
        "fg_tab": f(np.broadcast_to(np.asarray(final_norm_g, np.float32)[None, :], (128, D))),
        "colp": f(colp), "wdw": wdw,
        "sinks": f(np.broadcast_to(np.asarray(attn_sinks[0], np.float32)[None, :], (128, 16))),
        "cosT": cosT, "sinT": sinT, "maskT": mask, "ident": ident,
        "rperm": np.ascontiguousarray(ident[:, np.arange(128) ^ 32]),
    }
    if "nc" not in _CACHE:
        _CACHE["nc"] = build_program()
    nc = _CACHE["nc"]
    in_maps = []
    for b in range(NCORES):
        m = dict(shared)
        m["x"] = np.ascontiguousarray(x[b])
        in_maps.append(m)
    res = run_bass_kernel_spmd(nc, in_maps, core_ids=list(range(NCORES)))
    out = np.stack([np.asarray(res.results[b]["y"], dtype=np.float32).reshape(S, D) for b in range(NCORES)], axis=0)
    return out
```
